# Optimizing a Trainium2 kernel written in Bass

```python
import math
import jax, jax.numpy as jnp
from jax import lax
import numpy as np

D_MODEL = 1024
BATCH = 8
SEQ = 4096
DEPTH = 2

ATT_HEADS = 4
ATT_QK_DIM = 64
ATT_V_DIM = 2 * ATT_QK_DIM
ATT_WIDTH = ATT_HEADS * ATT_V_DIM
POOL_GROUPS = 4
POOL_WINDOWS = (2, 4, 8, 16)
POOL_WIDTH = 256
POOL_GDIM = POOL_WIDTH // POOL_GROUPS
CONV_WIDTH = 256
CONV_K = 3
MIX_WIDTH = ATT_WIDTH + POOL_WIDTH + CONV_WIDTH
QK_COLS = ATT_HEADS * 2 * ATT_QK_DIM
IN_COLS = 2 * QK_COLS + ATT_WIDTH + POOL_WIDTH + 3 * CONV_WIDTH
D_FF = ((8 * D_MODEL // 3 + 255) // 256) * 256
NUM_BUCKETS = 32
MAX_EXACT = NUM_BUCKETS // 2
MAX_DISTANCE = 128
Q_BLOCK = 128
EPS = 1e-6
SUBLN_EPS = 1e-5

kernel_name = "hybrid_diffattn_pool_shortconv_block"


def rmsnorm(x, g, eps=EPS):
    xf = x.astype(jnp.float32)
    y = xf * lax.rsqrt(jnp.mean(xf * xf, axis=-1, keepdims=True) + eps)
    return (y * g.astype(jnp.float32)).astype(x.dtype)


def rel_bucket(dist):
    n = jnp.maximum(dist, 0)
    nf = jnp.maximum(n, 1).astype(jnp.float32)
    large = MAX_EXACT + (jnp.log(nf / MAX_EXACT) / math.log(MAX_DISTANCE / MAX_EXACT)
                         * (NUM_BUCKETS - MAX_EXACT)).astype(jnp.int32)
    large = jnp.minimum(large, NUM_BUCKETS - 1)
    return jnp.where(n < MAX_EXACT, n, large)


def diff_attention(q1, q2, k1, k2, v, rel_bias, lam):
    b, h, s, _ = q1.shape
    nb = s // Q_BLOCK
    scale = ATT_QK_DIM ** -0.5
    kpos = jnp.arange(s)
    k1f, k2f, vf = k1.astype(jnp.float32), k2.astype(jnp.float32), v.astype(jnp.float32)

    def block(i):
        qs = i * Q_BLOCK
        qb1 = lax.dynamic_slice_in_dim(q1, qs, Q_BLOCK, axis=2).astype(jnp.float32)
        qb2 = lax.dynamic_slice_in_dim(q2, qs, Q_BLOCK, axis=2).astype(jnp.float32)
        dist = (qs + jnp.arange(Q_BLOCK))[:, None] - kpos[None, :]
        bias = jnp.transpose(rel_bias.astype(jnp.float32)[rel_bucket(dist)], (2, 0, 1))
        mask = dist >= 0
        s1 = jnp.einsum('bhqd,bhkd->bhqk', qb1, k1f) * scale + bias
        s2 = jnp.einsum('bhqd,bhkd->bhqk', qb2, k2f) * scale + bias
        p1 = jax.nn.softmax(jnp.where(mask, s1, -1e30), axis=-1)
        p2 = jax.nn.softmax(jnp.where(mask, s2, -1e30), axis=-1)
        return jnp.einsum('bhqk,bhkd->bqhd', p1 - lam * p2, vf)

    out = lax.map(block, jnp.arange(nb))
    return jnp.transpose(out, (1, 0, 2, 3, 4)).reshape(b, s, h, ATT_V_DIM)


def multiscale_pool(p, w_pool, pool_scale):
    b, s, _ = p.shape
    pf = p.astype(jnp.float32)
    csum = jnp.cumsum(pf, axis=1)
    t1 = jnp.arange(1, s + 1, dtype=jnp.float32)[None, :, None]
    groups = []
    for g, w in enumerate(POOL_WINDOWS):
        cg = csum[..., g * POOL_GDIM:(g + 1) * POOL_GDIM]
        prev = jnp.pad(cg, ((0, 0), (w, 0), (0, 0)))[:, :s]
        groups.append((cg - prev) / jnp.minimum(t1, float(w)))
    pooled = jnp.concatenate(groups, axis=-1) - pf
    mixed = jnp.einsum('bsgc,gcd->bsgd', pooled.reshape(b, s, POOL_GROUPS, POOL_GDIM),
                       w_pool.astype(jnp.float32)).reshape(b, s, POOL_WIDTH)
    return mixed * pool_scale.astype(jnp.float32)


def short_gated_conv(gb, gc, hin, conv_w):
    s = hin.shape[1]
    u = (gc * hin).astype(jnp.float32)
    up = jnp.pad(u, ((0, 0), (CONV_K - 1, 0), (0, 0)))
    wf = conv_w.astype(jnp.float32)
    y = sum(wf[k] * up[:, k:k + s] for k in range(CONV_K))
    return gb.astype(jnp.float32) * y


def setup_inputs(seed: int = 0) -> dict:
    key = jax.random.key(seed)
    ks = jax.random.split(key, 20)
    f32 = jnp.float32
    nrm = lambda k, shape, sc: jax.random.normal(k, shape, f32) * sc
    return {
        "x": nrm(ks[0], (BATCH, SEQ, D_MODEL), 1.0),
        "g_mix": 1.0 + nrm(ks[1], (DEPTH, D_MODEL), 0.01),
        "w_in": nrm(ks[2], (DEPTH, D_MODEL, IN_COLS), D_MODEL ** -0.5),
        "lambda_q1": nrm(ks[3], (DEPTH, ATT_QK_DIM), 0.1),
        "lambda_k1": nrm(ks[4], (DEPTH, ATT_QK_DIM), 0.1),
        "lambda_q2": nrm(ks[5], (DEPTH, ATT_QK_DIM), 0.1),
        "lambda_k2": nrm(ks[6], (DEPTH, ATT_QK_DIM), 0.1),
        "subln_g": 1.0 + nrm(ks[7], (DEPTH, ATT_V_DIM), 0.01),
        "rel_bias": nrm(ks[8], (NUM_BUCKETS, ATT_HEADS), 0.5),
        "w_pool": nrm(ks[9], (DEPTH, POOL_GROUPS, POOL_GDIM, POOL_GDIM), POOL_GDIM ** -0.5),
        "pool_scale": 1.0 + nrm(ks[10], (DEPTH, POOL_WIDTH), 0.1),
        "conv_w": nrm(ks[11], (DEPTH, CONV_K, CONV_WIDTH), CONV_K ** -0.5),
        "w_o": nrm(ks[12], (DEPTH, MIX_WIDTH, D_MODEL), MIX_WIDTH ** -0.5),
        "g_ffn": 1.0 + nrm(ks[13], (DEPTH, D_MODEL), 0.01),
        "w_gate": nrm(ks[14], (DEPTH, D_MODEL, D_FF), D_MODEL ** -0.5),
        "w_up": nrm(ks[15], (DEPTH, D_MODEL, D_FF), D_MODEL ** -0.5),
        "w_down": nrm(ks[16], (DEPTH, D_FF, D_MODEL), D_FF ** -0.5),
        "g_final": 1.0 + nrm(ks[17], (D_MODEL,), 0.01),
    }


def reference(x, g_mix, w_in, lambda_q1, lambda_k1, lambda_q2, lambda_k2, subln_g, rel_bias,
              w_pool, pool_scale, conv_w, w_o, g_ffn, w_gate, w_up, w_down, g_final):
    b, s, _ = x.shape
    split_pts = np.cumsum([QK_COLS, QK_COLS, ATT_WIDTH, POOL_WIDTH, CONV_WIDTH, CONV_WIDTH])
    for l in range(DEPTH):
        h = rmsnorm(x, g_mix[l])
        proj = h @ w_in[l]
        q, k, v, p, gb, gc, hin = jnp.split(proj, list(split_pts), axis=-1)
        q = q.reshape(b, s, ATT_HEADS, 2, ATT_QK_DIM)
        k = k.reshape(b, s, ATT_HEADS, 2, ATT_QK_DIM)
        q1 = jnp.transpose(q[:, :, :, 0], (0, 2, 1, 3))
        q2 = jnp.transpose(q[:, :, :, 1], (0, 2, 1, 3))
        k1 = jnp.transpose(k[:, :, :, 0], (0, 2, 1, 3))
        k2 = jnp.transpose(k[:, :, :, 1], (0, 2, 1, 3))
        vh = jnp.transpose(v.reshape(b, s, ATT_HEADS, ATT_V_DIM), (0, 2, 1, 3))
        lam_init = 0.8 - 0.6 * math.exp(-0.3 * l)
        lam = (jnp.exp(jnp.sum(lambda_q1[l].astype(jnp.float32) * lambda_k1[l].astype(jnp.float32)))
               - jnp.exp(jnp.sum(lambda_q2[l].astype(jnp.float32) * lambda_k2[l].astype(jnp.float32)))
               + lam_init)
        att = diff_attention(q1, q2, k1, k2, vh, rel_bias, lam)
        att = rmsnorm(att, subln_g[l], SUBLN_EPS) * (1.0 - lam_init)
        att = att.reshape(b, s, ATT_WIDTH)
        pool = multiscale_pool(p, w_pool[l], pool_scale[l])
        conv = short_gated_conv(gb, gc, hin, conv_w[l])
        mixed = jnp.concatenate([att, pool, conv], axis=-1).astype(x.dtype)
        x = x + (mixed @ w_o[l]).astype(x.dtype)
        h = rmsnorm(x, g_ffn[l])
        ff = (jax.nn.silu(h @ w_gate[l]) * (h @ w_up[l])) @ w_down[l]
        x = x + ff.astype(x.dtype)
    return rmsnorm(x, g_final)
```

```python
import math
from contextlib import ExitStack

import numpy as np
import concourse.bass as bass
import concourse.mybir as mybir
from concourse.bass_utils import run_bass_kernel_spmd

F32 = mybir.dt.float32
BF16 = mybir.dt.bfloat16
AF = mybir.ActivationFunctionType
ALU = mybir.AluOpType

S = 4096
D = 1024
DEPTH = 2
NH = 4
DFF = 2816
NF = DFF // 128
TB = 512
NTB = S // TB
INC = 2560
EPS = 1e-6
SUBLN_EPS = 1e-5
MASKV = -30000.0
TW = 640

PL = 8 + 8 + 2 + 6 + 128 + 256
OFF_GM, OFF_GF, OFF_PS, OFF_CW, OFF_SG, OFF_LAM = 0, 8, 16, 18, 24, 152
OFF_GFIN = DEPTH * PL
OFF_CH = OFF_GFIN + 8
OFF_ID0 = OFF_CH + 4
OFF_EPS = OFF_ID0 + 32
OFF_SEPS = OFF_EPS + 1
NP = OFF_SEPS + 1


class Buf:
    __slots__ = ("w", "r")

    def __init__(self):
        self.w = None
        self.r = []


class Rec:
    ENG = ("pe", "act", "dve", "pool", "sp")
    CE = ("pe", "act", "dve", "pool")

    def __init__(self, nc, es):
        self.nc = nc
        self.es = es
        self.sems = {}
        self.count = {}
        self.streams = {e: [] for e in self.ENG}
        self.waited = {e: {} for e in self.ENG}
        for e in self.CE:
            self._sem(e)

    def _sem(self, key):
        if key not in self.sems:
            self.sems[key] = self.es.enter_context(self.nc.semaphore("s_" + key))
            self.count[key] = 0
        return self.sems[key]

    def _collect(self, eng, reads, writes, extra):
        need = {}

        def add(tok, kind):
            if tok is None:
                return
            sem, val = tok
            if sem == eng:
                if eng == "pe":
                    return
                if val > self.count[eng]:
                    return
                if kind == "war":
                    return
            if need.get(sem, 0) < val:
                need[sem] = val

        for b in reads:
            add(b.w, "raw")
        for b in writes:
            add(b.w, "waw")
            for t in b.r:
                add(t, "war")
        for t in extra:
            add(t, "raw")
        out = []
        wd = self.waited[eng]
        for sem, val in need.items():
            if wd.get(sem, 0) < val:
                wd[sem] = val
                out.append((sem, val))
        return out

    def op(self, eng, fn, reads=(), writes=(), signal=True, extra=()):
        waits = self._collect(eng, reads, writes, extra)
        if signal:
            self.count[eng] += 1
            tok = (eng, self.count[eng])
            inc = (eng, 1)
        else:
            tok = (eng, self.count[eng] + 1)
            inc = None
        for b in reads:
            b.r.append(tok)
        for b in writes:
            b.w = tok
            b.r = []
        self.streams[eng].append((waits, fn, inc))
        return tok

    def dma(self, q, semkey, fn, reads=(), writes=(), extra=()):
        self._sem(semkey)
        waits = self._collect(q, reads, writes, extra)
        self.count[semkey] += 16
        tok = (semkey, self.count[semkey])
        for b in reads:
            b.r.append(tok)
        for b in writes:
            b.w = tok
            b.r = []
        self.streams[q].append((waits, fn, (semkey, 16)))
        return tok

    def barrier(self):
        toks = [(k, v) for k, v in self.count.items() if v > 0]
        for eng in self.ENG:
            waits = []
            wd = self.waited[eng]
            for sem, val in toks:
                if sem == eng:
                    continue
                if wd.get(sem, 0) < val:
                    wd[sem] = val
                    waits.append((sem, val))
            if waits:
                self.streams[eng].append((waits, None, None))

    def emit(self):
        nc = self.nc
        with nc.Block() as block:
            def mk(name):
                def run(e):
                    for waits, fn, inc in self.streams[name]:
                        for sem, val in waits:
                            e.wait_ge(self.sems[sem], val)
                        if fn is not None:
                            ins = fn(e)
                            if inc is not None:
                                ins.then_inc(self.sems[inc[0]], inc[1])
                return run
            block.tensor(mk("pe"))
            block.scalar(mk("act"))
            block.vector(mk("dve"))
            block.gpsimd(mk("pool"))
            block.sync(mk("sp"))
        for k in self.streams:
            self.streams[k] = []


def emit_norm(R, G, xt, xB, sq, sqB, ssbank, ssB, lnv, lnvB, rstd, rstdB, out, outB, gcol0):
    prm, ones, onesB = G["prm"], G["ones"], G["onesB"]
    R.op("act", lambda e: e.activation(out=sq[:, :, :], in_=xt[:, :, :], func=AF.Square),
         reads=[xB], writes=sqB)
    for kc in range(8):
        R.op("pe", lambda e, kc=kc: e.matmul(ssbank[:, :], ones[:, :], sq[:, kc, :],
                                             start=(kc == 0), stop=(kc == 7)),
             reads=sqB + [onesB], writes=[ssB], signal=(kc == 7))
    R.op("act", lambda e: e.activation(out=lnv[:, :], in_=ssbank[:, :], func=AF.Ln,
                                       bias=prm[:, OFF_EPS:OFF_EPS + 1], scale=1.0 / D),
         reads=[ssB], writes=[lnvB])
    R.op("act", lambda e: e.activation(out=rstd[:, :], in_=lnv[:, :], func=AF.Exp, scale=-0.5),
         reads=[lnvB], writes=[rstdB])
    for kc in range(8):
        R.op("dve", lambda e, kc=kc: e.scalar_tensor_tensor(
            out=out[:, kc, :], in0=xt[:, kc, :], scalar=prm[:, gcol0 + kc:gcol0 + kc + 1],
            in1=rstd[:, :], op0=ALU.mult, op1=ALU.mult),
            reads=[xB, rstdB], writes=[outB])


def x_view(ap2d):
    return ap2d.rearrange("(c p) n -> p c n", p=128)


def phase_A(R, nc, G, l, xsrc):
    prm = G["prm"]
    pb = l * PL
    win_d = G["w_in"][l].rearrange("(kc p) n -> p kc n", p=128)
    with ExitStack() as st:
        def sb(name, shape, dt):
            return st.enter_context(nc.sbuf_tensor(f"{name}_L{l}", shape, dt))

        win = sb("winA", [128, 8, INC], BF16)
        xb = [sb(f"xbA{i}", [128, 8, TB], F32) for i in range(2)]
        sq = sb("sqA", [128, 8, TB], BF16)
        hT = [sb(f"hTA{i}", [128, 8, TB], BF16) for i in range(2)]
        lnv = sb("lnvA", [128, TB], F32)
        rstd = sb("rstdA", [128, TB], F32)
        qst = sb("qstA", [128, 4, TB], BF16)
        kst = sb("kstA", [128, 4, TB], BF16)
        vst = sb("vstA", [128, 4, 512], BF16)
        mixst = sb("mixstA", [128, 4, TB], BF16)
        Pb = sb("PbA", [128, 2, 528], F32)
        Ub = sb("UbA", [128, 2, 528], F32)
        gcS = sb("gcSA", [128, 2, TB], F32)
        s2 = sb("s2A", [128, 2, 528], F32)
        s4 = sb("s4A", [128, 2, 528], F32)
        s8 = sb("s8A", [128, 528], F32)
        s16 = sb("s16A", [128, 528], F32)
        tmpf = sb("tmpfA", [128, 16], F32)
        pooled = sb("pooledA", [128, 2, TB], BF16)
        yv = sb("yvA", [128, 2, TB], F32)
        wblk = sb("wblkA", [128, 2, 128], BF16)
        banks = [st.enter_context(nc.psum_tensor(f"L{l}bkA{i}", [128, 512], F32)) for i in range(8)]
        bankB = [Buf() for _ in range(8)]

        winB = [Buf() for _ in range(8)]
        xB = [Buf(), Buf()]
        sqB = [Buf()]
        hB = [Buf(), Buf()]
        lnvB, rstdB, qstB, kstB, vstB, mixB = Buf(), Buf(), Buf(), Buf(), Buf(), Buf()
        PbB, UbB, gcB, s2B, s4B, s8B, s16B, tmpB, poolB, yvB, wblkB = (Buf() for _ in range(11))

        for kc in range(8):
            R.dma("pool", f"win{kc}", lambda e, kc=kc: e.dma_start(out=win[:, kc, :], in_=win_d[:, kc, :]),
                  writes=[winB[kc]])
        R.op("dve", lambda e: e.memset(wblk[:, :, :], 0.0), writes=[wblkB])
        for g in range(4):
            r0 = (g % 2) * 64
            R.dma("pool", "wblk", lambda e, g=g, r0=r0: e.dma_start(
                out=wblk[r0:r0 + 64, g // 2, r0:r0 + 64], in_=G["w_pool"][l, g, :, :]),
                writes=[wblkB])
        R.op("dve", lambda e: e.memset(Pb[:, :, 0:16], 0.0), writes=[PbB])
        R.op("dve", lambda e: e.memset(Ub[:, :, 0:16], 0.0), writes=[UbB])

        xv = x_view(xsrc)

        def load(tb):
            i = tb % 2
            R.dma("sp", f"xA{i}", lambda e: e.dma_start(out=xb[i][:, :, :], in_=xv[:, :, tb * TB:(tb + 1) * TB]),
                  writes=[xB[i]])

        def norm(tb):
            i = tb % 2
            emit_norm(R, G, xb[i], xB[i], sq, sqB, banks[7], bankB[7], lnv, lnvB, rstd, rstdB,
                      hT[i], hB[i], pb + OFF_GM)

        rr = [0]

        def nb():
            i = rr[0] % 5
            rr[0] += 1
            return banks[i], bankB[i]

        def mm_chain(bank, bB, h, hb, col, tsub=None):
            for kc in range(8):
                if tsub is None:
                    fn = lambda e, kc=kc: e.matmul(bank[:, :], win[:, kc, col:col + 128], h[:, kc, :],
                                                   start=(kc == 0), stop=(kc == 7))
                else:
                    fn = lambda e, kc=kc: e.matmul(bank[:, :], h[:, kc, tsub * 128:(tsub + 1) * 128],
                                                   win[:, kc, 1024:1536], start=(kc == 0), stop=(kc == 7))
                R.op("pe", fn, reads=[winB[kc], hb], writes=[bB], signal=(kc == 7))

        load(0)
        load(1)
        norm(0)
        for tb in range(NTB):
            i = tb % 2
            h, hb = hT[i], hB[i]
            tsl = slice(tb * TB, (tb + 1) * TB)
            for hh in range(4):
                bank, bB = nb()
                mm_chain(bank, bB, h, hb, hh * 128)
                R.op("act", lambda e, bank=bank, hh=hh: e.mul(out=qst[:, hh, :], in_=bank[:, :], mul=0.125),
                     reads=[bB], writes=[qstB])
            for hh in range(4):
                bank, bB = nb()
                mm_chain(bank, bB, h, hb, 512 + hh * 128)
                R.op("act", lambda e, bank=bank, hh=hh: e.copy(out=kst[:, hh, :], in_=bank[:, :]),
                     reads=[bB], writes=[kstB])
            R.dma("pool", "qstore", lambda e, tsl=tsl: e.dma_start(
                out=G["qT"].rearrange("h p n -> p h n")[:, :, tsl], in_=qst[:, :, :]), reads=[qstB])
            R.dma("pool", "kstore", lambda e, tsl=tsl: e.dma_start(
                out=G["kT"].rearrange("h p n -> p h n")[:, :, tsl], in_=kst[:, :, :]), reads=[kstB])
            if tb + 1 < NTB:
                norm(tb + 1)
            for c in range(2):
                bank, bB = nb()
                mm_chain(bank, bB, h, hb, 1536 + c * 128)
                R.op("act", lambda e, bank=bank, c=c: e.copy(out=Pb[:, c, 16:528], in_=bank[:, :]),
                     reads=[bB], writes=[PbB])
            for c in range(2):
                bank, bB = nb()
                mm_chain(bank, bB, h, hb, 2048 + c * 128)
                R.op("act", lambda e, bank=bank, c=c: e.copy(out=gcS[:, c, :], in_=bank[:, :]),
                     reads=[bB], writes=[gcB])
            for c in range(2):
                bank, bB = nb()
                mm_chain(bank, bB, h, hb, 2304 + c * 128)
                R.op("dve", lambda e, bank=bank, c=c: e.tensor_tensor(
                    out=Ub[:, c, 16:528], in0=bank[:, :], in1=gcS[:, c, :], op=ALU.mult),
                    reads=[bB, gcB], writes=[UbB])
            gbb = []
            for c in range(2):
                bank, bB = banks[5 + c], bankB[5 + c]
                mm_chain(bank, bB, h, hb, 1792 + c * 128)
                gbb.append((bank, bB))
            for ts in range(4):
                bank, bB = nb()
                mm_chain(bank, bB, h, hb, 0, tsub=ts)
                R.op("act", lambda e, bank=bank, ts=ts: e.copy(out=vst[:, ts, :], in_=bank[:, :]),
                     reads=[bB], writes=[vstB])
            R.dma("pool", "vstore", lambda e, tb=tb: e.dma_start(
                out=G["vS"].rearrange("(t p) f -> p t f", p=128)[:, tb * 4:(tb + 1) * 4, :], in_=vst[:, :, :]),
                reads=[vstB])
            for c in range(2):
                cw = lambda k, c=c: prm[:, pb + OFF_CW + k * 2 + c:pb + OFF_CW + k * 2 + c + 1]
                R.op("dve", lambda e, c=c, cw=cw: e.tensor_scalar(
                    out=yv[:, c, :], in0=Ub[:, c, 14:526], scalar1=cw(0), scalar2=None, op0=ALU.mult),
                    reads=[UbB], writes=[yvB])
                R.op("dve", lambda e, c=c, cw=cw: e.scalar_tensor_tensor(
                    out=yv[:, c, :], in0=Ub[:, c, 15:527], scalar=cw(1), in1=yv[:, c, :],
                    op0=ALU.mult, op1=ALU.add), reads=[UbB, yvB], writes=[yvB])
                R.op("dve", lambda e, c=c, cw=cw: e.scalar_tensor_tensor(
                    out=yv[:, c, :], in0=Ub[:, c, 16:528], scalar=cw(2), in1=yv[:, c, :],
                    op0=ALU.mult, op1=ALU.add), reads=[UbB, yvB], writes=[yvB])
                bank, bB = gbb[c]
                R.op("dve", lambda e, c=c, bank=bank: e.tensor_tensor(
                    out=mixst[:, 2 + c, :], in0=bank[:, :], in1=yv[:, c, :], op=ALU.mult),
                    reads=[bB, yvB], writes=[mixB])
            R.op("dve", lambda e: e.tensor_copy(out=Ub[:, :, 14:16], in_=Ub[:, :, 526:528]),
                 reads=[UbB], writes=[UbB])
            R.op("dve", lambda e: e.tensor_tensor(out=s2[:, :, 1:528], in0=Pb[:, :, 1:528], in1=Pb[:, :, 0:527],
                                                  op=ALU.add), reads=[PbB], writes=[s2B])
            R.op("dve", lambda e: e.tensor_tensor(out=s4[:, :, 3:528], in0=s2[:, :, 3:528], in1=s2[:, :, 1:526],
                                                  op=ALU.add), reads=[s2B], writes=[s4B])
            R.op("dve", lambda e: e.tensor_tensor(out=s8[:, 7:528], in0=s4[:, 1, 7:528], in1=s4[:, 1, 3:524],
                                                  op=ALU.add), reads=[s4B], writes=[s8B])
            R.op("dve", lambda e: e.tensor_tensor(out=s16[64:128, 15:528], in0=s8[64:128, 15:528],
                                                  in1=s8[64:128, 7:520], op=ALU.add), reads=[s8B], writes=[s16B])
            grp = [(s2, lambda r: s2[r, 0, 16:528], 0, 0, 2.0, s2B),
                   (s4, lambda r: s4[r, 0, 16:528], 64, 0, 4.0, s4B),
                   (s8, lambda r: s8[r, 16:528], 0, 1, 8.0, s8B),
                   (s16, lambda r: s16[r, 16:528], 64, 1, 16.0, s16B)]
            for (_, src, r0, c, w, sB) in grp:
                rs = slice(r0, r0 + 64)
                R.op("dve", lambda e, src=src, rs=rs, c=c, w=w: e.scalar_tensor_tensor(
                    out=pooled[rs, c, :], in0=src(rs), scalar=1.0 / w, in1=Pb[rs, c, 16:528],
                    op0=ALU.mult, op1=ALU.subtract), reads=[sB, PbB], writes=[poolB])
            if tb == 0:
                for (_, src, r0, c, w, sB) in grp:
                    rs = slice(r0, r0 + 64)
                    R.op("dve", lambda e, src=src, rs=rs, c=c: e.tensor_tensor(
                        out=tmpf[rs, :], in0=src(rs)[:, 0:16],
                        in1=prm[rs, OFF_ID0 + c * 16:OFF_ID0 + c * 16 + 16], op=ALU.mult),
                        reads=[sB], writes=[tmpB])
                    R.op("dve", lambda e, rs=rs, c=c: e.tensor_tensor(
                        out=pooled[rs, c, 0:16], in0=tmpf[rs, :], in1=Pb[rs, c, 16:32], op=ALU.subtract),
                        reads=[tmpB, PbB], writes=[poolB])
            R.op("dve", lambda e: e.tensor_copy(out=Pb[:, :, 0:16], in_=Pb[:, :, 512:528]),
                 reads=[PbB], writes=[PbB])
            for c in range(2):
                bank, bB = nb()
                R.op("pe", lambda e, bank=bank, c=c: e.matmul(bank[:, :], wblk[:, c, :], pooled[:, c, :],
                                                              start=True, stop=True),
                     reads=[wblkB, poolB], writes=[bB])
                R.op("dve", lambda e, bank=bank, c=c: e.tensor_scalar(
                    out=mixst[:, c, :], in0=bank[:, :], scalar1=prm[:, pb + OFF_PS + c:pb + OFF_PS + c + 1],
                    scalar2=None, op0=ALU.mult), reads=[bB], writes=[mixB])
            R.dma("pool", "mixstore", lambda e, tsl=tsl: e.dma_start(
                out=x_view(G["mixT"])[:, 4:8, tsl], in_=mixst[:, :, :]), reads=[mixB])
            if tb + 2 < NTB:
                load(tb + 2)
        R.barrier()
        R.emit()


def phase_B(R, nc, G, l, xsrc, xdst):
    prm = G["prm"]
    pb = l * PL
    lam_init = 0.8 - 0.6 * math.exp(-0.3 * l)
    wo_d = G["w_o"][l].rearrange("(kc p) n -> p kc n", p=128)
    with ExitStack() as st:
        def sb(name, shape, dt):
            return st.enter_context(nc.sbuf_tensor(f"{name}_L{l}", shape, dt))

        KT = sb("KTB", [128, 4, S], BF16)
        V = sb("VB", [128, 32, 4, 129], BF16)
        wo = sb("woB", [128, 8, D], BF16)
        tbl = sb("tblB", [128, 4, TW], F32)
        Qb = [sb(f"QbB{i}", [128, 4, TB], BF16) for i in range(2)]
        PT = [[sb(f"PTB{m}{i}", [128, TB], BF16) for i in range(2)] for m in range(2)]
        mixb = [sb(f"mixbB{i}", [128, 8, TB], BF16) for i in range(2)]
        xb = [sb(f"xbB{i}", [128, 8, TB], F32) for i in range(2)]
        rl = sb("rlB", [128, 4, 2], F32)
        r2n = sb("r2nB", [128, 4], F32)
        tt = sb("ttB", [128, 128], F32)
        attf = sb("attfB", [128, 4, 128], F32)
        junk = sb("junkB", [128, 128], F32)
        ss = sb("ssB", [128, 4], F32)
        lnv = sb("lnvB", [128, 4], F32)
        rstd = sb("rstdB", [128, 4], F32)
        attn = sb("attnB", [128, 4, 128], BF16)
        ident = sb("identB", [128, 128], BF16)
        Gp = sb("GpB", [128, 128], F32)
        lamt = sb("lamtB", [128, 8], F32)
        lprod = sb("lprodB", [128, 64], F32)
        Sb = [[st.enter_context(nc.psum_tensor(f"L{l}SbB{m}{i}", [128, 512], F32)) for i in range(2)] for m in range(2)]
        accb = [st.enter_context(nc.psum_tensor(f"L{l}accB{i}", [128, 3, 129], F32)) for i in range(3)]
        trb = st.enter_context(nc.psum_tensor(f"L{l}trB", [128, 512], BF16))

        SbB = [[Buf(), Buf()], [Buf(), Buf()]]
        PTB = [[Buf(), Buf()], [Buf(), Buf()]]
        accB = [[Buf(), Buf()] for _ in range(4)]
        trB = Buf()
        KTB = [Buf() for _ in range(NTB)]
        VBf = [Buf() for _ in range(NTB)]
        QB = [Buf(), Buf()]
        mixatt = [[Buf() for _ in range(4)] for _ in range(2)]
        mixpc = [Buf(), Buf()]
        xB = [[Buf() for _ in range(8)] for _ in range(2)]
        woB, tblB, identB, GpB, lamB, lprodB = (Buf() for _ in range(6))
        rlB, r2nB, ttB, attfB, junkB, ssB_, lnvB, rstdB, attnB = (Buf() for _ in range(9))

        def acc(sub, m):
            i = sub * 2 + m
            return accb[i // 3][:, i % 3, :]

        R.dma("pool", "woB", lambda e: e.dma_start(out=wo[:, :, :], in_=wo_d[:, :, :]), writes=[woB])
        R.dma("pool", "identB", lambda e: e.dma_start(out=ident[:, :], in_=G["ident"][:, :]), writes=[identB])
        R.dma("sp", "tblB", lambda e: e.dma_start(out=tbl[:, :, :], in_=G["tbl"][:, :, :]), writes=[tblB])
        R.op("dve", lambda e: e.memset(V[:, :, :, 128:129], 1.0), writes=VBf)
        kT_v = G["kT"].rearrange("h p n -> p h n")
        vS_v = G["vS"].rearrange("(t p) (h d) -> p t h d", p=128, h=4)
        for tb in range(NTB):
            R.dma("sp", f"kld{tb}", lambda e, tb=tb: e.dma_start(
                out=KT[:, :, tb * TB:(tb + 1) * TB], in_=kT_v[:, :, tb * TB:(tb + 1) * TB]), writes=[KTB[tb]])
            for t4 in range(4):
                R.dma("sp", f"vld{tb}", lambda e, t=tb * 4 + t4: e.dma_start(
                    out=V[:, t, :, 0:128], in_=vS_v[:, t, :, :]), writes=[VBf[tb]])
        lo = pb + OFF_LAM
        for j in range(2):
            R.op("dve", lambda e, j=j: e.tensor_tensor(
                out=lprod[:, :], in0=prm[:, lo + j * 128:lo + j * 128 + 64],
                in1=prm[:, lo + j * 128 + 64:lo + j * 128 + 128], op=ALU.mult), writes=[lprodB])
            R.op("dve", lambda e, j=j: e.reduce_sum(out=lamt[:, j:j + 1], in_=lprod[:, :],
                                                   axis=mybir.AxisListType.X),
                 reads=[lprodB], writes=[lamB])
        R.op("act", lambda e: e.activation(out=lamt[:, 2:4], in_=lamt[:, 0:2], func=AF.Exp),
             reads=[lamB], writes=[lamB])
        R.op("dve", lambda e: e.tensor_tensor(out=lamt[:, 4:5], in0=lamt[:, 2:3], in1=lamt[:, 3:4],
                                              op=ALU.subtract), reads=[lamB], writes=[lamB])
        R.op("dve", lambda e: e.tensor_scalar(out=lamt[:, 5:6], in0=lamt[:, 4:5], scalar1=lam_init,
                                              scalar2=-1.0, op0=ALU.add, op1=ALU.mult),
             reads=[lamB], writes=[lamB])
        R.op("dve", lambda e: e.tensor_scalar(out=Gp[:, :], in0=prm[:, pb + OFF_SG:pb + OFF_SG + 128],
                                              scalar1=1.0 - lam_init, scalar2=None, op0=ALU.mult),
             writes=[GpB])
        nlam = lamt[:, 5:6]

        xv = x_view(xsrc)
        xo = x_view(xdst)
        qT_v = G["qT"].rearrange("h p n -> p h n")
        mix_v = x_view(G["mixT"])

        def loads(qb):
            i = qb % 2
            tsl = slice(qb * TB, (qb + 1) * TB)
            R.dma("sp", f"qld{i}", lambda e: e.dma_start(out=Qb[i][:, :, :], in_=qT_v[:, :, tsl]), writes=[QB[i]])
            R.dma("sp", f"mld{i}", lambda e: e.dma_start(out=mixb[i][:, 4:8, :], in_=mix_v[:, 4:8, tsl]),
                  writes=[mixpc[i]])
            R.dma("sp", f"xld{i}", lambda e: e.dma_start(out=xb[i][:, :, :], in_=xv[:, :, tsl]), writes=xB[i])

        def attn_head(qb, h):
                i = qb % 2
                nk = 4 * (qb + 1)

                def qk(kt):
                    j = kt - 4 * qb
                    qlo = max(0, 128 * j)
                    p = kt % 2
                    for m in range(2):
                        R.op("pe", lambda e, m=m, p=p, kt=kt, qlo=qlo: e.matmul(
                            Sb[m][p][:, qlo:512], KT[m * 64:(m + 1) * 64, h, kt * 128:(kt + 1) * 128],
                            Qb[i][m * 64:(m + 1) * 64, h, qlo:512], start=True, stop=True),
                            reads=[KTB[kt // 4], QB[i]], writes=[SbB[m][p]])

                def soft(kt):
                    j = kt - 4 * qb
                    qlo = max(0, 128 * j)
                    p = kt % 2
                    for m in range(2):
                        if j >= -1:
                            mlo = 128 if j < 0 else 0
                            wdt = 512 - qlo
                            R.op("dve", lambda e, m=m, p=p, qlo=qlo, mlo=mlo, wdt=wdt: e.tensor_tensor(
                                out=Sb[m][p][:, qlo:512], in0=Sb[m][p][:, qlo:512],
                                in1=tbl[:, h, mlo:mlo + wdt], op=ALU.add),
                                reads=[SbB[m][p], tblB], writes=[SbB[m][p]])
                            R.op("act", lambda e, m=m, p=p, qlo=qlo: e.activation(
                                out=PT[m][p][:, qlo:512], in_=Sb[m][p][:, qlo:512], func=AF.Exp),
                                reads=[SbB[m][p]], writes=[PTB[m][p]])
                        else:
                            R.op("act", lambda e, m=m, p=p: e.activation(
                                out=PT[m][p][:, :], in_=Sb[m][p][:, :], func=AF.Exp,
                                bias=prm[:, OFF_CH + h:OFF_CH + h + 1]),
                                reads=[SbB[m][p]], writes=[PTB[m][p]])

                def pv(kt):
                    j = kt - 4 * qb
                    p = kt % 2
                    s0 = max(j, 0)
                    seen = set()
                    for m in range(2):
                        for sub in range(s0, 4):
                            last = (kt == 4 * qb + sub)
                            bnk = (sub * 2 + m) // 3
                            st_ = (kt == 0 and bnk not in seen)
                            seen.add(bnk)
                            R.op("pe", lambda e, m=m, p=p, sub=sub, kt=kt, last=last, st_=st_: e.matmul(
                                acc(sub, m), PT[m][p][:, sub * 128:(sub + 1) * 128], V[:, kt, h, :],
                                start=st_, stop=last, skip_group_check=True),
                                reads=[PTB[m][p], VBf[kt // 4]], writes=[accB[sub][m]],
                                signal=(last or sub == 3))

                qk(0)
                for kt in range(nk):
                    if kt + 1 < nk:
                        qk(kt + 1)
                    soft(kt)
                    pv(kt)
                for sub in range(4):
                    for m in range(2):
                        R.op("dve", lambda e, sub=sub, m=m: e.reciprocal(out=rl[:, sub, m:m + 1],
                                                                       in_=acc(sub, m)[:, 128:129]),
                             reads=[accB[sub][m]], writes=[rlB])
                R.op("dve", lambda e: e.tensor_scalar(out=r2n[:, :], in0=rl[:, :, 1], scalar1=nlam,
                                                      scalar2=None, op0=ALU.mult),
                     reads=[rlB, lamB], writes=[r2nB])
                for sub in range(4):
                    R.op("dve", lambda e, sub=sub: e.tensor_scalar(
                        out=tt[:, :], in0=acc(sub, 0)[:, 0:128], scalar1=rl[:, sub, 0:1], scalar2=None,
                        op0=ALU.mult), reads=[accB[sub][0], rlB], writes=[ttB])
                    R.op("dve", lambda e, sub=sub: e.scalar_tensor_tensor(
                        out=attf[:, sub, :], in0=acc(sub, 1)[:, 0:128], scalar=r2n[:, sub:sub + 1],
                        in1=tt[:, :], op0=ALU.mult, op1=ALU.add),
                        reads=[accB[sub][1], r2nB, ttB], writes=[attfB])
                    R.op("dve", lambda e, sub=sub: e.scalar_tensor_tensor(
                        out=junk[:, :], in0=attf[:, sub, :], scalar=1.0, in1=attf[:, sub, :],
                        op0=ALU.mult, op1=ALU.mult, accum_out=ss[:, sub:sub + 1]),
                        reads=[attfB], writes=[junkB, ssB_])
                R.op("act", lambda e: e.activation(out=lnv[:, :], in_=ss[:, :], func=AF.Ln,
                                                   bias=prm[:, OFF_SEPS:OFF_SEPS + 1], scale=1.0 / 128),
                     reads=[ssB_], writes=[lnvB])
                R.op("act", lambda e: e.activation(out=rstd[:, :], in_=lnv[:, :], func=AF.Exp, scale=-0.5),
                     reads=[lnvB], writes=[rstdB])
                for sub in range(4):
                    R.op("dve", lambda e, sub=sub: e.scalar_tensor_tensor(
                        out=attn[:, sub, :], in0=attf[:, sub, :], scalar=rstd[:, sub:sub + 1], in1=Gp[:, :],
                        op0=ALU.mult, op1=ALU.mult), reads=[attfB, rstdB, GpB], writes=[attnB])
                for sub in range(4):
                    R.op("pe", lambda e, sub=sub: e.transpose(trb[:, sub * 128:(sub + 1) * 128], attn[:, sub, :], ident[:, :]),
                         reads=[attnB, identB], writes=[trB], signal=(sub == 3))
                R.op("act", lambda e: e.copy(out=mixb[i][:, h, :], in_=trb[:, :]),
                     reads=[trB], writes=[mixatt[i][h]])
        def wo_block(qb):
            i = qb % 2
            tsl = slice(qb * TB, (qb + 1) * TB)
            for oc in range(8):
                bank, bB = Sb[oc % 2][(oc // 2) % 2], SbB[oc % 2][(oc // 2) % 2]
                for kc in range(8):
                    rd = [woB, mixatt[i][kc]] if kc < 4 else [woB, mixpc[i]]
                    R.op("pe", lambda e, bank=bank, kc=kc, oc=oc: e.matmul(
                        bank[:, :], wo[:, kc, oc * 128:(oc + 1) * 128], mixb[i][:, kc, :],
                        start=(kc == 0), stop=(kc == 7)), reads=rd, writes=[bB], signal=(kc == 7))
                R.op("dve", lambda e, bank=bank, oc=oc: e.tensor_tensor(
                    out=xb[i][:, oc, :], in0=bank[:, :], in1=xb[i][:, oc, :], op=ALU.add),
                    reads=[bB, xB[i][oc]], writes=[xB[i][oc]])
            R.dma("pool", f"xst{i}", lambda e, tsl=tsl: e.dma_start(out=xo[:, :, tsl], in_=xb[i][:, :, :]),
                  reads=xB[i])
        loads(0)
        for qb in range(NTB):
            if qb + 1 < NTB:
                loads(qb + 1)
            for h in range(NH):
                attn_head(qb, h)
            wo_block(qb)
        R.barrier()
        R.emit()


def phase_C(R, nc, G, l, xsrc, xdst, final):
    prm = G["prm"]
    pb = l * PL
    wg_d = G["w_gate"][l].rearrange("(kc p) n -> p kc n", p=128)
    wu_d = G["w_up"][l].rearrange("(kc p) n -> p kc n", p=128)
    wd_d = G["w_down"][l].rearrange("(f p) n -> p f n", p=128)
    with ExitStack() as st:
        def sb(name, shape, dt):
            return st.enter_context(nc.sbuf_tensor(f"{name}_L{l}", shape, dt))

        wg = sb("wgC", [128, 8, DFF], BF16)
        wu = sb("wuC", [128, 8, DFF], BF16)
        wd = sb("wdC", [128, NF, D], BF16)
        xb = [sb(f"xbC{i}", [128, 8, TB], F32) for i in range(2)]
        hT = sb("hTC", [128, 8, TB], BF16)
        aT = sb("aTC", [128, NF, TB], BF16)
        lnv = sb("lnvC", [128, TB], F32)
        rstd = sb("rstdC", [128, TB], F32)
        sg = [sb(f"sgC{i}", [128, TB], F32) for i in range(2)]
        banks = [st.enter_context(nc.psum_tensor(f"L{l}bkC{i}", [128, 512], F32)) for i in range(8)]
        bankB = [Buf() for _ in range(8)]
        wgB = [Buf() for _ in range(8)]
        wuB = [Buf() for _ in range(8)]
        wdB = [Buf() for _ in range(NF)]
        xB = [[Buf() for _ in range(8)] for _ in range(2)]
        hB, lnvB, rstdB = Buf(), Buf(), Buf()
        aB = [Buf() for _ in range(NF)]
        sgB = [Buf(), Buf()]
        sq = aT
        sqB = aB[0:8]

        for kc in range(8):
            R.dma("pool", f"wg{kc}", lambda e, kc=kc: e.dma_start(out=wg[:, kc, :], in_=wg_d[:, kc, :]),
                  writes=[wgB[kc]])
            R.dma("pool", f"wu{kc}", lambda e, kc=kc: e.dma_start(out=wu[:, kc, :], in_=wu_d[:, kc, :]),
                  writes=[wuB[kc]])
        for g2 in range(NF // 2):
            R.dma("pool", f"wd{g2}", lambda e, g2=g2: e.dma_start(out=wd[:, 2 * g2:2 * g2 + 2, :],
                                                                  in_=wd_d[:, 2 * g2:2 * g2 + 2, :]),
                  writes=[wdB[2 * g2], wdB[2 * g2 + 1]])

        xv = x_view(xsrc)
        xo = x_view(xdst)
        prm_g = pb + OFF_GF
        ones, onesB = G["ones"], G["onesB"]

        def load(tb):
            i = tb % 2
            R.dma("sp", f"xC{i}", lambda e: e.dma_start(out=xb[i][:, :, :], in_=xv[:, :, tb * TB:(tb + 1) * TB]),
                  writes=xB[i])

        def norm_to(i, dst, dstB_of, gcol0):
            R.op("act", lambda e: e.activation(out=sq[:, 0:8, :], in_=xb[i][:, :, :], func=AF.Square),
                 reads=xB[i], writes=sqB)
            for kc in range(8):
                R.op("pe", lambda e, kc=kc: e.matmul(banks[7][:, :], ones[:, :], sq[:, kc, :],
                                                     start=(kc == 0), stop=(kc == 7)),
                     reads=sqB + [onesB], writes=[bankB[7]], signal=(kc == 7))
            R.op("act", lambda e: e.activation(out=lnv[:, :], in_=banks[7][:, :], func=AF.Ln,
                                               bias=prm[:, OFF_EPS:OFF_EPS + 1], scale=1.0 / D),
                 reads=[bankB[7]], writes=[lnvB])
            R.op("act", lambda e: e.activation(out=rstd[:, :], in_=lnv[:, :], func=AF.Exp, scale=-0.5),
                 reads=[lnvB], writes=[rstdB])
            for kc in range(8):
                R.op("dve", lambda e, kc=kc: e.scalar_tensor_tensor(
                    out=dst[:, kc, :], in0=xb[i][:, kc, :], scalar=prm[:, gcol0 + kc:gcol0 + kc + 1],
                    in1=rstd[:, :], op0=ALU.mult, op1=ALU.mult),
                    reads=[xB[i][kc], rstdB], writes=[dstB_of(kc)])

        def block(tb):
            i = tb % 2
            tsl = slice(tb * TB, (tb + 1) * TB)
            norm_to(i, hT, lambda kc: hB, prm_g)
            for f in range(NF):
                bg, bgB = banks[(2 * f) % 6], bankB[(2 * f) % 6]
                bu, buB = banks[(2 * f + 1) % 6], bankB[(2 * f + 1) % 6]
                for kc in range(8):
                    R.op("pe", lambda e, kc=kc, f=f, bg=bg: e.matmul(
                        bg[:, :], wg[:, kc, f * 128:(f + 1) * 128], hT[:, kc, :],
                        start=(kc == 0), stop=(kc == 7)), reads=[wgB[kc], hB], writes=[bgB], signal=(kc == 7))
                for kc in range(8):
                    R.op("pe", lambda e, kc=kc, f=f, bu=bu: e.matmul(
                        bu[:, :], wu[:, kc, f * 128:(f + 1) * 128], hT[:, kc, :],
                        start=(kc == 0), stop=(kc == 7)), reads=[wuB[kc], hB], writes=[buB], signal=(kc == 7))
                R.op("act", lambda e, f=f, bg=bg: e.activation(out=sg[f % 2][:, :], in_=bg[:, :], func=AF.Silu),
                     reads=[bgB], writes=[sgB[f % 2]])
                R.op("dve", lambda e, f=f, bu=bu: e.tensor_tensor(
                    out=aT[:, f, :], in0=bu[:, :], in1=sg[f % 2][:, :], op=ALU.mult),
                    reads=[buB, sgB[f % 2]], writes=[aB[f]])
            for oc in range(8):
                bk, bkB = banks[oc % 6], bankB[oc % 6]
                for f in range(NF):
                    R.op("pe", lambda e, f=f, oc=oc, bk=bk: e.matmul(
                        bk[:, :], wd[:, f, oc * 128:(oc + 1) * 128], aT[:, f, :],
                        start=(f == 0), stop=(f == NF - 1)), reads=[wdB[f], aB[f]], writes=[bkB],
                        signal=(f == NF - 1))
                R.op("dve", lambda e, oc=oc, bk=bk: e.tensor_tensor(
                    out=xb[i][:, oc, :], in0=bk[:, :], in1=xb[i][:, oc, :], op=ALU.add),
                    reads=[bkB, xB[i][oc]], writes=[xB[i][oc]])
            if final:
                norm_to(i, xb[i], lambda kc: xB[i][kc], OFF_GFIN)
            R.dma("pool", f"xstC{i}", lambda e: e.dma_start(out=xo[:, :, tsl], in_=xb[i][:, :, :]),
                  reads=xB[i])

        load(0)
        for tb in range(NTB):
            if tb + 1 < NTB:
                load(tb + 1)
            block(tb)
        R.barrier()
        R.emit()


def build_program(stop_after=None, debug=False):
    nc = bass.Bass("TRN2", target_bir_lowering=False)
    dk = "ExternalOutput" if debug else "Internal"
    G = {}
    xT = nc.dram_tensor("xT", [D, S], F32, kind="ExternalInput").ap()
    G["w_in"] = nc.dram_tensor("w_in", [DEPTH, D, INC], F32, kind="ExternalInput").ap()
    G["w_o"] = nc.dram_tensor("w_o", [DEPTH, D, D], F32, kind="ExternalInput").ap()
    G["w_gate"] = nc.dram_tensor("w_gate", [DEPTH, D, DFF], F32, kind="ExternalInput").ap()
    G["w_up"] = nc.dram_tensor("w_up", [DEPTH, D, DFF], F32, kind="ExternalInput").ap()
    G["w_down"] = nc.dram_tensor("w_down", [DEPTH, DFF, D], F32, kind="ExternalInput").ap()
    G["w_pool"] = nc.dram_tensor("w_pool", [DEPTH, 4, 64, 64], F32, kind="ExternalInput").ap()
    prm_d = nc.dram_tensor("prm", [128, NP], F32, kind="ExternalInput").ap()
    G["tbl"] = nc.dram_tensor("tbl", [128, 4, TW], F32, kind="ExternalInput").ap()
    G["ident"] = nc.dram_tensor("ident", [128, 128], F32, kind="ExternalInput").ap()
    yT = nc.dram_tensor("yT", [D, S], F32, kind="ExternalOutput").ap()
    xs = nc.dram_tensor("xs", [D, S], F32, kind=dk).ap()
    G["qT"] = nc.dram_tensor("qT", [4, 128, S], BF16, kind=dk).ap()
    G["kT"] = nc.dram_tensor("kT", [4, 128, S], BF16, kind=dk).ap()
    G["vS"] = nc.dram_tensor("vS", [S, 512], BF16, kind=dk).ap()
    G["mixT"] = nc.dram_tensor("mixT", [D, S], BF16, kind=dk).ap()

    with ExitStack() as es:
        R = Rec(nc, es)
        prm = es.enter_context(nc.sbuf_tensor("prm_sb", [128, NP], F32))
        ones = es.enter_context(nc.sbuf_tensor("ones_sb", [128, 128], BF16))
        G["prm"], G["ones"], G["onesB"] = prm, ones, Buf()
        prmB = Buf()
        R.dma("sp", "prm", lambda e: e.dma_start(out=prm[:, :], in_=prm_d[:, :]), writes=[prmB])
        R.op("dve", lambda e: e.memset(ones[:, :], 1.0), writes=[G["onesB"]])
        R.barrier()
        done = False
        for l in range(DEPTH):
            xin = xT if l == 0 else xs
            phase_A(R, nc, G, l, xin)
            if stop_after == (l, "A"):
                done = True
                break
            phase_B(R, nc, G, l, xin, xs)
            if stop_after == (l, "B"):
                done = True
                break
            last = (l == DEPTH - 1)
            phase_C(R, nc, G, l, xs, yT if last else xs, last)
            if stop_after == (l, "C"):
                done = True
                break
    return nc


def _rel_bucket_np(d):
    n = np.maximum(d, 0)
    nf = np.maximum(n, 1).astype(np.float32)
    large = 16 + (np.log(nf / np.float32(16)) / np.float32(math.log(128 / 16)) * np.float32(16)).astype(np.int32)
    large = np.minimum(large, 31)
    return np.where(n < 16, n, large)


def host_prep(inputs):
    f32 = np.float32
    g = {k: np.asarray(v, dtype=f32) for k, v in inputs.items()}
    prm = np.zeros((128, NP), f32)
    for l in range(DEPTH):
        pb = l * PL
        prm[:, pb + OFF_GM:pb + OFF_GM + 8] = g["g_mix"][l].reshape(8, 128).T
        prm[:, pb + OFF_GF:pb + OFF_GF + 8] = g["g_ffn"][l].reshape(8, 128).T
        prm[:, pb + OFF_PS:pb + OFF_PS + 2] = g["pool_scale"][l].reshape(2, 128).T
        cw = g["conv_w"][l].reshape(3, 2, 128)
        prm[:, pb + OFF_CW:pb + OFF_CW + 6] = cw.transpose(2, 0, 1).reshape(128, 6)
        prm[:, pb + OFF_SG:pb + OFF_SG + 128] = np.broadcast_to(g["subln_g"][l][None, :], (128, 128))
        lo = pb + OFF_LAM
        prm[:, lo:lo + 64] = g["lambda_q1"][l][None, :]
        prm[:, lo + 64:lo + 128] = g["lambda_k1"][l][None, :]
        prm[:, lo + 128:lo + 192] = g["lambda_q2"][l][None, :]
        prm[:, lo + 192:lo + 256] = g["lambda_k2"][l][None, :]
    prm[:, OFF_GFIN:OFF_GFIN + 8] = g["g_final"].reshape(8, 128).T
    prm[:, OFF_CH:OFF_CH + 4] = g["rel_bias"][31][None, :]
    wins = [2.0, 4.0, 8.0, 16.0]
    t1 = np.arange(1, 17, dtype=f32)
    for c in range(2):
        for half in range(2):
            w = wins[c * 2 + half]
            prm[half * 64:(half + 1) * 64, OFF_ID0 + c * 16:OFF_ID0 + c * 16 + 16] = \
                (1.0 / np.minimum(t1, w)).astype(f32)[None, :]
    prm[:, OFF_EPS] = EPS
    prm[:, OFF_SEPS] = SUBLN_EPS
    kl = np.arange(128)[:, None]
    m = np.arange(TW)[None, :]
    d = m - kl
    bidx = _rel_bucket_np(d)
    tbl = np.empty((128, 4, TW), f32)
    for h in range(4):
        tbl[:, h, :] = np.where(d >= 0, g["rel_bias"][:, h][bidx], f32(MASKV))
    ident = np.eye(128, dtype=f32)
    common = {
        "w_in": g["w_in"], "w_o": g["w_o"], "w_gate": g["w_gate"], "w_up": g["w_up"],
        "w_down": g["w_down"], "w_pool": g["w_pool"], "prm": prm, "tbl": tbl, "ident": ident,
    }
    return g, common


_NC_CACHE = {}


def kernel(**inputs):
    g, common = host_prep(inputs)
    x = g["x"]
    B = x.shape[0]
    if "nc" not in _NC_CACHE:
        _NC_CACHE["nc"] = build_program()
    nc = _NC_CACHE["nc"]
    in_maps = []
    for b in range(B):
        m = dict(common)
        m["xT"] = np.ascontiguousarray(x[b].T)
        in_maps.append(m)
    res = run_bass_kernel_spmd(nc, in_maps, core_ids=list(range(B)))
    out = np.stack([np.ascontiguousarray(res.results[b]["yT"].T) for b in range(B)], axis=0)
    return out.astype(np.float32)
```

```python
import math
from contextlib import ExitStack

import numpy as np
import concourse.bass as bass
import concourse.mybir as mybir
from concourse.bass_utils import run_bass_kernel_spmd

F32 = mybir.dt.float32
BF16 = mybir.dt.bfloat16
AF = mybir.ActivationFunctionType
ALU = mybir.AluOpType

S = 4096
D = 1024
DEPTH = 2
NH = 4
DFF = 2816
NF = DFF // 128
TB = 512
NTB = S // TB
INC = 2560
EPS = 1e-6
SUBLN_EPS = 1e-5
MASKV = -30000.0
TW = 640

PL = 8 + 8 + 2 + 6 + 128 + 256
OFF_GM, OFF_GF, OFF_PS, OFF_CW, OFF_SG, OFF_LAM = 0, 8, 16, 18, 24, 152
OFF_GFIN = DEPTH * PL
OFF_CH = OFF_GFIN + 8
OFF_ID0 = OFF_CH + 4
OFF_EPS = OFF_ID0 + 32
OFF_SEPS = OFF_EPS + 1
NP = OFF_SEPS + 1


class Buf:
    __slots__ = ("w", "r")

    def __init__(self):
        self.w = None
        self.r = []


class Rec:
    ENG = ("pe", "act", "dve", "pool", "sp")
    CE = ("pe", "act", "dve", "pool")

    def __init__(self, nc, es):
        self.nc = nc
        self.es = es
        self.sems = {}
        self.count = {}
        self.streams = {e: [] for e in self.ENG}
        self.waited = {e: {} for e in self.ENG}
        for e in self.CE:
            self._sem(e)

    def _sem(self, key):
        if key not in self.sems:
            self.sems[key] = self.es.enter_context(self.nc.semaphore("s_" + key))
            self.count[key] = 0
        return self.sems[key]

    def _collect(self, eng, reads, writes, extra):
        need = {}

        def add(tok, kind):
            if tok is None:
                return
            sem, val = tok
            if sem == eng:
                if eng == "pe":
                    return
                if val > self.count[eng]:
                    return
            if need.get(sem, 0) < val:
                need[sem] = val

        for b in reads:
            add(b.w, "raw")
        for b in writes:
            add(b.w, "waw")
            for t in b.r:
                add(t, "war")
        for t in extra:
            add(t, "raw")
        out = []
        wd = self.waited[eng]
        for sem, val in need.items():
            if wd.get(sem, 0) < val:
                wd[sem] = val
                out.append((sem, val))
        return out

    def op(self, eng, fn, reads=(), writes=(), signal=True, extra=()):
        waits = self._collect(eng, reads, writes, extra)
        if signal:
            self.count[eng] += 1
            tok = (eng, self.count[eng])
            inc = (eng, 1)
        else:
            tok = (eng, self.count[eng] + 1)
            inc = None
        for b in reads:
            b.r.append(tok)
        for b in writes:
            b.w = tok
            b.r = []
        self.streams[eng].append((waits, fn, inc))
        return tok

    def dma(self, q, semkey, fn, reads=(), writes=(), extra=()):
        self._sem(semkey)
        waits = self._collect(q, reads, writes, extra)
        self.count[semkey] += 16
        tok = (semkey, self.count[semkey])
        for b in reads:
            b.r.append(tok)
        for b in writes:
            b.w = tok
            b.r = []
        self.streams[q].append((waits, fn, (semkey, 16)))
        return tok

    def barrier(self):
        toks = [(k, v) for k, v in self.count.items() if v > 0]
        for eng in self.ENG:
            waits = []
            wd = self.waited[eng]
            for sem, val in toks:
                if sem == eng:
                    continue
                if wd.get(sem, 0) < val:
                    wd[sem] = val
                    waits.append((sem, val))
            if waits:
                self.streams[eng].append((waits, None, None))

    def emit(self):
        nc = self.nc
        with nc.Block() as block:
            def mk(name):
                def run(e):
                    for waits, fn, inc in self.streams[name]:
                        for sem, val in waits:
                            e.wait_ge(self.sems[sem], val)
                        if fn is not None:
                            ins = fn(e)
                            if inc is not None:
                                ins.then_inc(self.sems[inc[0]], inc[1])
                return run
            block.tensor(mk("pe"))
            block.scalar(mk("act"))
            block.vector(mk("dve"))
            block.gpsimd(mk("pool"))
            block.sync(mk("sp"))
        for k in self.streams:
            self.streams[k] = []


def emit_norm(R, G, xt, xB, sq, sqB, ssbank, ssB, lnv, lnvB, rstd, rstdB, out, outB, gcol0):
    prm, ones, onesB = G["prm"], G["ones"], G["onesB"]
    R.op("act", lambda e: e.activation(out=sq[:, :, :], in_=xt[:, :, :], func=AF.Square),
         reads=[xB], writes=sqB)
    for kc in range(8):
        R.op("pe", lambda e, kc=kc: e.matmul(ssbank[:, :], ones[:, :], sq[:, kc, :],
                                             start=(kc == 0), stop=(kc == 7)),
             reads=sqB + [onesB], writes=[ssB], signal=(kc == 7))
    R.op("act", lambda e: e.activation(out=lnv[:, :], in_=ssbank[:, :], func=AF.Ln,
                                       bias=prm[:, OFF_EPS:OFF_EPS + 1], scale=1.0 / D),
         reads=[ssB], writes=[lnvB])
    R.op("act", lambda e: e.activation(out=rstd[:, :], in_=lnv[:, :], func=AF.Exp, scale=-0.5),
         reads=[lnvB], writes=[rstdB])
    for kc in range(8):
        R.op("dve", lambda e, kc=kc: e.scalar_tensor_tensor(
            out=out[:, kc, :], in0=xt[:, kc, :], scalar=prm[:, gcol0 + kc:gcol0 + kc + 1],
            in1=rstd[:, :], op0=ALU.mult, op1=ALU.mult),
            reads=[xB, rstdB], writes=[outB])


def x_view(ap2d):
    return ap2d.rearrange("(c p) n -> p c n", p=128)


def phase_A(R, nc, G, l, xsrc):
    prm = G["prm"]
    pb = l * PL
    win_d = G["w_in"][l].rearrange("(kc p) n -> p kc n", p=128)
    with ExitStack() as st:
        def sb(name, shape, dt):
            return st.enter_context(nc.sbuf_tensor(f"{name}_L{l}", shape, dt))

        win = sb("winA", [128, 8, INC], BF16)
        xb = [sb(f"xbA{i}", [128, 8, TB], F32) for i in range(2)]
        sq = sb("sqA", [128, 8, TB], BF16)
        hT = [sb(f"hTA{i}", [128, 8, TB], BF16) for i in range(2)]
        lnv = sb("lnvA", [128, TB], F32)
        rstd = sb("rstdA", [128, TB], F32)
        qst = sb("qstA", [128, 4, TB], BF16)
        kst = sb("kstA", [128, 4, TB], BF16)
        vst = sb("vstA", [128, 4, 512], BF16)
        mixst = sb("mixstA", [128, 4, TB], BF16)
        Pb = sb("PbA", [128, 2, 528], F32)
        Ub = sb("UbA", [128, 2, 528], F32)
        gcS = sb("gcSA", [128, 2, TB], F32)
        s2 = sb("s2A", [128, 2, 528], F32)
        s4 = sb("s4A", [128, 2, 528], F32)
        s8 = sb("s8A", [128, 528], F32)
        s16 = sb("s16A", [128, 528], F32)
        tmpf = sb("tmpfA", [128, 16], F32)
        pooled = sb("pooledA", [128, 2, TB], BF16)
        yv = sb("yvA", [128, 2, TB], F32)
        wblk = sb("wblkA", [128, 2, 128], BF16)
        banks = [st.enter_context(nc.psum_tensor(f"L{l}bkA{i}", [128, 512], F32)) for i in range(8)]
        bankB = [Buf() for _ in range(8)]

        winB = [Buf() for _ in range(5)]
        xB = [Buf(), Buf()]
        sqB = [Buf()]
        hB = [Buf(), Buf()]
        lnvB, rstdB, qstB, kstB, vstB, mixB = Buf(), Buf(), Buf(), Buf(), Buf(), Buf()
        PbB, UbB, gcB, s2B, s4B, s8B, s16B, tmpB, poolB, yvB, wblkB = (Buf() for _ in range(11))

        for cg in (0, 1, 3, 4, 2):
            R.dma("pool", f"win{cg}", lambda e, cg=cg: e.dma_start(
                out=win[:, :, cg * 512:(cg + 1) * 512], in_=win_d[:, :, cg * 512:(cg + 1) * 512]),
                writes=[winB[cg]])
        R.op("dve", lambda e: e.memset(wblk[:, :, :], 0.0), writes=[wblkB])
        for g in range(4):
            r0 = (g % 2) * 64
            R.dma("pool", "wblk", lambda e, g=g, r0=r0: e.dma_start(
                out=wblk[r0:r0 + 64, g // 2, r0:r0 + 64], in_=G["w_pool"][l, g, :, :]),
                writes=[wblkB])
        R.op("dve", lambda e: e.memset(Pb[:, :, 0:16], 0.0), writes=[PbB])
        R.op("dve", lambda e: e.memset(Ub[:, :, 0:16], 0.0), writes=[UbB])

        xv = x_view(xsrc)

        def load(tb):
            i = tb % 2
            R.dma("sp", f"xA{i}", lambda e: e.dma_start(out=xb[i][:, :, :], in_=xv[:, :, tb * TB:(tb + 1) * TB]),
                  writes=[xB[i]])

        def norm(tb):
            i = tb % 2
            emit_norm(R, G, xb[i], xB[i], sq, sqB, banks[7], bankB[7], lnv, lnvB, rstd, rstdB,
                      hT[i], hB[i], pb + OFF_GM)

        rr = [0]

        def nb():
            i = rr[0] % 5
            rr[0] += 1
            return banks[i], bankB[i]

        def mm_chain(bank, bB, h, hb, col, tsub=None):
            for kc in range(8):
                if tsub is None:
                    fn = lambda e, kc=kc: e.matmul(bank[:, :], win[:, kc, col:col + 128], h[:, kc, :],
                                                   start=(kc == 0), stop=(kc == 7))
                else:
                    fn = lambda e, kc=kc: e.matmul(bank[:, :], h[:, kc, tsub * 128:(tsub + 1) * 128],
                                                   win[:, kc, 1024:1536], start=(kc == 0), stop=(kc == 7))
                R.op("pe", fn, reads=[winB[2 if tsub is not None else col // 512], hb], writes=[bB],
                     signal=(kc == 7))

        load(0)
        load(1)
        norm(0)
        for tb in range(NTB):
            i = tb % 2
            h, hb = hT[i], hB[i]
            tsl = slice(tb * TB, (tb + 1) * TB)
            for hh in range(4):
                bank, bB = nb()
                mm_chain(bank, bB, h, hb, hh * 128)
                R.op("act", lambda e, bank=bank, hh=hh: e.mul(out=qst[:, hh, :], in_=bank[:, :], mul=0.125),
                     reads=[bB], writes=[qstB])
            for hh in range(4):
                bank, bB = nb()
                mm_chain(bank, bB, h, hb, 512 + hh * 128)
                R.op("act", lambda e, bank=bank, hh=hh: e.copy(out=kst[:, hh, :], in_=bank[:, :]),
                     reads=[bB], writes=[kstB])
            R.dma("pool", "qstore", lambda e, tsl=tsl: e.dma_start(
                out=G["qT"].rearrange("h p n -> p h n")[:, :, tsl], in_=qst[:, :, :]), reads=[qstB])
            R.dma("pool", "kstore", lambda e, tsl=tsl: e.dma_start(
                out=G["kT"].rearrange("h p n -> p h n")[:, :, tsl], in_=kst[:, :, :]), reads=[kstB])
            if tb + 1 < NTB:
                norm(tb + 1)
            for c in range(2):
                bank, bB = nb()
                mm_chain(bank, bB, h, hb, 1536 + c * 128)
                R.op("act", lambda e, bank=bank, c=c: e.copy(out=Pb[:, c, 16:528], in_=bank[:, :]),
                     reads=[bB], writes=[PbB])
            for c in range(2):
                bank, bB = nb()
                mm_chain(bank, bB, h, hb, 2048 + c * 128)
                R.op("act", lambda e, bank=bank, c=c: e.copy(out=gcS[:, c, :], in_=bank[:, :]),
                     reads=[bB], writes=[gcB])
            for c in range(2):
                bank, bB = nb()
                mm_chain(bank, bB, h, hb, 2304 + c * 128)
                R.op("dve", lambda e, bank=bank, c=c: e.tensor_tensor(
                    out=Ub[:, c, 16:528], in0=bank[:, :], in1=gcS[:, c, :], op=ALU.mult),
                    reads=[bB, gcB], writes=[UbB])
            gbb = []
            for c in range(2):
                bank, bB = banks[5 + c], bankB[5 + c]
                mm_chain(bank, bB, h, hb, 1792 + c * 128)
                gbb.append((bank, bB))
            for ts in range(4):
                bank, bB = nb()
                mm_chain(bank, bB, h, hb, 0, tsub=ts)
                R.op("act", lambda e, bank=bank, ts=ts: e.copy(out=vst[:, ts, :], in_=bank[:, :]),
                     reads=[bB], writes=[vstB])
            R.dma("pool", "vstore", lambda e, tb=tb: e.dma_start(
                out=G["vS"].rearrange("(t p) f -> p t f", p=128)[:, tb * 4:(tb + 1) * 4, :], in_=vst[:, :, :]),
                reads=[vstB])
            for c in range(2):
                cw = lambda k, c=c: prm[:, pb + OFF_CW + k * 2 + c:pb + OFF_CW + k * 2 + c + 1]
                R.op("dve", lambda e, c=c, cw=cw: e.tensor_scalar(
                    out=yv[:, c, :], in0=Ub[:, c, 14:526], scalar1=cw(0), scalar2=None, op0=ALU.mult),
                    reads=[UbB], writes=[yvB])
                R.op("dve", lambda e, c=c, cw=cw: e.scalar_tensor_tensor(
                    out=yv[:, c, :], in0=Ub[:, c, 15:527], scalar=cw(1), in1=yv[:, c, :],
                    op0=ALU.mult, op1=ALU.add), reads=[UbB, yvB], writes=[yvB])
                R.op("dve", lambda e, c=c, cw=cw: e.scalar_tensor_tensor(
                    out=yv[:, c, :], in0=Ub[:, c, 16:528], scalar=cw(2), in1=yv[:, c, :],
                    op0=ALU.mult, op1=ALU.add), reads=[UbB, yvB], writes=[yvB])
                bank, bB = gbb[c]
                R.op("dve", lambda e, c=c, bank=bank: e.tensor_tensor(
                    out=mixst[:, 2 + c, :], in0=bank[:, :], in1=yv[:, c, :], op=ALU.mult),
                    reads=[bB, yvB], writes=[mixB])
            R.op("dve", lambda e: e.tensor_copy(out=Ub[:, :, 14:16], in_=Ub[:, :, 526:528]),
                 reads=[UbB], writes=[UbB])
            R.op("dve", lambda e: e.tensor_tensor(out=s2[:, :, 1:528], in0=Pb[:, :, 1:528], in1=Pb[:, :, 0:527],
                                                  op=ALU.add), reads=[PbB], writes=[s2B])
            R.op("dve", lambda e: e.tensor_tensor(out=s4[:, :, 3:528], in0=s2[:, :, 3:528], in1=s2[:, :, 1:526],
                                                  op=ALU.add), reads=[s2B], writes=[s4B])
            R.op("dve", lambda e: e.tensor_tensor(out=s8[:, 7:528], in0=s4[:, 1, 7:528], in1=s4[:, 1, 3:524],
                                                  op=ALU.add), reads=[s4B], writes=[s8B])
            R.op("dve", lambda e: e.tensor_tensor(out=s16[64:128, 15:528], in0=s8[64:128, 15:528],
                                                  in1=s8[64:128, 7:520], op=ALU.add), reads=[s8B], writes=[s16B])
            grp = [(s2, lambda r: s2[r, 0, 16:528], 0, 0, 2.0, s2B),
                   (s4, lambda r: s4[r, 0, 16:528], 64, 0, 4.0, s4B),
                   (s8, lambda r: s8[r, 16:528], 0, 1, 8.0, s8B),
                   (s16, lambda r: s16[r, 16:528], 64, 1, 16.0, s16B)]
            for (_, src, r0, c, w, sB) in grp:
                rs = slice(r0, r0 + 64)
                R.op("dve", lambda e, src=src, rs=rs, c=c, w=w: e.scalar_tensor_tensor(
                    out=pooled[rs, c, :], in0=src(rs), scalar=1.0 / w, in1=Pb[rs, c, 16:528],
                    op0=ALU.mult, op1=ALU.subtract), reads=[sB, PbB], writes=[poolB])
            if tb == 0:
                for (_, src, r0, c, w, sB) in grp:
                    rs = slice(r0, r0 + 64)
                    R.op("dve", lambda e, src=src, rs=rs, c=c: e.tensor_tensor(
                        out=tmpf[rs, :], in0=src(rs)[:, 0:16],
                        in1=prm[rs, OFF_ID0 + c * 16:OFF_ID0 + c * 16 + 16], op=ALU.mult),
                        reads=[sB], writes=[tmpB])
                    R.op("dve", lambda e, rs=rs, c=c: e.tensor_tensor(
                        out=pooled[rs, c, 0:16], in0=tmpf[rs, :], in1=Pb[rs, c, 16:32], op=ALU.subtract),
                        reads=[tmpB, PbB], writes=[poolB])
            R.op("dve", lambda e: e.tensor_copy(out=Pb[:, :, 0:16], in_=Pb[:, :, 512:528]),
                 reads=[PbB], writes=[PbB])
            for c in range(2):
                bank, bB = nb()
                R.op("pe", lambda e, bank=bank, c=c: e.matmul(bank[:, :], wblk[:, c, :], pooled[:, c, :],
                                                              start=True, stop=True),
                     reads=[wblkB, poolB], writes=[bB])
                R.op("dve", lambda e, bank=bank, c=c: e.tensor_scalar(
                    out=mixst[:, c, :], in0=bank[:, :], scalar1=prm[:, pb + OFF_PS + c:pb + OFF_PS + c + 1],
                    scalar2=None, op0=ALU.mult), reads=[bB], writes=[mixB])
            R.dma("pool", "mixstore", lambda e, tsl=tsl: e.dma_start(
                out=x_view(G["mixT"])[:, 4:8, tsl], in_=mixst[:, :, :]), reads=[mixB])
            if tb + 2 < NTB:
                load(tb + 2)
        R.barrier()
        R.emit()


def phase_B(R, nc, G, l, xsrc, xdst):
    prm = G["prm"]
    pb = l * PL
    lam_init = 0.8 - 0.6 * math.exp(-0.3 * l)
    wo_d = G["w_o"][l].rearrange("(kc p) n -> p kc n", p=128)
    with ExitStack() as st:
        def sb(name, shape, dt):
            return st.enter_context(nc.sbuf_tensor(f"{name}_L{l}", shape, dt))

        KT = sb("KTB", [128, 4, S], BF16)
        V = sb("VB", [128, 32, 4, 129], BF16)
        wo = sb("woB", [128, 8, D], BF16)
        tbl = sb("tblB", [128, 4, TW], F32)
        Qb = [sb(f"QbB{i}", [128, 4, TB], BF16) for i in range(2)]
        PT = [sb(f"PTB{i}", [128, 2, TB], BF16) for i in range(2)]
        mixb = [sb(f"mixbB{i}", [128, 8, TB], BF16) for i in range(2)]
        xb = [sb(f"xbB{i}", [128, 8, TB], F32) for i in range(2)]
        accS = sb("accSB", [128, 8, 129], F32)
        rl = sb("rlB", [128, 2, 4], F32)
        r2n = sb("r2nB", [128, 4], F32)
        tt = sb("ttB", [128, 128], F32)
        attf = sb("attfB", [128, 4, 128], F32)
        junk = sb("junkB", [128, 128], F32)
        ss = sb("ssB", [128, 4], F32)
        vv = sb("vvB", [128, 4], F32)
        nhalf = sb("nhalfB", [128, 4], F32)
        rstd = sb("rstdB", [128, 4], F32)
        attn = sb("attnB", [128, 4, 128], BF16)
        ident = sb("identB", [128, 128], BF16)
        Gp = sb("GpB", [128, 128], F32)
        lamt = sb("lamtB", [128, 8], F32)
        lprod = sb("lprodB", [128, 64], F32)
        Sp = [st.enter_context(nc.psum_tensor(f"L{l}SpB{i}", [128, 2, 512], F32)) for i in range(2)]
        accb = [st.enter_context(nc.psum_tensor(f"L{l}accB{i}", [128, 3, 129], F32)) for i in range(3)]
        trb = st.enter_context(nc.psum_tensor(f"L{l}trB", [128, 512], BF16))

        SB_ = [Buf(), Buf()]
        WOB = [[Buf(), Buf()], [Buf(), Buf()]]
        PTB = [Buf(), Buf()]
        accB = [Buf() for _ in range(8)]
        trB = Buf()
        KTB = [Buf() for _ in range(NTB)]
        VBf = [Buf() for _ in range(NTB)]
        QB = [Buf(), Buf()]
        mixatt = [[Buf() for _ in range(4)] for _ in range(2)]
        mixpc = [Buf(), Buf()]
        xB = [[Buf() for _ in range(8)] for _ in range(2)]
        woB, tblB, identB, GpB, lamB, lprodB, nhB = (Buf() for _ in range(7))
        accSB, rlB, r2nB, ttB, attfB, junkB, ssB_, vvB, rstdB, attnB = (Buf() for _ in range(10))

        def acc(idx):
            return accb[idx // 3][:, idx % 3, :]

        R.dma("sp", "tblB", lambda e: e.dma_start(out=tbl[:, :, :], in_=G["tbl"][:, :, :]), writes=[tblB])
        R.op("dve", lambda e: e.memset(V[:, :, :, 128:129], 1.0), writes=VBf)
        R.op("dve", lambda e: e.memset(nhalf[:, :], -0.5), writes=[nhB])
        kT_v = G["kT"].rearrange("h p n -> p h n")
        vS_v = G["vS"].rearrange("(t p) (h d) -> p t h d", p=128, h=4)
        for tb in range(NTB):
            R.dma("sp", f"kld{tb}", lambda e, tb=tb: e.dma_start(
                out=KT[:, :, tb * TB:(tb + 1) * TB], in_=kT_v[:, :, tb * TB:(tb + 1) * TB]), writes=[KTB[tb]])
            for t4 in range(4):
                R.dma("sp", f"vld{tb}", lambda e, t=tb * 4 + t4: e.dma_start(
                    out=V[:, t, :, 0:128], in_=vS_v[:, t, :, :]), writes=[VBf[tb]])
        R.dma("pool", "identB", lambda e: e.dma_start(out=ident[:, :], in_=G["ident"][:, :]), writes=[identB])
        R.dma("pool", "woB", lambda e: e.dma_start(out=wo[:, :, :], in_=wo_d[:, :, :]), writes=[woB])
        lo = pb + OFF_LAM
        for j in range(2):
            R.op("dve", lambda e, j=j: e.tensor_tensor(
                out=lprod[:, :], in0=prm[:, lo + j * 128:lo + j * 128 + 64],
                in1=prm[:, lo + j * 128 + 64:lo + j * 128 + 128], op=ALU.mult), writes=[lprodB])
            R.op("dve", lambda e, j=j: e.reduce_sum(out=lamt[:, j:j + 1], in_=lprod[:, :],
                                                   axis=mybir.AxisListType.X),
                 reads=[lprodB], writes=[lamB])
        R.op("act", lambda e: e.activation(out=lamt[:, 2:4], in_=lamt[:, 0:2], func=AF.Exp),
             reads=[lamB], writes=[lamB])
        R.op("dve", lambda e: e.tensor_tensor(out=lamt[:, 4:5], in0=lamt[:, 2:3], in1=lamt[:, 3:4],
                                              op=ALU.subtract), reads=[lamB], writes=[lamB])
        R.op("dve", lambda e: e.tensor_scalar(out=lamt[:, 5:6], in0=lamt[:, 4:5], scalar1=lam_init,
                                              scalar2=-1.0, op0=ALU.add, op1=ALU.mult),
             reads=[lamB], writes=[lamB])
        R.op("dve", lambda e: e.tensor_scalar(out=Gp[:, :], in0=prm[:, pb + OFF_SG:pb + OFF_SG + 128],
                                              scalar1=1.0 - lam_init, scalar2=None, op0=ALU.mult),
             writes=[GpB])
        nlam = lamt[:, 5:6]

        xv = x_view(xsrc)
        xo = x_view(xdst)
        qT_v = G["qT"].rearrange("h p n -> p h n")
        mix_v = x_view(G["mixT"])

        def loads(qb):
            i = qb % 2
            tsl = slice(qb * TB, (qb + 1) * TB)
            R.dma("sp", f"qld{i}", lambda e: e.dma_start(out=Qb[i][:, :, :], in_=qT_v[:, :, tsl]), writes=[QB[i]])
            R.dma("sp", f"mld{i}", lambda e: e.dma_start(out=mixb[i][:, 4:8, :], in_=mix_v[:, 4:8, tsl]),
                  writes=[mixpc[i]])
            R.dma("sp", f"xld{i}", lambda e: e.dma_start(out=xb[i][:, :, :], in_=xv[:, :, tsl]), writes=xB[i])

        pending = []

        def flush_pending():
            while pending:
                pending.pop(0)()

        def attn_head(qb, h):
            i = qb % 2
            nk = 4 * (qb + 1)

            def qk(kt):
                j = kt - 4 * qb
                qlo = max(0, 128 * j)
                p = kt % 2
                for m in range(2):
                    R.op("pe", lambda e, m=m, p=p, kt=kt, qlo=qlo: e.matmul(
                        Sp[p][:, m, qlo:512], KT[m * 64:(m + 1) * 64, h, kt * 128:(kt + 1) * 128],
                        Qb[i][m * 64:(m + 1) * 64, h, qlo:512], start=True, stop=True),
                        reads=[KTB[kt // 4], QB[i]], writes=[SB_[p], WOB[p][0], WOB[p][1]], signal=(m == 1))

            def soft(kt):
                j = kt - 4 * qb
                qlo = max(0, 128 * j)
                p = kt % 2
                if j >= -1:
                    mlo = 128 if j < 0 else 0
                    wdt = 512 - qlo
                    for m in range(2):
                        R.op("dve", lambda e, m=m, p=p, qlo=qlo, mlo=mlo, wdt=wdt: e.tensor_tensor(
                            out=Sp[p][:, m, qlo:512], in0=Sp[p][:, m, qlo:512],
                            in1=tbl[:, h, mlo:mlo + wdt], op=ALU.add),
                            reads=[SB_[p], tblB], writes=[SB_[p]])
                    R.op("act", lambda e, p=p, qlo=qlo: e.activation(
                        out=PT[p][:, :, qlo:512], in_=Sp[p][:, :, qlo:512], func=AF.Exp),
                        reads=[SB_[p]], writes=[PTB[p]])
                else:
                    R.op("act", lambda e, p=p: e.activation(
                        out=PT[p][:, :, :], in_=Sp[p][:, :, :], func=AF.Exp,
                        bias=prm[:, OFF_CH + h:OFF_CH + h + 1]),
                        reads=[SB_[p]], writes=[PTB[p]])

            def pv(kt):
                j = kt - 4 * qb
                p = kt % 2
                s0 = max(j, 0)
                seen = set()
                for m in range(2):
                    for sub in range(s0, 4):
                        idx = m * 4 + sub
                        last = (kt == 4 * qb + sub)
                        bnk = idx // 3
                        st_ = (kt == 0 and bnk not in seen)
                        seen.add(bnk)
                        R.op("pe", lambda e, m=m, p=p, sub=sub, kt=kt, last=last, st_=st_, idx=idx: e.matmul(
                            acc(idx), PT[p][:, m, sub * 128:(sub + 1) * 128], V[:, kt, h, :],
                            start=st_, stop=last, skip_group_check=True),
                            reads=[PTB[p], VBf[kt // 4]], writes=[accB[idx]],
                            signal=(last or (m == 1 and sub == 3)))

            defer_at = min(nk - 1, 7)
            qk(0)
            for kt in range(nk):
                if kt + 1 < nk:
                    qk(kt + 1)
                soft(kt)
                pv(kt)
                if kt == defer_at:
                    flush_pending()
            for bnk in range(3):
                n = 3 if bnk < 2 else 2
                R.op("dve", lambda e, bnk=bnk, n=n: e.tensor_copy(
                    out=accS[:, 3 * bnk:3 * bnk + n, :], in_=accb[bnk][:, 0:n, :]),
                    reads=accB[3 * bnk:3 * bnk + n], writes=[accSB])
            for m in range(2):
                R.op("dve", lambda e, m=m: e.reciprocal(out=rl[:, m, :], in_=accS[:, 4 * m:4 * m + 4, 128]),
                     reads=[accSB], writes=[rlB])
            R.op("dve", lambda e: e.tensor_scalar(out=r2n[:, :], in0=rl[:, 1, :], scalar1=nlam,
                                                  scalar2=None, op0=ALU.mult),
                 reads=[rlB, lamB], writes=[r2nB])
            for sub in range(4):
                R.op("dve", lambda e, sub=sub: e.tensor_scalar(
                    out=tt[:, :], in0=accS[:, sub, 0:128], scalar1=rl[:, 0, sub:sub + 1], scalar2=None,
                    op0=ALU.mult), reads=[accSB, rlB], writes=[ttB])
                R.op("dve", lambda e, sub=sub: e.scalar_tensor_tensor(
                    out=attf[:, sub, :], in0=accS[:, 4 + sub, 0:128], scalar=r2n[:, sub:sub + 1],
                    in1=tt[:, :], op0=ALU.mult, op1=ALU.add),
                    reads=[accSB, r2nB, ttB], writes=[attfB])
                R.op("dve", lambda e, sub=sub: e.scalar_tensor_tensor(
                    out=junk[:, :], in0=attf[:, sub, :], scalar=1.0, in1=attf[:, sub, :],
                    op0=ALU.mult, op1=ALU.mult, accum_out=ss[:, sub:sub + 1]),
                    reads=[attfB], writes=[junkB, ssB_])
            R.op("dve", lambda e: e.tensor_scalar(out=vv[:, :], in0=ss[:, :], scalar1=1.0 / 128,
                                                  scalar2=SUBLN_EPS, op0=ALU.mult, op1=ALU.add),
                 reads=[ssB_], writes=[vvB])
            R.op("pool", lambda e: e.tensor_tensor(out=rstd[:, :], in0=vv[:, :], in1=nhalf[:, :], op=ALU.pow),
                 reads=[vvB, nhB], writes=[rstdB])
            for sub in range(4):
                R.op("dve", lambda e, sub=sub: e.scalar_tensor_tensor(
                    out=attn[:, sub, :], in0=attf[:, sub, :], scalar=rstd[:, sub:sub + 1], in1=Gp[:, :],
                    op0=ALU.mult, op1=ALU.mult), reads=[attfB, rstdB, GpB], writes=[attnB])

            def stage2():
                for sub in range(4):
                    R.op("pe", lambda e, sub=sub: e.transpose(trb[:, sub * 128:(sub + 1) * 128], attn[:, sub, :],
                                                              ident[:, :]),
                         reads=[attnB, identB], writes=[trB], signal=(sub == 3))
                R.op("dve", lambda e: e.tensor_copy(out=mixb[i][:, h, :], in_=trb[:, :]),
                     reads=[trB], writes=[mixatt[i][h]])
            pending.append(stage2)

        def wo_block(qb):
            i = qb % 2
            tsl = slice(qb * TB, (qb + 1) * TB)
            for oc in range(8):
                p, m = (oc // 2) % 2, oc % 2
                bB = WOB[p][m]
                for kc in (4, 5, 6, 7, 0, 1, 2, 3):
                    rd = [woB, mixatt[i][kc]] if kc < 4 else [woB, mixpc[i]]
                    R.op("pe", lambda e, p=p, m=m, kc=kc, oc=oc: e.matmul(
                        Sp[p][:, m, :], wo[:, kc, oc * 128:(oc + 1) * 128], mixb[i][:, kc, :],
                        start=(kc == 4), stop=(kc == 3)), reads=rd, writes=[bB, SB_[p]], signal=(kc == 3))
                R.op("dve", lambda e, p=p, m=m, oc=oc: e.tensor_tensor(
                    out=xb[i][:, oc, :], in0=Sp[p][:, m, :], in1=xb[i][:, oc, :], op=ALU.add),
                    reads=[bB, xB[i][oc]], writes=[xB[i][oc]])
            R.dma("pool", f"xst{i}", lambda e: e.dma_start(out=xo[:, :, tsl], in_=xb[i][:, :, :]),
                  reads=xB[i])

        loads(0)
        for qb in range(NTB):
            if qb + 1 < NTB:
                loads(qb + 1)
            for h in range(NH):
                attn_head(qb, h)
            flush_pending()
            wo_block(qb)
        R.barrier()
        R.emit()


def phase_C(R, nc, G, l, xsrc, xdst, final):
    prm = G["prm"]
    pb = l * PL
    wg_d = G["w_gate"][l].rearrange("(kc p) n -> p kc n", p=128)
    wu_d = G["w_up"][l].rearrange("(kc p) n -> p kc n", p=128)
    wd_d = G["w_down"][l].rearrange("(f p) n -> p f n", p=128)
    with ExitStack() as st:
        def sb(name, shape, dt):
            return st.enter_context(nc.sbuf_tensor(f"{name}_L{l}", shape, dt))

        wg = sb("wgC", [128, 8, DFF], BF16)
        wu = sb("wuC", [128, 8, DFF], BF16)
        wd = sb("wdC", [128, NF, D], BF16)
        xb = [sb(f"xbC{i}", [128, 8, TB], F32) for i in range(2)]
        hT = sb("hTC", [128, 8, TB], BF16)
        aT = sb("aTC", [128, NF, TB], BF16)
        lnv = sb("lnvC", [128, TB], F32)
        rstd = sb("rstdC", [128, TB], F32)
        sg = [sb(f"sgC{i}", [128, TB], F32) for i in range(2)]
        banks = [st.enter_context(nc.psum_tensor(f"L{l}bkC{i}", [128, 512], F32)) for i in range(8)]
        bankB = [Buf() for _ in range(8)]
        wgB = [Buf() for _ in range(NF // 2)]
        wuB = [Buf() for _ in range(NF // 2)]
        wdB = [Buf() for _ in range(NF)]
        xB = [[Buf() for _ in range(8)] for _ in range(2)]
        hB, lnvB, rstdB = Buf(), Buf(), Buf()
        aB = [Buf() for _ in range(NF)]
        sgB = [Buf(), Buf()]
        sq = aT
        sqB = aB[0:8]

        for g2 in range(NF // 2):
            csl = slice(g2 * 256, (g2 + 1) * 256)
            R.dma("pool", f"wg{g2}", lambda e, csl=csl: e.dma_start(out=wg[:, :, csl], in_=wg_d[:, :, csl]),
                  writes=[wgB[g2]])
            R.dma("pool", f"wu{g2}", lambda e, csl=csl: e.dma_start(out=wu[:, :, csl], in_=wu_d[:, :, csl]),
                  writes=[wuB[g2]])
        for g2 in range(NF // 2):
            R.dma("pool", f"wd{g2}", lambda e, g2=g2: e.dma_start(out=wd[:, 2 * g2:2 * g2 + 2, :],
                                                                  in_=wd_d[:, 2 * g2:2 * g2 + 2, :]),
                  writes=[wdB[2 * g2], wdB[2 * g2 + 1]])

        xv = x_view(xsrc)
        xo = x_view(xdst)
        prm_g = pb + OFF_GF
        ones, onesB = G["ones"], G["onesB"]

        def load(tb):
            i = tb % 2
            R.dma("sp", f"xC{i}", lambda e: e.dma_start(out=xb[i][:, :, :], in_=xv[:, :, tb * TB:(tb + 1) * TB]),
                  writes=xB[i])

        def norm_to(i, dst, dstB_of, gcol0):
            R.op("act", lambda e: e.activation(out=sq[:, 0:8, :], in_=xb[i][:, :, :], func=AF.Square),
                 reads=xB[i], writes=sqB)
            for kc in range(8):
                R.op("pe", lambda e, kc=kc: e.matmul(banks[7][:, :], ones[:, :], sq[:, kc, :],
                                                     start=(kc == 0), stop=(kc == 7)),
                     reads=sqB + [onesB], writes=[bankB[7]], signal=(kc == 7))
            R.op("act", lambda e: e.activation(out=lnv[:, :], in_=banks[7][:, :], func=AF.Ln,
                                               bias=prm[:, OFF_EPS:OFF_EPS + 1], scale=1.0 / D),
                 reads=[bankB[7]], writes=[lnvB])
            R.op("act", lambda e: e.activation(out=rstd[:, :], in_=lnv[:, :], func=AF.Exp, scale=-0.5),
                 reads=[lnvB], writes=[rstdB])
            for kc in range(8):
                R.op("dve", lambda e, kc=kc: e.scalar_tensor_tensor(
                    out=dst[:, kc, :], in0=xb[i][:, kc, :], scalar=prm[:, gcol0 + kc:gcol0 + kc + 1],
                    in1=rstd[:, :], op0=ALU.mult, op1=ALU.mult),
                    reads=[xB[i][kc], rstdB], writes=[dstB_of(kc)])

        def block(tb):
            i = tb % 2
            tsl = slice(tb * TB, (tb + 1) * TB)
            norm_to(i, hT, lambda kc: hB, prm_g)
            for f in range(NF):
                bg, bgB = banks[(2 * f) % 6], bankB[(2 * f) % 6]
                bu, buB = banks[(2 * f + 1) % 6], bankB[(2 * f + 1) % 6]
                for kc in range(8):
                    R.op("pe", lambda e, kc=kc, f=f, bg=bg: e.matmul(
                        bg[:, :], wg[:, kc, f * 128:(f + 1) * 128], hT[:, kc, :],
                        start=(kc == 0), stop=(kc == 7)), reads=[wgB[f // 2], hB], writes=[bgB], signal=(kc == 7))
                for kc in range(8):
                    R.op("pe", lambda e, kc=kc, f=f, bu=bu: e.matmul(
                        bu[:, :], wu[:, kc, f * 128:(f + 1) * 128], hT[:, kc, :],
                        start=(kc == 0), stop=(kc == 7)), reads=[wuB[f // 2], hB], writes=[buB], signal=(kc == 7))
                R.op("act", lambda e, f=f, bg=bg: e.activation(out=sg[f % 2][:, :], in_=bg[:, :], func=AF.Silu),
                     reads=[bgB], writes=[sgB[f % 2]])
                R.op("dve", lambda e, f=f, bu=bu: e.tensor_tensor(
                    out=aT[:, f, :], in0=bu[:, :], in1=sg[f % 2][:, :], op=ALU.mult),
                    reads=[buB, sgB[f % 2]], writes=[aB[f]])
            for oc in range(8):
                bk, bkB = banks[oc % 6], bankB[oc % 6]
                for f in range(NF):
                    R.op("pe", lambda e, f=f, oc=oc, bk=bk: e.matmul(
                        bk[:, :], wd[:, f, oc * 128:(oc + 1) * 128], aT[:, f, :],
                        start=(f == 0), stop=(f == NF - 1)), reads=[wdB[f], aB[f]], writes=[bkB],
                        signal=(f == NF - 1))
                R.op("dve", lambda e, oc=oc, bk=bk: e.tensor_tensor(
                    out=xb[i][:, oc, :], in0=bk[:, :], in1=xb[i][:, oc, :], op=ALU.add),
                    reads=[bkB, xB[i][oc]], writes=[xB[i][oc]])
            if final:
                norm_to(i, xb[i], lambda kc: xB[i][kc], OFF_GFIN)
            R.dma("pool", f"xstC{i}", lambda e: e.dma_start(out=xo[:, :, tsl], in_=xb[i][:, :, :]),
                  reads=xB[i])

        load(0)
        for tb in range(NTB):
            if tb + 1 < NTB:
                load(tb + 1)
            block(tb)
        R.barrier()
        R.emit()


def build_program(stop_after=None, debug=False):
    nc = bass.Bass("TRN2", target_bir_lowering=False)
    dk = "ExternalOutput" if debug else "Internal"
    G = {}
    xT = nc.dram_tensor("xT", [D, S], F32, kind="ExternalInput").ap()
    G["w_in"] = nc.dram_tensor("w_in", [DEPTH, D, INC], F32, kind="ExternalInput").ap()
    G["w_o"] = nc.dram_tensor("w_o", [DEPTH, D, D], F32, kind="ExternalInput").ap()
    G["w_gate"] = nc.dram_tensor("w_gate", [DEPTH, D, DFF], F32, kind="ExternalInput").ap()
    G["w_up"] = nc.dram_tensor("w_up", [DEPTH, D, DFF], F32, kind="ExternalInput").ap()
    G["w_down"] = nc.dram_tensor("w_down", [DEPTH, DFF, D], F32, kind="ExternalInput").ap()
    G["w_pool"] = nc.dram_tensor("w_pool", [DEPTH, 4, 64, 64], F32, kind="ExternalInput").ap()
    prm_d = nc.dram_tensor("prm", [128, NP], F32, kind="ExternalInput").ap()
    G["tbl"] = nc.dram_tensor("tbl", [128, 4, TW], F32, kind="ExternalInput").ap()
    G["ident"] = nc.dram_tensor("ident", [128, 128], F32, kind="ExternalInput").ap()
    yT = nc.dram_tensor("yT", [D, S], F32, kind="ExternalOutput").ap()
    xs = nc.dram_tensor("xs", [D, S], F32, kind=dk).ap()
    G["qT"] = nc.dram_tensor("qT", [4, 128, S], BF16, kind=dk).ap()
    G["kT"] = nc.dram_tensor("kT", [4, 128, S], BF16, kind=dk).ap()
    G["vS"] = nc.dram_tensor("vS", [S, 512], BF16, kind=dk).ap()
    G["mixT"] = nc.dram_tensor("mixT", [D, S], BF16, kind=dk).ap()

    with ExitStack() as es:
        R = Rec(nc, es)
        prm = es.enter_context(nc.sbuf_tensor("prm_sb", [128, NP], F32))
        ones = es.enter_context(nc.sbuf_tensor("ones_sb", [128, 128], BF16))
        G["prm"], G["ones"], G["onesB"] = prm, ones, Buf()
        prmB = Buf()
        R.dma("sp", "prm", lambda e: e.dma_start(out=prm[:, :], in_=prm_d[:, :]), writes=[prmB])
        R.op("dve", lambda e: e.memset(ones[:, :], 1.0), writes=[G["onesB"]])
        R.barrier()
        done = False
        for l in range(DEPTH):
            xin = xT if l == 0 else xs
            phase_A(R, nc, G, l, xin)
            if stop_after == (l, "A"):
                done = True
                break
            phase_B(R, nc, G, l, xin, xs)
            if stop_after == (l, "B"):
                done = True
                break
            last = (l == DEPTH - 1)
            phase_C(R, nc, G, l, xs, yT if last else xs, last)
            if stop_after == (l, "C"):
                done = True
                break
    return nc


def _rel_bucket_np(d):
    n = np.maximum(d, 0)
    nf = np.maximum(n, 1).astype(np.float32)
    large = 16 + (np.log(nf / np.float32(16)) / np.float32(math.log(128 / 16)) * np.float32(16)).astype(np.int32)
    large = np.minimum(large, 31)
    return np.where(n < 16, n, large)


def host_prep(inputs):
    f32 = np.float32
    g = {k: np.asarray(v, dtype=f32) for k, v in inputs.items()}
    prm = np.zeros((128, NP), f32)
    for l in range(DEPTH):
        pb = l * PL
        prm[:, pb + OFF_GM:pb + OFF_GM + 8] = g["g_mix"][l].reshape(8, 128).T
        prm[:, pb + OFF_GF:pb + OFF_GF + 8] = g["g_ffn"][l].reshape(8, 128).T
        prm[:, pb + OFF_PS:pb + OFF_PS + 2] = g["pool_scale"][l].reshape(2, 128).T
        cw = g["conv_w"][l].reshape(3, 2, 128)
        prm[:, pb + OFF_CW:pb + OFF_CW + 6] = cw.transpose(2, 0, 1).reshape(128, 6)
        prm[:, pb + OFF_SG:pb + OFF_SG + 128] = np.broadcast_to(g["subln_g"][l][None, :], (128, 128))
        lo = pb + OFF_LAM
        prm[:, lo:lo + 64] = g["lambda_q1"][l][None, :]
        prm[:, lo + 64:lo + 128] = g["lambda_k1"][l][None, :]
        prm[:, lo + 128:lo + 192] = g["lambda_q2"][l][None, :]
        prm[:, lo + 192:lo + 256] = g["lambda_k2"][l][None, :]
    prm[:, OFF_GFIN:OFF_GFIN + 8] = g["g_final"].reshape(8, 128).T
    prm[:, OFF_CH:OFF_CH + 4] = g["rel_bias"][31][None, :]
    wins = [2.0, 4.0, 8.0, 16.0]
    t1 = np.arange(1, 17, dtype=f32)
    for c in range(2):
        for half in range(2):
            w = wins[c * 2 + half]
            prm[half * 64:(half + 1) * 64, OFF_ID0 + c * 16:OFF_ID0 + c * 16 + 16] = \
                (1.0 / np.minimum(t1, w)).astype(f32)[None, :]
    prm[:, OFF_EPS] = EPS
    prm[:, OFF_SEPS] = SUBLN_EPS
    kl = np.arange(128)[:, None]
    m = np.arange(TW)[None, :]
    d = m - kl
    bidx = _rel_bucket_np(d)
    tbl = np.empty((128, 4, TW), f32)
    for h in range(4):
        tbl[:, h, :] = np.where(d >= 0, g["rel_bias"][:, h][bidx], f32(MASKV))
    ident = np.eye(128, dtype=f32)
    common = {
        "w_in": g["w_in"], "w_o": g["w_o"], "w_gate": g["w_gate"], "w_up": g["w_up"],
        "w_down": g["w_down"], "w_pool": g["w_pool"], "prm": prm, "tbl": tbl, "ident": ident,
    }
    return g, common


_NC_CACHE = {}


def kernel(**inputs):
    g, common = host_prep(inputs)
    x = g["x"]
    B = x.shape[0]
    if "nc" not in _NC_CACHE:
        _NC_CACHE["nc"] = build_program()
    nc = _NC_CACHE["nc"]
    in_maps = []
    for b in range(B):
        m = dict(common)
        m["xT"] = np.ascontiguousarray(x[b].T)
        in_maps.append(m)
    res = run_bass_kernel_spmd(nc, in_maps, core_ids=list(range(B)))
    out = np.stack([np.ascontiguousarray(res.results[b]["yT"].T) for b in range(B)], axis=0)
    return out.astype(np.float32)
```

```python
import math
from contextlib import ExitStack

import numpy as np
import concourse.bass as bass
import concourse.mybir as mybir
from concourse.bass_utils import run_bass_kernel_spmd

F32 = mybir.dt.float32
BF16 = mybir.dt.bfloat16
AF = mybir.ActivationFunctionType
ALU = mybir.AluOpType

S = 4096
D = 1024
DEPTH = 2
NH = 4
DFF = 2816
NF = DFF // 128
TB = 512
NTB = S // TB
INC = 2560
EPS = 1e-6
SUBLN_EPS = 1e-5
MASKV = -30000.0
TW = 256
BCAST_TBL = True

PL = 8 + 8 + 2 + 6 + 128 + 256
OFF_GM, OFF_GF, OFF_PS, OFF_CW, OFF_SG, OFF_LAM = 0, 8, 16, 18, 24, 152
OFF_GFIN = DEPTH * PL
OFF_CH = OFF_GFIN + 8
OFF_ID0 = OFF_CH + 4
OFF_EPS = OFF_ID0 + 32
OFF_SEPS = OFF_EPS + 1
NP = OFF_SEPS + 1


class Buf:
    __slots__ = ("w", "r")

    def __init__(self):
        self.w = None
        self.r = []


class Rec:
    ENG = ("pe", "act", "dve", "pool", "sp")
    CE = ("pe", "act", "dve", "pool")

    def __init__(self, nc, es):
        self.nc = nc
        self.es = es
        self.sems = {}
        self.count = {}
        self.streams = {e: [] for e in self.ENG}
        self.waited = {e: {} for e in self.ENG}
        for e in self.CE:
            self._sem(e)

    def _sem(self, key):
        if key not in self.sems:
            self.sems[key] = self.es.enter_context(self.nc.semaphore("s_" + key))
            self.count[key] = 0
        return self.sems[key]

    def _collect(self, eng, reads, writes, extra):
        need = {}

        def add(tok, kind):
            if tok is None:
                return
            sem, val = tok
            if sem == eng:
                if eng == "pe":
                    return
                if val > self.count[eng]:
                    return
            if need.get(sem, 0) < val:
                need[sem] = val

        for b in reads:
            add(b.w, "raw")
        for b in writes:
            add(b.w, "waw")
            for t in b.r:
                add(t, "war")
        for t in extra:
            add(t, "raw")
        out = []
        wd = self.waited[eng]
        for sem, val in need.items():
            if wd.get(sem, 0) < val:
                wd[sem] = val
                out.append((sem, val))
        return out

    def op(self, eng, fn, reads=(), writes=(), signal=True, extra=()):
        waits = self._collect(eng, reads, writes, extra)
        if signal:
            self.count[eng] += 1
            tok = (eng, self.count[eng])
            inc = (eng, 1)
        else:
            tok = (eng, self.count[eng] + 1)
            inc = None
        for b in reads:
            b.r.append(tok)
        for b in writes:
            b.w = tok
            b.r = []
        self.streams[eng].append((waits, fn, inc))
        return tok

    def dma(self, q, semkey, fn, reads=(), writes=(), extra=()):
        self._sem(semkey)
        waits = self._collect(q, reads, writes, extra)
        self.count[semkey] += 16
        tok = (semkey, self.count[semkey])
        for b in reads:
            b.r.append(tok)
        for b in writes:
            b.w = tok
            b.r = []
        self.streams[q].append((waits, fn, (semkey, 16)))
        return tok

    def barrier(self):
        toks = [(k, v) for k, v in self.count.items() if v > 0]
        for eng in self.ENG:
            waits = []
            wd = self.waited[eng]
            for sem, val in toks:
                if sem == eng:
                    continue
                if wd.get(sem, 0) < val:
                    wd[sem] = val
                    waits.append((sem, val))
            if waits:
                self.streams[eng].append((waits, None, None))

    def emit(self):
        nc = self.nc
        with nc.Block() as block:
            def mk(name):
                def run(e):
                    for waits, fn, inc in self.streams[name]:
                        for sem, val in waits:
                            e.wait_ge(self.sems[sem], val)
                        if fn is not None:
                            ins = fn(e)
                            if inc is not None:
                                ins.then_inc(self.sems[inc[0]], inc[1])
                return run
            block.tensor(mk("pe"))
            block.scalar(mk("act"))
            block.vector(mk("dve"))
            block.gpsimd(mk("pool"))
            block.sync(mk("sp"))
        for k in self.streams:
            self.streams[k] = []


def emit_norm(R, G, xt, xB, sq, sqB, ssbank, ssB, lnv, lnvB, rstd, rstdB, out, outB, gcol0):
    prm, ones, onesB = G["prm"], G["ones"], G["onesB"]
    R.op("act", lambda e: e.activation(out=sq[:, :, :], in_=xt[:, :, :], func=AF.Square),
         reads=[xB], writes=sqB)
    for kc in range(8):
        R.op("pe", lambda e, kc=kc: e.matmul(ssbank[:, :], ones[:, :], sq[:, kc, :],
                                             start=(kc == 0), stop=(kc == 7)),
             reads=sqB + [onesB], writes=[ssB], signal=(kc == 7))
    R.op("act", lambda e: e.activation(out=lnv[:, :], in_=ssbank[:, :], func=AF.Ln,
                                       bias=prm[:, OFF_EPS:OFF_EPS + 1], scale=1.0 / D),
         reads=[ssB], writes=[lnvB])
    R.op("act", lambda e: e.activation(out=rstd[:, :], in_=lnv[:, :], func=AF.Exp, scale=-0.5),
         reads=[lnvB], writes=[rstdB])
    for kc in range(8):
        R.op("dve", lambda e, kc=kc: e.scalar_tensor_tensor(
            out=out[:, kc, :], in0=xt[:, kc, :], scalar=prm[:, gcol0 + kc:gcol0 + kc + 1],
            in1=rstd[:, :], op0=ALU.mult, op1=ALU.mult),
            reads=[xB, rstdB], writes=[outB])


def x_view(ap2d):
    return ap2d.rearrange("(c p) n -> p c n", p=128)


def phase_A(R, nc, G, l, xsrc):
    prm = G["prm"]
    pb = l * PL
    win_d = G["w_in"][l].rearrange("(kc p) n -> p kc n", p=128)
    with ExitStack() as st:
        def sb(name, shape, dt):
            return st.enter_context(nc.sbuf_tensor(f"{name}_L{l}", shape, dt))

        win = sb("winA", [128, 8, INC], BF16)
        xb = [sb(f"xbA{i}", [128, 8, TB], F32) for i in range(2)]
        sq = sb("sqA", [128, 8, TB], BF16)
        hT = [sb(f"hTA{i}", [128, 8, TB], BF16) for i in range(2)]
        lnv = sb("lnvA", [128, TB], F32)
        rstd = sb("rstdA", [128, TB], F32)
        qst = sb("qstA", [128, 4, TB], BF16)
        kst = sb("kstA", [128, 4, TB], BF16)
        vst = sb("vstA", [128, 4, 512], BF16)
        mixst = sb("mixstA", [128, 4, TB], BF16)
        Pb = sb("PbA", [128, 2, 528], F32)
        Ub = sb("UbA", [128, 2, 528], F32)
        gcS = sb("gcSA", [128, 2, TB], F32)
        s2 = sb("s2A", [128, 2, 528], F32)
        s4 = sb("s4A", [128, 2, 528], F32)
        s8 = sb("s8A", [128, 528], F32)
        s16 = sb("s16A", [128, 528], F32)
        tmpf = sb("tmpfA", [128, 16], F32)
        pooled = sb("pooledA", [128, 2, TB], BF16)
        yv = sb("yvA", [128, 2, TB], F32)
        wblk = sb("wblkA", [128, 2, 128], BF16)
        banks = [st.enter_context(nc.psum_tensor(f"L{l}bkA{i}", [128, 512], F32)) for i in range(8)]
        bankB = [Buf() for _ in range(8)]

        winB = [Buf() for _ in range(5)]
        xB = [Buf(), Buf()]
        sqB = [Buf()]
        hB = [Buf(), Buf()]
        lnvB, rstdB, qstB, kstB, vstB, mixB = Buf(), Buf(), Buf(), Buf(), Buf(), Buf()
        PbB, UbB, gcB, s2B, s4B, s8B, s16B, tmpB, poolB, yvB, wblkB = (Buf() for _ in range(11))

        for cg in (0, 1, 3, 4, 2):
            R.dma("pool", f"win{cg}", lambda e, cg=cg: e.dma_start(
                out=win[:, :, cg * 512:(cg + 1) * 512], in_=win_d[:, :, cg * 512:(cg + 1) * 512]),
                writes=[winB[cg]])
        R.op("dve", lambda e: e.memset(wblk[:, :, :], 0.0), writes=[wblkB])
        for g in range(4):
            r0 = (g % 2) * 64
            R.dma("pool", "wblk", lambda e, g=g, r0=r0: e.dma_start(
                out=wblk[r0:r0 + 64, g // 2, r0:r0 + 64], in_=G["w_pool"][l, g, :, :]),
                writes=[wblkB])
        R.op("dve", lambda e: e.memset(Pb[:, :, 0:16], 0.0), writes=[PbB])
        R.op("dve", lambda e: e.memset(Ub[:, :, 0:16], 0.0), writes=[UbB])

        xv = x_view(xsrc)

        def load(tb):
            i = tb % 2
            R.dma("sp", f"xA{i}", lambda e: e.dma_start(out=xb[i][:, :, :], in_=xv[:, :, tb * TB:(tb + 1) * TB]),
                  writes=[xB[i]])

        def norm(tb):
            i = tb % 2
            emit_norm(R, G, xb[i], xB[i], sq, sqB, banks[7], bankB[7], lnv, lnvB, rstd, rstdB,
                      hT[i], hB[i], pb + OFF_GM)

        rr = [0]

        def nb():
            i = rr[0] % 5
            rr[0] += 1
            return banks[i], bankB[i]

        def mm_chain(bank, bB, h, hb, col, tsub=None):
            for kc in range(8):
                if tsub is None:
                    fn = lambda e, kc=kc: e.matmul(bank[:, :], win[:, kc, col:col + 128], h[:, kc, :],
                                                   start=(kc == 0), stop=(kc == 7))
                else:
                    fn = lambda e, kc=kc: e.matmul(bank[:, :], h[:, kc, tsub * 128:(tsub + 1) * 128],
                                                   win[:, kc, 1024:1536], start=(kc == 0), stop=(kc == 7))
                R.op("pe", fn, reads=[winB[2 if tsub is not None else col // 512], hb], writes=[bB],
                     signal=(kc == 7))

        load(0)
        load(1)
        norm(0)
        for tb in range(NTB):
            i = tb % 2
            h, hb = hT[i], hB[i]
            tsl = slice(tb * TB, (tb + 1) * TB)
            for hh in range(4):
                bank, bB = nb()
                mm_chain(bank, bB, h, hb, hh * 128)
                R.op("act", lambda e, bank=bank, hh=hh: e.mul(out=qst[:, hh, :], in_=bank[:, :], mul=0.125),
                     reads=[bB], writes=[qstB])
            for hh in range(4):
                bank, bB = nb()
                mm_chain(bank, bB, h, hb, 512 + hh * 128)
                R.op("act", lambda e, bank=bank, hh=hh: e.copy(out=kst[:, hh, :], in_=bank[:, :]),
                     reads=[bB], writes=[kstB])
            R.dma("pool", "qstore", lambda e, tsl=tsl: e.dma_start(
                out=G["qT"].rearrange("h p n -> p h n")[:, :, tsl], in_=qst[:, :, :]), reads=[qstB])
            R.dma("pool", "kstore", lambda e, tsl=tsl: e.dma_start(
                out=G["kT"].rearrange("h p n -> p h n")[:, :, tsl], in_=kst[:, :, :]), reads=[kstB])
            if tb + 1 < NTB:
                norm(tb + 1)
            for c in range(2):
                bank, bB = nb()
                mm_chain(bank, bB, h, hb, 1536 + c * 128)
                R.op("act", lambda e, bank=bank, c=c: e.copy(out=Pb[:, c, 16:528], in_=bank[:, :]),
                     reads=[bB], writes=[PbB])
            for c in range(2):
                bank, bB = nb()
                mm_chain(bank, bB, h, hb, 2048 + c * 128)
                R.op("act", lambda e, bank=bank, c=c: e.copy(out=gcS[:, c, :], in_=bank[:, :]),
                     reads=[bB], writes=[gcB])
            for c in range(2):
                bank, bB = nb()
                mm_chain(bank, bB, h, hb, 2304 + c * 128)
                R.op("dve", lambda e, bank=bank, c=c: e.tensor_tensor(
                    out=Ub[:, c, 16:528], in0=bank[:, :], in1=gcS[:, c, :], op=ALU.mult),
                    reads=[bB, gcB], writes=[UbB])
            gbb = []
            for c in range(2):
                bank, bB = banks[5 + c], bankB[5 + c]
                mm_chain(bank, bB, h, hb, 1792 + c * 128)
                gbb.append((bank, bB))
            for ts in range(4):
                bank, bB = nb()
                mm_chain(bank, bB, h, hb, 0, tsub=ts)
                R.op("act", lambda e, bank=bank, ts=ts: e.copy(out=vst[:, ts, :], in_=bank[:, :]),
                     reads=[bB], writes=[vstB])
            R.dma("pool", "vstore", lambda e, tb=tb: e.dma_start(
                out=G["vS"].rearrange("(t p) f -> p t f", p=128)[:, tb * 4:(tb + 1) * 4, :], in_=vst[:, :, :]),
                reads=[vstB])
            for c in range(2):
                cw = lambda k, c=c: prm[:, pb + OFF_CW + k * 2 + c:pb + OFF_CW + k * 2 + c + 1]
                R.op("dve", lambda e, c=c, cw=cw: e.tensor_scalar(
                    out=yv[:, c, :], in0=Ub[:, c, 14:526], scalar1=cw(0), scalar2=None, op0=ALU.mult),
                    reads=[UbB], writes=[yvB])
                R.op("dve", lambda e, c=c, cw=cw: e.scalar_tensor_tensor(
                    out=yv[:, c, :], in0=Ub[:, c, 15:527], scalar=cw(1), in1=yv[:, c, :],
                    op0=ALU.mult, op1=ALU.add), reads=[UbB, yvB], writes=[yvB])
                R.op("dve", lambda e, c=c, cw=cw: e.scalar_tensor_tensor(
                    out=yv[:, c, :], in0=Ub[:, c, 16:528], scalar=cw(2), in1=yv[:, c, :],
                    op0=ALU.mult, op1=ALU.add), reads=[UbB, yvB], writes=[yvB])
                bank, bB = gbb[c]
                R.op("dve", lambda e, c=c, bank=bank: e.tensor_tensor(
                    out=mixst[:, 2 + c, :], in0=bank[:, :], in1=yv[:, c, :], op=ALU.mult),
                    reads=[bB, yvB], writes=[mixB])
            R.op("dve", lambda e: e.tensor_copy(out=Ub[:, :, 14:16], in_=Ub[:, :, 526:528]),
                 reads=[UbB], writes=[UbB])
            R.op("dve", lambda e: e.tensor_tensor(out=s2[:, :, 1:528], in0=Pb[:, :, 1:528], in1=Pb[:, :, 0:527],
                                                  op=ALU.add), reads=[PbB], writes=[s2B])
            R.op("dve", lambda e: e.tensor_tensor(out=s4[:, :, 3:528], in0=s2[:, :, 3:528], in1=s2[:, :, 1:526],
                                                  op=ALU.add), reads=[s2B], writes=[s4B])
            R.op("dve", lambda e: e.tensor_tensor(out=s8[:, 7:528], in0=s4[:, 1, 7:528], in1=s4[:, 1, 3:524],
                                                  op=ALU.add), reads=[s4B], writes=[s8B])
            R.op("dve", lambda e: e.tensor_tensor(out=s16[64:128, 15:528], in0=s8[64:128, 15:528],
                                                  in1=s8[64:128, 7:520], op=ALU.add), reads=[s8B], writes=[s16B])
            grp = [(s2, lambda r: s2[r, 0, 16:528], 0, 0, 2.0, s2B),
                   (s4, lambda r: s4[r, 0, 16:528], 64, 0, 4.0, s4B),
                   (s8, lambda r: s8[r, 16:528], 0, 1, 8.0, s8B),
                   (s16, lambda r: s16[r, 16:528], 64, 1, 16.0, s16B)]
            for (_, src, r0, c, w, sB) in grp:
                rs = slice(r0, r0 + 64)
                R.op("dve", lambda e, src=src, rs=rs, c=c, w=w: e.scalar_tensor_tensor(
                    out=pooled[rs, c, :], in0=src(rs), scalar=1.0 / w, in1=Pb[rs, c, 16:528],
                    op0=ALU.mult, op1=ALU.subtract), reads=[sB, PbB], writes=[poolB])
            if tb == 0:
                for (_, src, r0, c, w, sB) in grp:
                    rs = slice(r0, r0 + 64)
                    R.op("dve", lambda e, src=src, rs=rs, c=c: e.tensor_tensor(
                        out=tmpf[rs, :], in0=src(rs)[:, 0:16],
                        in1=prm[rs, OFF_ID0 + c * 16:OFF_ID0 + c * 16 + 16], op=ALU.mult),
                        reads=[sB], writes=[tmpB])
                    R.op("dve", lambda e, rs=rs, c=c: e.tensor_tensor(
                        out=pooled[rs, c, 0:16], in0=tmpf[rs, :], in1=Pb[rs, c, 16:32], op=ALU.subtract),
                        reads=[tmpB, PbB], writes=[poolB])
            R.op("dve", lambda e: e.tensor_copy(out=Pb[:, :, 0:16], in_=Pb[:, :, 512:528]),
                 reads=[PbB], writes=[PbB])
            for c in range(2):
                bank, bB = nb()
                R.op("pe", lambda e, bank=bank, c=c: e.matmul(bank[:, :], wblk[:, c, :], pooled[:, c, :],
                                                              start=True, stop=True),
                     reads=[wblkB, poolB], writes=[bB])
                R.op("dve", lambda e, bank=bank, c=c: e.tensor_scalar(
                    out=mixst[:, c, :], in0=bank[:, :], scalar1=prm[:, pb + OFF_PS + c:pb + OFF_PS + c + 1],
                    scalar2=None, op0=ALU.mult), reads=[bB], writes=[mixB])
            R.dma("pool", "mixstore", lambda e, tsl=tsl: e.dma_start(
                out=x_view(G["mixT"])[:, 4:8, tsl], in_=mixst[:, :, :]), reads=[mixB])
            if tb + 2 < NTB:
                load(tb + 2)
        R.barrier()
        R.emit()


def phase_B(R, nc, G, l, xsrc, xdst):
    prm = G["prm"]
    pb = l * PL
    lam_init = 0.8 - 0.6 * math.exp(-0.3 * l)
    wo_d = G["w_o"][l].rearrange("(kc p) n -> p kc n", p=128)
    with ExitStack() as st:
        def sb(name, shape, dt):
            return st.enter_context(nc.sbuf_tensor(f"{name}_L{l}", shape, dt))

        KT = sb("KTB", [128, 4, S], BF16)
        V = sb("VB", [128, 32, 4, 129], BF16)
        wo = sb("woB", [128, 8, D], BF16)
        tbl = sb("tblB", [128, 4, TW], F32)
        tblh = sb("tblhB", [128, 4, TW], BF16)
        tbll = sb("tbllB", [128, 4, TW], BF16)
        tblhB, tbllB = Buf(), Buf()
        Qb = [sb(f"QbB{i}", [128, 4, TB], BF16) for i in range(2)]
        PT = [sb(f"PTB{i}", [128, 2, TB], BF16) for i in range(2)]
        mixb = [sb(f"mixbB{i}", [128, 8, TB], BF16) for i in range(2)]
        xb = [sb(f"xbB{i}", [128, 8, TB], F32) for i in range(2)]
        accS = sb("accSB", [128, 8, 129], F32)
        rl = sb("rlB", [128, 2, 4], F32)
        r2n = sb("r2nB", [128, 4], F32)
        tt = sb("ttB", [128, 128], F32)
        attf = sb("attfB", [128, 4, 128], F32)
        junk = sb("junkB", [128, 128], F32)
        ss = sb("ssB", [128, 4], F32)
        vv = sb("vvB", [128, 4], F32)
        nhalf = sb("nhalfB", [128, 4], F32)
        rstd = sb("rstdB", [128, 4], F32)
        attn = sb("attnB", [128, 4, 128], BF16)
        ident = sb("identB", [128, 128], BF16)
        Gp = sb("GpB", [128, 128], F32)
        lamt = sb("lamtB", [128, 8], F32)
        lprod = sb("lprodB", [128, 64], F32)
        Sp = [st.enter_context(nc.psum_tensor(f"L{l}SpB{i}", [128, 2, 512], F32)) for i in range(2)]
        accb = [st.enter_context(nc.psum_tensor(f"L{l}accB{i}", [128, 3, 129], F32)) for i in range(3)]
        trb = st.enter_context(nc.psum_tensor(f"L{l}trB", [128, 512], BF16))

        SB_ = [Buf(), Buf()]
        WOB = [[Buf(), Buf()], [Buf(), Buf()]]
        PTB = [Buf(), Buf()]
        accB = [Buf() for _ in range(8)]
        trB = Buf()
        KTB = [Buf() for _ in range(NTB)]
        VBf = [Buf() for _ in range(NTB)]
        QB = [Buf(), Buf()]
        mixatt = [[Buf() for _ in range(4)] for _ in range(2)]
        mixpc = [Buf(), Buf()]
        xB = [[Buf() for _ in range(8)] for _ in range(2)]
        woB, tblB, identB, GpB, lamB, lprodB, nhB = (Buf() for _ in range(7))
        accSB, rlB, r2nB, ttB, attfB, junkB, ssB_, vvB, rstdB, attnB = (Buf() for _ in range(10))

        def acc(idx):
            return accb[idx // 3][:, idx % 3, :]

        R.dma("sp", "tblB", lambda e: e.dma_start(out=tbl[:, :, :], in_=G["tbl"][:, :, :]), writes=[tblB])
        for hh in range(NH):
            R.op("dve", lambda e, hh=hh: e.tensor_scalar(
                out=tbl[:, hh, :], in0=tbl[:, hh, :], scalar1=prm[:, OFF_CH + hh:OFF_CH + hh + 1],
                scalar2=None, op0=ALU.subtract), reads=[tblB], writes=[tblB])
        R.op("dve", lambda e: e.tensor_copy(out=tblh[:, :, :], in_=tbl[:, :, :]), reads=[tblB], writes=[tblhB])
        R.op("dve", lambda e: e.tensor_tensor(out=tbl[:, :, :], in0=tbl[:, :, :], in1=tblh[:, :, :],
                                              op=ALU.subtract), reads=[tblB, tblhB], writes=[tblB])
        R.op("dve", lambda e: e.tensor_copy(out=tbll[:, :, :], in_=tbl[:, :, :]), reads=[tblB], writes=[tbllB])
        R.op("dve", lambda e: e.memset(V[:, :, :, 128:129], 1.0), writes=VBf)
        R.op("dve", lambda e: e.memset(nhalf[:, :], -0.5), writes=[nhB])
        kT_v = G["kT"].rearrange("h p n -> p h n")
        vS_v = G["vS"].rearrange("(t p) (h d) -> p t h d", p=128, h=4)
        for tb in range(NTB):
            R.dma("sp", f"kld{tb}", lambda e, tb=tb: e.dma_start(
                out=KT[:, :, tb * TB:(tb + 1) * TB], in_=kT_v[:, :, tb * TB:(tb + 1) * TB]), writes=[KTB[tb]])
            for t4 in range(4):
                R.dma("sp", f"vld{tb}", lambda e, t=tb * 4 + t4: e.dma_start(
                    out=V[:, t, :, 0:128], in_=vS_v[:, t, :, :]), writes=[VBf[tb]])
        R.dma("pool", "identB", lambda e: e.dma_start(out=ident[:, :], in_=G["ident"][:, :]), writes=[identB])
        R.dma("pool", "woB", lambda e: e.dma_start(out=wo[:, :, :], in_=wo_d[:, :, :]), writes=[woB])
        lo = pb + OFF_LAM
        for j in range(2):
            R.op("dve", lambda e, j=j: e.tensor_tensor(
                out=lprod[:, :], in0=prm[:, lo + j * 128:lo + j * 128 + 64],
                in1=prm[:, lo + j * 128 + 64:lo + j * 128 + 128], op=ALU.mult), writes=[lprodB])
            R.op("dve", lambda e, j=j: e.reduce_sum(out=lamt[:, j:j + 1], in_=lprod[:, :],
                                                   axis=mybir.AxisListType.X),
                 reads=[lprodB], writes=[lamB])
        R.op("act", lambda e: e.activation(out=lamt[:, 2:4], in_=lamt[:, 0:2], func=AF.Exp),
             reads=[lamB], writes=[lamB])
        R.op("dve", lambda e: e.tensor_tensor(out=lamt[:, 4:5], in0=lamt[:, 2:3], in1=lamt[:, 3:4],
                                              op=ALU.subtract), reads=[lamB], writes=[lamB])
        R.op("dve", lambda e: e.tensor_scalar(out=lamt[:, 5:6], in0=lamt[:, 4:5], scalar1=lam_init,
                                              scalar2=-1.0, op0=ALU.add, op1=ALU.mult),
             reads=[lamB], writes=[lamB])
        R.op("dve", lambda e: e.tensor_scalar(out=Gp[:, :], in0=prm[:, pb + OFF_SG:pb + OFF_SG + 128],
                                              scalar1=1.0 - lam_init, scalar2=None, op0=ALU.mult),
             writes=[GpB])
        nlam = lamt[:, 5:6]

        xv = x_view(xsrc)
        xo = x_view(xdst)
        qT_v = G["qT"].rearrange("h p n -> p h n")
        mix_v = x_view(G["mixT"])

        def loads(qb):
            i = qb % 2
            tsl = slice(qb * TB, (qb + 1) * TB)
            R.dma("sp", f"qld{i}", lambda e: e.dma_start(out=Qb[i][:, :, :], in_=qT_v[:, :, tsl]), writes=[QB[i]])
            R.dma("sp", f"mld{i}", lambda e: e.dma_start(out=mixb[i][:, 4:8, :], in_=mix_v[:, 4:8, tsl]),
                  writes=[mixpc[i]])
            R.dma("sp", f"xld{i}", lambda e: e.dma_start(out=xb[i][:, :, :], in_=xv[:, :, tsl]), writes=xB[i])

        pending = []

        def flush_pending():
            while pending:
                pending.pop(0)()

        def attn_head(qb, h):
            i = qb % 2
            nk = 4 * (qb + 1)

            def qk(kt):
                j = kt - 4 * qb
                qlo = max(0, 128 * j)
                p = kt % 2
                near = (j >= -1)
                for m in range(2):
                    R.op("pe", lambda e, m=m, p=p, kt=kt, qlo=qlo, near=near: e.matmul(
                        Sp[p][:, m, qlo:512], KT[m * 64:(m + 1) * 64, h, kt * 128:(kt + 1) * 128],
                        Qb[i][m * 64:(m + 1) * 64, h, qlo:512], start=True, stop=(not near)),
                        reads=[KTB[kt // 4], QB[i]], writes=[SB_[p], WOB[p][0], WOB[p][1]],
                        signal=(m == 1 and not near))
                if near:
                    mlo = 128 if j < 0 else 0
                    mhi = min(256, mlo + 512 - qlo)
                    c0, c1 = mlo + 128 * j, mhi + 128 * j
                    for m in range(2):
                        for part, tb_ in enumerate((tblh, tbll)):
                            R.op("pe", lambda e, m=m, p=p, c0=c0, c1=c1, mlo=mlo, mhi=mhi, tb_=tb_, part=part: e.matmul(
                                Sp[p][:, m, c0:c1], ident[:, :], tb_[:, h, mlo:mhi], start=False,
                                stop=(part == 1)),
                                reads=[identB, tblhB, tbllB], writes=[SB_[p]], signal=(m == 1 and part == 1))

            def soft(kt):
                j = kt - 4 * qb
                qlo = max(0, 128 * j)
                p = kt % 2
                R.op("act", lambda e, p=p, qlo=qlo: e.activation(
                    out=PT[p][:, :, qlo:512], in_=Sp[p][:, :, qlo:512], func=AF.Exp),
                    reads=[SB_[p]], writes=[PTB[p]])

            def pv(kt):
                j = kt - 4 * qb
                p = kt % 2
                s0 = max(j, 0)
                seen = set()
                for m in range(2):
                    for sub in range(s0, 4):
                        idx = m * 4 + sub
                        last = (kt == 4 * qb + sub)
                        bnk = idx // 3
                        st_ = (kt == 0 and bnk not in seen)
                        seen.add(bnk)
                        R.op("pe", lambda e, m=m, p=p, sub=sub, kt=kt, last=last, st_=st_, idx=idx: e.matmul(
                            acc(idx), PT[p][:, m, sub * 128:(sub + 1) * 128], V[:, kt, h, :],
                            start=st_, stop=last, skip_group_check=True),
                            reads=[PTB[p], VBf[kt // 4]], writes=[accB[idx]],
                            signal=(last or (m == 1 and sub == 3)))

            defer_at = min(nk - 1, 7)
            qk(0)
            for kt in range(nk):
                if kt + 1 < nk:
                    qk(kt + 1)
                soft(kt)
                pv(kt)
                if kt == defer_at:
                    flush_pending()
            for bnk in range(3):
                n = 3 if bnk < 2 else 2
                R.op("dve", lambda e, bnk=bnk, n=n: e.tensor_copy(
                    out=accS[:, 3 * bnk:3 * bnk + n, :], in_=accb[bnk][:, 0:n, :]),
                    reads=accB[3 * bnk:3 * bnk + n], writes=[accSB])
            for m in range(2):
                R.op("dve", lambda e, m=m: e.reciprocal(out=rl[:, m, :], in_=accS[:, 4 * m:4 * m + 4, 128]),
                     reads=[accSB], writes=[rlB])
            R.op("dve", lambda e: e.tensor_scalar(out=r2n[:, :], in0=rl[:, 1, :], scalar1=nlam,
                                                  scalar2=None, op0=ALU.mult),
                 reads=[rlB, lamB], writes=[r2nB])
            for sub in range(4):
                R.op("dve", lambda e, sub=sub: e.tensor_scalar(
                    out=tt[:, :], in0=accS[:, sub, 0:128], scalar1=rl[:, 0, sub:sub + 1], scalar2=None,
                    op0=ALU.mult), reads=[accSB, rlB], writes=[ttB])
                R.op("dve", lambda e, sub=sub: e.scalar_tensor_tensor(
                    out=attf[:, sub, :], in0=accS[:, 4 + sub, 0:128], scalar=r2n[:, sub:sub + 1],
                    in1=tt[:, :], op0=ALU.mult, op1=ALU.add),
                    reads=[accSB, r2nB, ttB], writes=[attfB])
                R.op("dve", lambda e, sub=sub: e.scalar_tensor_tensor(
                    out=junk[:, :], in0=attf[:, sub, :], scalar=1.0, in1=attf[:, sub, :],
                    op0=ALU.mult, op1=ALU.mult, accum_out=ss[:, sub:sub + 1]),
                    reads=[attfB], writes=[junkB, ssB_])
            R.op("dve", lambda e: e.tensor_scalar(out=vv[:, :], in0=ss[:, :], scalar1=1.0 / 128,
                                                  scalar2=SUBLN_EPS, op0=ALU.mult, op1=ALU.add),
                 reads=[ssB_], writes=[vvB])
            R.op("pool", lambda e: e.tensor_tensor(out=rstd[:, :], in0=vv[:, :], in1=nhalf[:, :], op=ALU.pow),
                 reads=[vvB, nhB], writes=[rstdB])
            for sub in range(4):
                R.op("dve", lambda e, sub=sub: e.scalar_tensor_tensor(
                    out=attn[:, sub, :], in0=attf[:, sub, :], scalar=rstd[:, sub:sub + 1], in1=Gp[:, :],
                    op0=ALU.mult, op1=ALU.mult), reads=[attfB, rstdB, GpB], writes=[attnB])

            def stage2():
                for sub in range(4):
                    R.op("pe", lambda e, sub=sub: e.transpose(trb[:, sub * 128:(sub + 1) * 128], attn[:, sub, :],
                                                              ident[:, :]),
                         reads=[attnB, identB], writes=[trB], signal=(sub == 3))
                R.op("dve", lambda e: e.tensor_copy(out=mixb[i][:, h, :], in_=trb[:, :]),
                     reads=[trB], writes=[mixatt[i][h]])
            pending.append(stage2)

        def wo_block(qb):
            i = qb % 2
            tsl = slice(qb * TB, (qb + 1) * TB)
            for half in range(2):
                ocs = range(4 * half, 4 * half + 4)
                for oc in ocs:
                    p, m = (oc // 2) % 2, oc % 2
                    for kc in (4, 5, 6, 7, 0, 1, 2):
                        rd = [woB, mixatt[i][kc]] if kc < 4 else [woB, mixpc[i]]
                        R.op("pe", lambda e, p=p, m=m, kc=kc, oc=oc: e.matmul(
                            Sp[p][:, m, :], wo[:, kc, oc * 128:(oc + 1) * 128], mixb[i][:, kc, :],
                            start=(kc == 4), stop=False), reads=rd, writes=[WOB[p][m], SB_[p]], signal=False)
                for oc in ocs:
                    p, m = (oc // 2) % 2, oc % 2
                    R.op("pe", lambda e, p=p, m=m, oc=oc: e.matmul(
                        Sp[p][:, m, :], wo[:, 3, oc * 128:(oc + 1) * 128], mixb[i][:, 3, :],
                        start=False, stop=True), reads=[woB, mixatt[i][3]], writes=[WOB[p][m], SB_[p]])
                    R.op("dve", lambda e, p=p, m=m, oc=oc: e.tensor_tensor(
                        out=xb[i][:, oc, :], in0=Sp[p][:, m, :], in1=xb[i][:, oc, :], op=ALU.add),
                        reads=[WOB[p][m], xB[i][oc]], writes=[xB[i][oc]])
            R.dma("pool", f"xst{i}", lambda e: e.dma_start(out=xo[:, :, tsl], in_=xb[i][:, :, :]),
                  reads=xB[i])

        loads(0)
        for qb in range(NTB):
            if qb + 1 < NTB:
                loads(qb + 1)
            for h in range(NH):
                attn_head(qb, h)
            flush_pending()
            wo_block(qb)
        R.barrier()
        R.emit()


def phase_C(R, nc, G, l, xsrc, xdst, final):
    prm = G["prm"]
    pb = l * PL
    wg_d = G["w_gate"][l].rearrange("(kc p) n -> p kc n", p=128)
    wu_d = G["w_up"][l].rearrange("(kc p) n -> p kc n", p=128)
    wd_d = G["w_down"][l].rearrange("(f p) n -> p f n", p=128)
    with ExitStack() as st:
        def sb(name, shape, dt):
            return st.enter_context(nc.sbuf_tensor(f"{name}_L{l}", shape, dt))

        wg = sb("wgC", [128, 8, DFF], BF16)
        wu = sb("wuC", [128, 8, DFF], BF16)
        wd = sb("wdC", [128, NF, D], BF16)
        xb = [sb(f"xbC{i}", [128, 8, TB], F32) for i in range(2)]
        hT = sb("hTC", [128, 8, TB], BF16)
        aT = sb("aTC", [128, NF, TB], BF16)
        rstd = sb("rstdC", [128, TB], F32)
        sg = sb("sgC", [128, TB], F32)
        sqh = sb("sqhC", [128, 4, TB], BF16)
        banks = [st.enter_context(nc.psum_tensor(f"L{l}bkC{i}", [128, 512], F32)) for i in range(8)]
        bankB = [Buf() for _ in range(8)]
        wgB = [Buf() for _ in range(NF // 2)]
        wuB = [Buf() for _ in range(NF // 2)]
        wdB = [Buf() for _ in range(NF)]
        xB = [[Buf() for _ in range(8)] for _ in range(2)]
        hB, rstdB, sgB, sqhB = Buf(), Buf(), Buf(), Buf()
        aB = [Buf() for _ in range(NF)]

        for g2 in range(NF // 2):
            csl = slice(g2 * 256, (g2 + 1) * 256)
            R.dma("pool", f"wg{g2}", lambda e, csl=csl: e.dma_start(out=wg[:, :, csl], in_=wg_d[:, :, csl]),
                  writes=[wgB[g2]])
            R.dma("pool", f"wu{g2}", lambda e, csl=csl: e.dma_start(out=wu[:, :, csl], in_=wu_d[:, :, csl]),
                  writes=[wuB[g2]])
        for g2 in range(NF // 2):
            R.dma("pool", f"wd{g2}", lambda e, g2=g2: e.dma_start(out=wd[:, 2 * g2:2 * g2 + 2, :],
                                                                  in_=wd_d[:, 2 * g2:2 * g2 + 2, :]),
                  writes=[wdB[2 * g2], wdB[2 * g2 + 1]])

        xv = x_view(xsrc)
        xo = x_view(xdst)
        prm_g = pb + OFF_GF
        ones, onesB = G["ones"], G["onesB"]

        def load(tb):
            i = tb % 2
            R.dma("sp", f"xC{i}", lambda e: e.dma_start(out=xb[i][:, :, :], in_=xv[:, :, tb * TB:(tb + 1) * TB]),
                  writes=xB[i])

        def norm_rstd(i):
            for half in range(2):
                R.op("act", lambda e, half=half: e.activation(
                    out=sqh[:, :, :], in_=xb[i][:, 4 * half:4 * half + 4, :], func=AF.Square),
                    reads=xB[i][4 * half:4 * half + 4], writes=[sqhB])
                for k4 in range(4):
                    kc = 4 * half + k4
                    R.op("pe", lambda e, k4=k4, kc=kc: e.matmul(banks[7][:, :], ones[:, :], sqh[:, k4, :],
                                                                start=(kc == 0), stop=(kc == 7)),
                         reads=[sqhB, onesB], writes=[bankB[7]], signal=(k4 == 3))
            R.op("act", lambda e: e.activation(out=rstd[:, :], in_=banks[7][:, :], func=AF.Ln,
                                               bias=prm[:, OFF_EPS:OFF_EPS + 1], scale=1.0 / D),
                 reads=[bankB[7]], writes=[rstdB])
            R.op("act", lambda e: e.activation(out=rstd[:, :], in_=rstd[:, :], func=AF.Exp, scale=-0.5),
                 reads=[rstdB], writes=[rstdB])

        def norm_apply(i, dst, dstB_of, gcol0):
            for kc in range(8):
                R.op("dve", lambda e, kc=kc: e.scalar_tensor_tensor(
                    out=dst[:, kc, :], in0=xb[i][:, kc, :], scalar=prm[:, gcol0 + kc:gcol0 + kc + 1],
                    in1=rstd[:, :], op0=ALU.mult, op1=ALU.mult),
                    reads=[xB[i][kc], rstdB], writes=[dstB_of(kc)])

        pending = []

        def finish(tb):
            i = tb % 2
            tsl = slice(tb * TB, (tb + 1) * TB)
            if final:
                norm_rstd(i)
                norm_apply(i, xb[i], lambda kc: xB[i][kc], OFF_GFIN)
            R.dma("pool", f"xstC{i}", lambda e: e.dma_start(out=xo[:, :, tsl], in_=xb[i][:, :, :]),
                  reads=xB[i])

        def block(tb):
            i = tb % 2
            for f in range(NF):
                if f == 1:
                    while pending:
                        pending.pop(0)()
                    if tb + 1 < NTB:
                        load(tb + 1)
                if f == 5 and tb + 1 < NTB:
                    norm_rstd((tb + 1) % 2)
                bg, bgB = banks[(2 * f) % 6], bankB[(2 * f) % 6]
                bu, buB = banks[(2 * f + 1) % 6], bankB[(2 * f + 1) % 6]
                for kc in range(8):
                    R.op("pe", lambda e, kc=kc, f=f, bg=bg: e.matmul(
                        bg[:, :], wg[:, kc, f * 128:(f + 1) * 128], hT[:, kc, :],
                        start=(kc == 0), stop=(kc == 7)), reads=[wgB[f // 2], hB], writes=[bgB], signal=(kc == 7))
                for kc in range(8):
                    R.op("pe", lambda e, kc=kc, f=f, bu=bu: e.matmul(
                        bu[:, :], wu[:, kc, f * 128:(f + 1) * 128], hT[:, kc, :],
                        start=(kc == 0), stop=(kc == 7)), reads=[wuB[f // 2], hB], writes=[buB], signal=(kc == 7))
                R.op("act", lambda e, f=f, bg=bg: e.activation(out=sg[:, :], in_=bg[:, :], func=AF.Silu),
                     reads=[bgB], writes=[sgB])
                R.op("dve", lambda e, f=f, bu=bu: e.tensor_tensor(
                    out=aT[:, f, :], in0=bu[:, :], in1=sg[:, :], op=ALU.mult),
                    reads=[buB, sgB], writes=[aB[f]])
            if tb + 1 < NTB:
                norm_apply((tb + 1) % 2, hT, lambda kc: hB, prm_g)
            for oc in range(8):
                bk, bkB = banks[oc % 6], bankB[oc % 6]
                for f in range(NF):
                    R.op("pe", lambda e, f=f, oc=oc, bk=bk: e.matmul(
                        bk[:, :], wd[:, f, oc * 128:(oc + 1) * 128], aT[:, f, :],
                        start=(f == 0), stop=(f == NF - 1)), reads=[wdB[f], aB[f]], writes=[bkB],
                        signal=(f == NF - 1))
                R.op("dve", lambda e, oc=oc, bk=bk: e.tensor_tensor(
                    out=xb[i][:, oc, :], in0=bk[:, :], in1=xb[i][:, oc, :], op=ALU.add),
                    reads=[bkB, xB[i][oc]], writes=[xB[i][oc]])
            pending.append(lambda: finish(tb))

        load(0)
        norm_rstd(0)
        norm_apply(0, hT, lambda kc: hB, prm_g)
        for tb in range(NTB):
            block(tb)
        while pending:
            pending.pop(0)()
        R.barrier()
        R.emit()


def build_program(stop_after=None, debug=False):
    nc = bass.Bass("TRN2", target_bir_lowering=False)
    dk = "ExternalOutput" if debug else "Internal"
    G = {}
    xT = nc.dram_tensor("xT", [D, S], F32, kind="ExternalInput").ap()
    G["w_in"] = nc.dram_tensor("w_in", [DEPTH, D, INC], F32, kind="ExternalInput").ap()
    G["w_o"] = nc.dram_tensor("w_o", [DEPTH, D, D], F32, kind="ExternalInput").ap()
    G["w_gate"] = nc.dram_tensor("w_gate", [DEPTH, D, DFF], F32, kind="ExternalInput").ap()
    G["w_up"] = nc.dram_tensor("w_up", [DEPTH, D, DFF], F32, kind="ExternalInput").ap()
    G["w_down"] = nc.dram_tensor("w_down", [DEPTH, DFF, D], F32, kind="ExternalInput").ap()
    G["w_pool"] = nc.dram_tensor("w_pool", [DEPTH, 4, 64, 64], F32, kind="ExternalInput").ap()
    prm_d = nc.dram_tensor("prm", [128, NP], F32, kind="ExternalInput").ap()
    G["tbl"] = nc.dram_tensor("tbl", [128, 4, TW], F32, kind="ExternalInput").ap()
    G["ident"] = nc.dram_tensor("ident", [128, 128], F32, kind="ExternalInput").ap()
    yT = nc.dram_tensor("yT", [D, S], F32, kind="ExternalOutput").ap()
    xs = nc.dram_tensor("xs", [D, S], F32, kind=dk).ap()
    G["qT"] = nc.dram_tensor("qT", [4, 128, S], BF16, kind=dk).ap()
    G["kT"] = nc.dram_tensor("kT", [4, 128, S], BF16, kind=dk).ap()
    G["vS"] = nc.dram_tensor("vS", [S, 512], BF16, kind=dk).ap()
    G["mixT"] = nc.dram_tensor("mixT", [D, S], BF16, kind=dk).ap()

    with ExitStack() as es:
        R = Rec(nc, es)
        prm = es.enter_context(nc.sbuf_tensor("prm_sb", [128, NP], F32))
        ones = es.enter_context(nc.sbuf_tensor("ones_sb", [128, 128], BF16))
        G["prm"], G["ones"], G["onesB"] = prm, ones, Buf()
        prmB = Buf()
        R.dma("sp", "prm", lambda e: e.dma_start(out=prm[:, :], in_=prm_d[:, :]), writes=[prmB])
        R.op("dve", lambda e: e.memset(ones[:, :], 1.0), writes=[G["onesB"]])
        R.barrier()
        done = False
        for l in range(DEPTH):
            xin = xT if l == 0 else xs
            phase_A(R, nc, G, l, xin)
            if stop_after == (l, "A"):
                done = True
                break
            phase_B(R, nc, G, l, xin, xs)
            if stop_after == (l, "B"):
                done = True
                break
            last = (l == DEPTH - 1)
            phase_C(R, nc, G, l, xs, yT if last else xs, last)
            if stop_after == (l, "C"):
                done = True
                break
    return nc


def _rel_bucket_np(d):
    n = np.maximum(d, 0)
    nf = np.maximum(n, 1).astype(np.float32)
    large = 16 + (np.log(nf / np.float32(16)) / np.float32(math.log(128 / 16)) * np.float32(16)).astype(np.int32)
    large = np.minimum(large, 31)
    return np.where(n < 16, n, large)


def host_prep(inputs):
    f32 = np.float32
    g = {k: np.asarray(v, dtype=f32) for k, v in inputs.items()}
    prm = np.zeros((128, NP), f32)
    for l in range(DEPTH):
        pb = l * PL
        prm[:, pb + OFF_GM:pb + OFF_GM + 8] = g["g_mix"][l].reshape(8, 128).T
        prm[:, pb + OFF_GF:pb + OFF_GF + 8] = g["g_ffn"][l].reshape(8, 128).T
        prm[:, pb + OFF_PS:pb + OFF_PS + 2] = g["pool_scale"][l].reshape(2, 128).T
        cw = g["conv_w"][l].reshape(3, 2, 128)
        prm[:, pb + OFF_CW:pb + OFF_CW + 6] = cw.transpose(2, 0, 1).reshape(128, 6)
        prm[:, pb + OFF_SG:pb + OFF_SG + 128] = np.broadcast_to(g["subln_g"][l][None, :], (128, 128))
        lo = pb + OFF_LAM
        prm[:, lo:lo + 64] = g["lambda_q1"][l][None, :]
        prm[:, lo + 64:lo + 128] = g["lambda_k1"][l][None, :]
        prm[:, lo + 128:lo + 192] = g["lambda_q2"][l][None, :]
        prm[:, lo + 192:lo + 256] = g["lambda_k2"][l][None, :]
    prm[:, OFF_GFIN:OFF_GFIN + 8] = g["g_final"].reshape(8, 128).T
    prm[:, OFF_CH:OFF_CH + 4] = g["rel_bias"][31][None, :]
    wins = [2.0, 4.0, 8.0, 16.0]
    t1 = np.arange(1, 17, dtype=f32)
    for c in range(2):
        for half in range(2):
            w = wins[c * 2 + half]
            prm[half * 64:(half + 1) * 64, OFF_ID0 + c * 16:OFF_ID0 + c * 16 + 16] = \
                (1.0 / np.minimum(t1, w)).astype(f32)[None, :]
    prm[:, OFF_EPS] = EPS
    prm[:, OFF_SEPS] = SUBLN_EPS
    kl = np.arange(128)[:, None]
    m = np.arange(TW)[None, :]
    d = m - kl
    bidx = _rel_bucket_np(d)
    tbl = np.empty((128, 4, TW), f32)
    for h in range(4):
        tbl[:, h, :] = np.where(d >= 0, g["rel_bias"][:, h][bidx], f32(MASKV))
    ident = np.eye(128, dtype=f32)
    common = {
        "w_in": g["w_in"], "w_o": g["w_o"], "w_gate": g["w_gate"], "w_up": g["w_up"],
        "w_down": g["w_down"], "w_pool": g["w_pool"], "prm": prm, "tbl": tbl, "ident": ident,
    }
    return g, common


_NC_CACHE = {}


def kernel(**inputs):
    g, common = host_prep(inputs)
    x = g["x"]
    B = x.shape[0]
    if "nc" not in _NC_CACHE:
        _NC_CACHE["nc"] = build_program()
    nc = _NC_CACHE["nc"]
    in_maps = []
    for b in range(B):
        m = dict(common)
        m["xT"] = np.ascontiguousarray(x[b].T)
        in_maps.append(m)
    res = run_bass_kernel_spmd(nc, in_maps, core_ids=list(range(B)))
    out = np.stack([np.ascontiguousarray(res.results[b]["yT"].T) for b in range(B)], axis=0)
    return out.astype(np.float32)
```

```python
import math
from contextlib import ExitStack

import numpy as np
import concourse.bass as bass
import concourse.mybir as mybir
from concourse.bass_utils import run_bass_kernel_spmd

F32 = mybir.dt.float32
BF16 = mybir.dt.bfloat16
AF = mybir.ActivationFunctionType
ALU = mybir.AluOpType

S = 4096
D = 1024
DEPTH = 2
NH = 4
DFF = 2816
NF = DFF // 128
TB = 512
NTB = S // TB
INC = 2560
EPS = 1e-6
SUBLN_EPS = 1e-5
MASKV = -30000.0
TW = 256
BCAST_TBL = True

PL = 8 + 8 + 2 + 6 + 128 + 256
OFF_GM, OFF_GF, OFF_PS, OFF_CW, OFF_SG, OFF_LAM = 0, 8, 16, 18, 24, 152
OFF_GFIN = DEPTH * PL
OFF_CH = OFF_GFIN + 8
OFF_ID0 = OFF_CH + 4
OFF_EPS = OFF_ID0 + 32
OFF_SEPS = OFF_EPS + 1
NP = OFF_SEPS + 1


class Buf:
    __slots__ = ("w", "r")

    def __init__(self):
        self.w = None
        self.r = []


class Rec:
    ENG = ("pe", "act", "dve", "pool", "sp")
    CE = ("pe", "act", "dve", "pool")

    def __init__(self, nc, es):
        self.nc = nc
        self.es = es
        self.sems = {}
        self.count = {}
        self.streams = {e: [] for e in self.ENG}
        self.waited = {e: {} for e in self.ENG}
        for e in self.CE:
            self._sem(e)

    def _sem(self, key):
        if key not in self.sems:
            self.sems[key] = self.es.enter_context(self.nc.semaphore("s_" + key))
            self.count[key] = 0
        return self.sems[key]

    def _collect(self, eng, reads, writes, extra):
        need = {}

        def add(tok, kind):
            if tok is None:
                return
            sem, val = tok
            if sem == eng:
                if eng == "pe":
                    return
                if val > self.count[eng]:
                    return
            if need.get(sem, 0) < val:
                need[sem] = val

        for b in reads:
            add(b.w, "raw")
        for b in writes:
            add(b.w, "waw")
            for t in b.r:
                add(t, "war")
        for t in extra:
            add(t, "raw")
        out = []
        wd = self.waited[eng]
        for sem, val in need.items():
            if wd.get(sem, 0) < val:
                wd[sem] = val
                out.append((sem, val))
        return out

    def op(self, eng, fn, reads=(), writes=(), signal=True, extra=()):
        waits = self._collect(eng, reads, writes, extra)
        if signal:
            self.count[eng] += 1
            tok = (eng, self.count[eng])
            inc = (eng, 1)
        else:
            tok = (eng, self.count[eng] + 1)
            inc = None
        for b in reads:
            b.r.append(tok)
        for b in writes:
            b.w = tok
            b.r = []
        self.streams[eng].append((waits, fn, inc))
        return tok

    def dma(self, q, semkey, fn, reads=(), writes=(), extra=()):
        self._sem(semkey)
        waits = self._collect(q, reads, writes, extra)
        self.count[semkey] += 16
        tok = (semkey, self.count[semkey])
        for b in reads:
            b.r.append(tok)
        for b in writes:
            b.w = tok
            b.r = []
        self.streams[q].append((waits, fn, (semkey, 16)))
        return tok

    def barrier(self):
        toks = [(k, v) for k, v in self.count.items() if v > 0]
        for eng in self.ENG:
            waits = []
            wd = self.waited[eng]
            for sem, val in toks:
                if sem == eng:
                    continue
                if wd.get(sem, 0) < val:
                    wd[sem] = val
                    waits.append((sem, val))
            if waits:
                self.streams[eng].append((waits, None, None))

    def emit(self):
        nc = self.nc
        with nc.Block() as block:
            def mk(name):
                def run(e):
                    for waits, fn, inc in self.streams[name]:
                        for sem, val in waits:
                            e.wait_ge(self.sems[sem], val)
                        if fn is not None:
                            ins = fn(e)
                            if inc is not None:
                                ins.then_inc(self.sems[inc[0]], inc[1])
                return run
            block.tensor(mk("pe"))
            block.scalar(mk("act"))
            block.vector(mk("dve"))
            block.gpsimd(mk("pool"))
            block.sync(mk("sp"))
        for k in self.streams:
            self.streams[k] = []


def emit_norm(R, G, xt, xB, sq, sqB, ssbank, ssB, lnv, lnvB, rstd, rstdB, out, outB, gcol0):
    prm, ones, onesB = G["prm"], G["ones"], G["onesB"]
    R.op("act", lambda e: e.activation(out=sq[:, :, :], in_=xt[:, :, :], func=AF.Square),
         reads=[xB], writes=sqB)
    for kc in range(8):
        R.op("pe", lambda e, kc=kc: e.matmul(ssbank[:, :], ones[:, :], sq[:, kc, :],
                                             start=(kc == 0), stop=(kc == 7)),
             reads=sqB + [onesB], writes=[ssB], signal=(kc == 7))
    R.op("act", lambda e: e.activation(out=lnv[:, :], in_=ssbank[:, :], func=AF.Ln,
                                       bias=prm[:, OFF_EPS:OFF_EPS + 1], scale=1.0 / D),
         reads=[ssB], writes=[lnvB])
    R.op("act", lambda e: e.activation(out=rstd[:, :], in_=lnv[:, :], func=AF.Exp, scale=-0.5),
         reads=[lnvB], writes=[rstdB])
    for kc in range(8):
        R.op("dve", lambda e, kc=kc: e.scalar_tensor_tensor(
            out=out[:, kc, :], in0=xt[:, kc, :], scalar=prm[:, gcol0 + kc:gcol0 + kc + 1],
            in1=rstd[:, :], op0=ALU.mult, op1=ALU.mult),
            reads=[xB, rstdB], writes=[outB])


def x_view(ap2d):
    return ap2d.rearrange("(c p) n -> p c n", p=128)


def phase_A(R, nc, G, l, xsrc):
    prm = G["prm"]
    pb = l * PL
    win_d = G["w_in"][l].rearrange("(kc p) n -> p kc n", p=128)
    with ExitStack() as st:
        def sb(name, shape, dt):
            return st.enter_context(nc.sbuf_tensor(f"{name}_L{l}", shape, dt))

        win = sb("winA", [128, 8, INC], BF16)
        xb = [sb(f"xbA{i}", [128, 8, TB], F32) for i in range(2)]
        sq = sb("sqA", [128, 8, TB], BF16)
        hT = [sb(f"hTA{i}", [128, 8, TB], BF16) for i in range(2)]
        lnv = sb("lnvA", [128, TB], F32)
        rstd = sb("rstdA", [128, TB], F32)
        qst = sb("qstA", [128, 4, TB], BF16)
        kst = sb("kstA", [128, 4, TB], BF16)
        vst = sb("vstA", [128, 4, 516], BF16)
        mixst = sb("mixstA", [128, 4, TB], BF16)
        Pb = sb("PbA", [128, 2, 528], F32)
        Ub = sb("UbA", [128, 2, 528], F32)
        gcS = sb("gcSA", [128, 2, TB], F32)
        s2 = sb("s2A", [128, 2, 528], F32)
        s4 = sb("s4A", [128, 2, 528], F32)
        s8 = sb("s8A", [128, 528], F32)
        s16 = sb("s16A", [128, 528], F32)
        tmpf = sb("tmpfA", [128, 16], F32)
        pooled = sb("pooledA", [128, 2, TB], BF16)
        yv = sb("yvA", [128, 2, TB], F32)
        wblk = sb("wblkA", [128, 2, 128], BF16)
        banks = [st.enter_context(nc.psum_tensor(f"L{l}bkA{i}", [128, 512], F32)) for i in range(8)]
        bankB = [Buf() for _ in range(8)]

        winB = [Buf() for _ in range(5)]
        xB = [Buf(), Buf()]
        sqB = [Buf()]
        hB = [Buf(), Buf()]
        lnvB, rstdB, qstB, kstB, vstB, mixB = Buf(), Buf(), Buf(), Buf(), Buf(), Buf()
        PbB, UbB, gcB, s2B, s4B, s8B, s16B, tmpB, poolB, yvB, wblkB = (Buf() for _ in range(11))

        for cg in (0, 1, 3, 4, 2):
            R.dma("pool", f"win{cg}", lambda e, cg=cg: e.dma_start(
                out=win[:, :, cg * 512:(cg + 1) * 512], in_=win_d[:, :, cg * 512:(cg + 1) * 512]),
                writes=[winB[cg]])
        R.op("dve", lambda e: e.memset(wblk[:, :, :], 0.0), writes=[wblkB])
        for g in range(4):
            r0 = (g % 2) * 64
            R.dma("pool", "wblk", lambda e, g=g, r0=r0: e.dma_start(
                out=wblk[r0:r0 + 64, g // 2, r0:r0 + 64], in_=G["w_pool"][l, g, :, :]),
                writes=[wblkB])
        R.op("dve", lambda e: e.memset(
            vst[:, :, :].rearrange("p t (h d) -> p t h d", h=4)[:, :, :, 128:129], 1.0), writes=[vstB])
        R.op("dve", lambda e: e.memset(Pb[:, :, 0:16], 0.0), writes=[PbB])
        R.op("dve", lambda e: e.memset(Ub[:, :, 0:16], 0.0), writes=[UbB])

        xv = x_view(xsrc)

        def load(tb):
            i = tb % 2
            R.dma("sp", f"xA{i}", lambda e: e.dma_start(out=xb[i][:, :, :], in_=xv[:, :, tb * TB:(tb + 1) * TB]),
                  writes=[xB[i]])

        def norm(tb):
            i = tb % 2
            emit_norm(R, G, xb[i], xB[i], sq, sqB, banks[7], bankB[7], lnv, lnvB, rstd, rstdB,
                      hT[i], hB[i], pb + OFF_GM)

        rr = [0]

        def nb():
            i = rr[0] % 5
            rr[0] += 1
            return banks[i], bankB[i]

        def mm_chain(bank, bB, h, hb, col, tsub=None):
            for kc in range(8):
                if tsub is None:
                    fn = lambda e, kc=kc: e.matmul(bank[:, :], win[:, kc, col:col + 128], h[:, kc, :],
                                                   start=(kc == 0), stop=(kc == 7))
                else:
                    fn = lambda e, kc=kc: e.matmul(bank[:, :], h[:, kc, tsub * 128:(tsub + 1) * 128],
                                                   win[:, kc, 1024:1536], start=(kc == 0), stop=(kc == 7))
                R.op("pe", fn, reads=[winB[2 if tsub is not None else col // 512], hb], writes=[bB],
                     signal=(kc == 7))

        load(0)
        load(1)
        norm(0)
        for tb in range(NTB):
            i = tb % 2
            h, hb = hT[i], hB[i]
            tsl = slice(tb * TB, (tb + 1) * TB)
            for hh in range(4):
                bank, bB = nb()
                mm_chain(bank, bB, h, hb, hh * 128)
                R.op("act", lambda e, bank=bank, hh=hh: e.mul(out=qst[:, hh, :], in_=bank[:, :], mul=0.125),
                     reads=[bB], writes=[qstB])
            for hh in range(4):
                bank, bB = nb()
                mm_chain(bank, bB, h, hb, 512 + hh * 128)
                R.op("act", lambda e, bank=bank, hh=hh: e.copy(out=kst[:, hh, :], in_=bank[:, :]),
                     reads=[bB], writes=[kstB])
            R.dma("pool", "qstore", lambda e, tsl=tsl: e.dma_start(
                out=G["qT"].rearrange("h p n -> p h n")[:, :, tsl], in_=qst[:, :, :]), reads=[qstB])
            R.dma("pool", "kstore", lambda e, tsl=tsl: e.dma_start(
                out=G["kT"].rearrange("h p n -> p h n")[:, :, tsl], in_=kst[:, :, :]), reads=[kstB])
            if tb + 1 < NTB:
                norm(tb + 1)
            for c in range(2):
                bank, bB = nb()
                mm_chain(bank, bB, h, hb, 1536 + c * 128)
                R.op("act", lambda e, bank=bank, c=c: e.copy(out=Pb[:, c, 16:528], in_=bank[:, :]),
                     reads=[bB], writes=[PbB])
            for c in range(2):
                bank, bB = nb()
                mm_chain(bank, bB, h, hb, 2048 + c * 128)
                R.op("act", lambda e, bank=bank, c=c: e.copy(out=gcS[:, c, :], in_=bank[:, :]),
                     reads=[bB], writes=[gcB])
            for c in range(2):
                bank, bB = nb()
                mm_chain(bank, bB, h, hb, 2304 + c * 128)
                R.op("dve", lambda e, bank=bank, c=c: e.tensor_tensor(
                    out=Ub[:, c, 16:528], in0=bank[:, :], in1=gcS[:, c, :], op=ALU.mult),
                    reads=[bB, gcB], writes=[UbB])
            gbb = []
            for c in range(2):
                bank, bB = banks[5 + c], bankB[5 + c]
                mm_chain(bank, bB, h, hb, 1792 + c * 128)
                gbb.append((bank, bB))
            for ts in range(4):
                bank, bB = nb()
                mm_chain(bank, bB, h, hb, 0, tsub=ts)
                R.op("act", lambda e, bank=bank, ts=ts: e.copy(
                    out=vst[:, ts, :].rearrange("p (h d) -> p h d", h=4)[:, :, 0:128],
                    in_=bank[:, :].rearrange("p (h d) -> p h d", h=4)),
                    reads=[bB], writes=[vstB])
            R.dma("pool", "vstore", lambda e, tb=tb: e.dma_start(
                out=G["vS"][:, tb * 4:(tb + 1) * 4, :], in_=vst[:, :, :]), reads=[vstB])
            for c in range(2):
                cw = lambda k, c=c: prm[:, pb + OFF_CW + k * 2 + c:pb + OFF_CW + k * 2 + c + 1]
                R.op("dve", lambda e, c=c, cw=cw: e.tensor_scalar(
                    out=yv[:, c, :], in0=Ub[:, c, 14:526], scalar1=cw(0), scalar2=None, op0=ALU.mult),
                    reads=[UbB], writes=[yvB])
                R.op("dve", lambda e, c=c, cw=cw: e.scalar_tensor_tensor(
                    out=yv[:, c, :], in0=Ub[:, c, 15:527], scalar=cw(1), in1=yv[:, c, :],
                    op0=ALU.mult, op1=ALU.add), reads=[UbB, yvB], writes=[yvB])
                R.op("dve", lambda e, c=c, cw=cw: e.scalar_tensor_tensor(
                    out=yv[:, c, :], in0=Ub[:, c, 16:528], scalar=cw(2), in1=yv[:, c, :],
                    op0=ALU.mult, op1=ALU.add), reads=[UbB, yvB], writes=[yvB])
                bank, bB = gbb[c]
                R.op("dve", lambda e, c=c, bank=bank: e.tensor_tensor(
                    out=mixst[:, 2 + c, :], in0=bank[:, :], in1=yv[:, c, :], op=ALU.mult),
                    reads=[bB, yvB], writes=[mixB])
            R.op("dve", lambda e: e.tensor_copy(out=Ub[:, :, 14:16], in_=Ub[:, :, 526:528]),
                 reads=[UbB], writes=[UbB])
            R.op("dve", lambda e: e.tensor_tensor(out=s2[:, :, 1:528], in0=Pb[:, :, 1:528], in1=Pb[:, :, 0:527],
                                                  op=ALU.add), reads=[PbB], writes=[s2B])
            R.op("dve", lambda e: e.tensor_tensor(out=s4[:, :, 3:528], in0=s2[:, :, 3:528], in1=s2[:, :, 1:526],
                                                  op=ALU.add), reads=[s2B], writes=[s4B])
            R.op("dve", lambda e: e.tensor_tensor(out=s8[:, 7:528], in0=s4[:, 1, 7:528], in1=s4[:, 1, 3:524],
                                                  op=ALU.add), reads=[s4B], writes=[s8B])
            R.op("dve", lambda e: e.tensor_tensor(out=s16[64:128, 15:528], in0=s8[64:128, 15:528],
                                                  in1=s8[64:128, 7:520], op=ALU.add), reads=[s8B], writes=[s16B])
            grp = [(s2, lambda r: s2[r, 0, 16:528], 0, 0, 2.0, s2B),
                   (s4, lambda r: s4[r, 0, 16:528], 64, 0, 4.0, s4B),
                   (s8, lambda r: s8[r, 16:528], 0, 1, 8.0, s8B),
                   (s16, lambda r: s16[r, 16:528], 64, 1, 16.0, s16B)]
            for (_, src, r0, c, w, sB) in grp:
                rs = slice(r0, r0 + 64)
                R.op("dve", lambda e, src=src, rs=rs, c=c, w=w: e.scalar_tensor_tensor(
                    out=pooled[rs, c, :], in0=src(rs), scalar=1.0 / w, in1=Pb[rs, c, 16:528],
                    op0=ALU.mult, op1=ALU.subtract), reads=[sB, PbB], writes=[poolB])
            if tb == 0:
                for (_, src, r0, c, w, sB) in grp:
                    rs = slice(r0, r0 + 64)
                    R.op("dve", lambda e, src=src, rs=rs, c=c: e.tensor_tensor(
                        out=tmpf[rs, :], in0=src(rs)[:, 0:16],
                        in1=prm[rs, OFF_ID0 + c * 16:OFF_ID0 + c * 16 + 16], op=ALU.mult),
                        reads=[sB], writes=[tmpB])
                    R.op("dve", lambda e, rs=rs, c=c: e.tensor_tensor(
                        out=pooled[rs, c, 0:16], in0=tmpf[rs, :], in1=Pb[rs, c, 16:32], op=ALU.subtract),
                        reads=[tmpB, PbB], writes=[poolB])
            R.op("dve", lambda e: e.tensor_copy(out=Pb[:, :, 0:16], in_=Pb[:, :, 512:528]),
                 reads=[PbB], writes=[PbB])
            for c in range(2):
                bank, bB = nb()
                R.op("pe", lambda e, bank=bank, c=c: e.matmul(bank[:, :], wblk[:, c, :], pooled[:, c, :],
                                                              start=True, stop=True),
                     reads=[wblkB, poolB], writes=[bB])
                R.op("dve", lambda e, bank=bank, c=c: e.tensor_scalar(
                    out=mixst[:, c, :], in0=bank[:, :], scalar1=prm[:, pb + OFF_PS + c:pb + OFF_PS + c + 1],
                    scalar2=None, op0=ALU.mult), reads=[bB], writes=[mixB])
            R.dma("pool", "mixstore", lambda e, tsl=tsl: e.dma_start(
                out=x_view(G["mixT"])[:, 4:8, tsl], in_=mixst[:, :, :]), reads=[mixB])
            if tb + 2 < NTB:
                load(tb + 2)
        R.barrier()
        R.emit()


def phase_B(R, nc, G, l, xsrc, xdst):
    prm = G["prm"]
    pb = l * PL
    lam_init = 0.8 - 0.6 * math.exp(-0.3 * l)
    wo_d = G["w_o"][l].rearrange("(kc p) n -> p kc n", p=128)
    with ExitStack() as st:
        def sb(name, shape, dt):
            return st.enter_context(nc.sbuf_tensor(f"{name}_L{l}", shape, dt))

        KT = sb("KTB", [128, 4, S], BF16)
        V = sb("VB", [128, 32, 516], BF16)
        wo = sb("woB", [128, 8, D], BF16)
        tbl = sb("tblB", [128, 4, TW], F32)
        tblh = sb("tblhB", [128, 4, TW], BF16)
        tbll = sb("tbllB", [128, 4, TW], BF16)
        tblhB, tbllB = Buf(), Buf()
        Qb = [sb(f"QbB{i}", [128, 4, TB], BF16) for i in range(2)]
        PT = [sb(f"PTB{i}", [128, 2, TB], BF16) for i in range(2)]
        mixb = [sb(f"mixbB{i}", [128, 8, TB], BF16) for i in range(2)]
        xb = [sb(f"xbB{i}", [128, 8, TB], F32) for i in range(2)]
        accS = sb("accSB", [128, 8, 129], F32)
        rl = sb("rlB", [128, 2, 4], F32)
        r2n = sb("r2nB", [128, 4], F32)
        tt = sb("ttB", [128, 128], F32)
        attf = sb("attfB", [128, 4, 128], F32)
        junk = sb("junkB", [128, 128], F32)
        ss = sb("ssB", [128, 4], F32)
        vv = sb("vvB", [128, 4], F32)
        nhalf = sb("nhalfB", [128, 4], F32)
        rstd = sb("rstdB", [128, 4], F32)
        attn = sb("attnB", [128, 4, 128], BF16)
        ident = sb("identB", [128, 128], BF16)
        Gp = sb("GpB", [128, 128], F32)
        lamt = sb("lamtB", [128, 8], F32)
        lprod = sb("lprodB", [128, 64], F32)
        Sp = [st.enter_context(nc.psum_tensor(f"L{l}SpB{i}", [128, 2, 512], F32)) for i in range(2)]
        accb = [st.enter_context(nc.psum_tensor(f"L{l}accB{i}", [128, 3, 129], F32)) for i in range(3)]
        trb = st.enter_context(nc.psum_tensor(f"L{l}trB", [128, 512], BF16))

        SB_ = [Buf(), Buf()]
        WOB = [[Buf(), Buf()], [Buf(), Buf()]]
        PTB = [Buf(), Buf()]
        accB = [Buf() for _ in range(8)]
        trB = Buf()
        KTB = [Buf() for _ in range(NTB)]
        VBf = [Buf() for _ in range(NTB)]
        QB = [Buf(), Buf()]
        mixatt = [[Buf() for _ in range(4)] for _ in range(2)]
        mixpc = [Buf(), Buf()]
        xB = [[Buf() for _ in range(8)] for _ in range(2)]
        woB, tblB, identB, GpB, lamB, lprodB, nhB = (Buf() for _ in range(7))
        accSB, rlB, r2nB, ttB, attfB, junkB, ssB_, vvB, rstdB, attnB = (Buf() for _ in range(10))

        def acc(idx):
            return accb[idx // 3][:, idx % 3, :]

        R.dma("sp", "tblB", lambda e: e.dma_start(out=tbl[:, :, :], in_=G["tbl"][:, :, :]), writes=[tblB])
        for hh in range(NH):
            R.op("dve", lambda e, hh=hh: e.tensor_scalar(
                out=tbl[:, hh, :], in0=tbl[:, hh, :], scalar1=prm[:, OFF_CH + hh:OFF_CH + hh + 1],
                scalar2=None, op0=ALU.subtract), reads=[tblB], writes=[tblB])
        R.op("dve", lambda e: e.tensor_copy(out=tblh[:, :, :], in_=tbl[:, :, :]), reads=[tblB], writes=[tblhB])
        R.op("dve", lambda e: e.tensor_tensor(out=tbl[:, :, :], in0=tbl[:, :, :], in1=tblh[:, :, :],
                                              op=ALU.subtract), reads=[tblB, tblhB], writes=[tblB])
        R.op("dve", lambda e: e.tensor_copy(out=tbll[:, :, :], in_=tbl[:, :, :]), reads=[tblB], writes=[tbllB])
        R.op("dve", lambda e: e.memset(nhalf[:, :], -0.5), writes=[nhB])
        kT_v = G["kT"].rearrange("h p n -> p h n")

        def kv_load(tb):
            R.dma("sp", f"kld{tb}", lambda e: e.dma_start(
                out=KT[:, :, tb * TB:(tb + 1) * TB], in_=kT_v[:, :, tb * TB:(tb + 1) * TB]), writes=[KTB[tb]])
            R.dma("sp", f"vld{tb}", lambda e: e.dma_start(
                out=V[:, tb * 4:(tb + 1) * 4, :], in_=G["vS"][:, tb * 4:(tb + 1) * 4, :]), writes=[VBf[tb]])
        R.dma("pool", "identB", lambda e: e.dma_start(out=ident[:, :], in_=G["ident"][:, :]), writes=[identB])
        R.dma("pool", "woB", lambda e: e.dma_start(out=wo[:, :, :], in_=wo_d[:, :, :]), writes=[woB])
        lo = pb + OFF_LAM
        for j in range(2):
            R.op("dve", lambda e, j=j: e.tensor_tensor(
                out=lprod[:, :], in0=prm[:, lo + j * 128:lo + j * 128 + 64],
                in1=prm[:, lo + j * 128 + 64:lo + j * 128 + 128], op=ALU.mult), writes=[lprodB])
            R.op("dve", lambda e, j=j: e.reduce_sum(out=lamt[:, j:j + 1], in_=lprod[:, :],
                                                   axis=mybir.AxisListType.X),
                 reads=[lprodB], writes=[lamB])
        R.op("act", lambda e: e.activation(out=lamt[:, 2:4], in_=lamt[:, 0:2], func=AF.Exp),
             reads=[lamB], writes=[lamB])
        R.op("dve", lambda e: e.tensor_tensor(out=lamt[:, 4:5], in0=lamt[:, 2:3], in1=lamt[:, 3:4],
                                              op=ALU.subtract), reads=[lamB], writes=[lamB])
        R.op("dve", lambda e: e.tensor_scalar(out=lamt[:, 5:6], in0=lamt[:, 4:5], scalar1=lam_init,
                                              scalar2=-1.0, op0=ALU.add, op1=ALU.mult),
             reads=[lamB], writes=[lamB])
        R.op("dve", lambda e: e.tensor_scalar(out=Gp[:, :], in0=prm[:, pb + OFF_SG:pb + OFF_SG + 128],
                                              scalar1=1.0 - lam_init, scalar2=None, op0=ALU.mult),
             writes=[GpB])
        nlam = lamt[:, 5:6]

        xv = x_view(xsrc)
        xo = x_view(xdst)
        qT_v = G["qT"].rearrange("h p n -> p h n")
        mix_v = x_view(G["mixT"])

        def loads(qb):
            i = qb % 2
            tsl = slice(qb * TB, (qb + 1) * TB)
            R.dma("sp", f"qld{i}", lambda e: e.dma_start(out=Qb[i][:, :, :], in_=qT_v[:, :, tsl]), writes=[QB[i]])
            R.dma("sp", f"mld{i}", lambda e: e.dma_start(out=mixb[i][:, 4:8, :], in_=mix_v[:, 4:8, tsl]),
                  writes=[mixpc[i]])
            R.dma("sp", f"xld{i}", lambda e: e.dma_start(out=xb[i][:, :, :], in_=xv[:, :, tsl]), writes=xB[i])

        pending = []

        def flush_pending():
            while pending:
                pending.pop(0)()

        def attn_head(qb, h):
            i = qb % 2
            nk = 4 * (qb + 1)

            def qk(kt):
                j = kt - 4 * qb
                qlo = max(0, 128 * j)
                p = kt % 2
                near = (j >= -1)
                for m in range(2):
                    R.op("pe", lambda e, m=m, p=p, kt=kt, qlo=qlo, near=near: e.matmul(
                        Sp[p][:, m, qlo:512], KT[m * 64:(m + 1) * 64, h, kt * 128:(kt + 1) * 128],
                        Qb[i][m * 64:(m + 1) * 64, h, qlo:512], start=True, stop=(not near)),
                        reads=[KTB[kt // 4], QB[i]], writes=[SB_[p], WOB[p][0], WOB[p][1]],
                        signal=(m == 1 and not near))
                if near:
                    mlo = 128 if j < 0 else 0
                    mhi = min(256, mlo + 512 - qlo)
                    c0, c1 = mlo + 128 * j, mhi + 128 * j
                    for m in range(2):
                        for part, tb_ in enumerate((tblh, tbll)):
                            R.op("pe", lambda e, m=m, p=p, c0=c0, c1=c1, mlo=mlo, mhi=mhi, tb_=tb_, part=part: e.matmul(
                                Sp[p][:, m, c0:c1], ident[:, :], tb_[:, h, mlo:mhi], start=False,
                                stop=(part == 1)),
                                reads=[identB, tblhB, tbllB], writes=[SB_[p]], signal=(m == 1 and part == 1))

            def soft(kt):
                j = kt - 4 * qb
                qlo = max(0, 128 * j)
                p = kt % 2
                R.op("act", lambda e, p=p, qlo=qlo: e.activation(
                    out=PT[p][:, :, qlo:512], in_=Sp[p][:, :, qlo:512], func=AF.Exp),
                    reads=[SB_[p]], writes=[PTB[p]])

            def pv(kt):
                j = kt - 4 * qb
                p = kt % 2
                s0 = max(j, 0)
                seen = set()
                for m in range(2):
                    for sub in range(s0, 4):
                        idx = m * 4 + sub
                        last = (kt == 4 * qb + sub)
                        bnk = idx // 3
                        st_ = (kt == 0 and bnk not in seen)
                        seen.add(bnk)
                        R.op("pe", lambda e, m=m, p=p, sub=sub, kt=kt, last=last, st_=st_, idx=idx: e.matmul(
                            acc(idx), PT[p][:, m, sub * 128:(sub + 1) * 128], V[:, kt, h * 129:(h + 1) * 129],
                            start=st_, stop=last, skip_group_check=True),
                            reads=[PTB[p], VBf[kt // 4]], writes=[accB[idx]],
                            signal=(last or (m == 1 and sub == 3)))

            defer_at = min(nk - 1, 7)
            qk(0)
            for kt in range(nk):
                if kt + 1 < nk:
                    qk(kt + 1)
                soft(kt)
                pv(kt)
                if kt == defer_at:
                    flush_pending()
            for bnk in range(3):
                n = 3 if bnk < 2 else 2
                R.op("dve", lambda e, bnk=bnk, n=n: e.tensor_copy(
                    out=accS[:, 3 * bnk:3 * bnk + n, :], in_=accb[bnk][:, 0:n, :]),
                    reads=accB[3 * bnk:3 * bnk + n], writes=[accSB])
            for m in range(2):
                R.op("dve", lambda e, m=m: e.reciprocal(out=rl[:, m, :], in_=accS[:, 4 * m:4 * m + 4, 128]),
                     reads=[accSB], writes=[rlB])
            R.op("dve", lambda e: e.tensor_scalar(out=r2n[:, :], in0=rl[:, 1, :], scalar1=nlam,
                                                  scalar2=None, op0=ALU.mult),
                 reads=[rlB, lamB], writes=[r2nB])
            for sub in range(4):
                R.op("dve", lambda e, sub=sub: e.tensor_scalar(
                    out=tt[:, :], in0=accS[:, sub, 0:128], scalar1=rl[:, 0, sub:sub + 1], scalar2=None,
                    op0=ALU.mult), reads=[accSB, rlB], writes=[ttB])
                R.op("dve", lambda e, sub=sub: e.scalar_tensor_tensor(
                    out=attf[:, sub, :], in0=accS[:, 4 + sub, 0:128], scalar=r2n[:, sub:sub + 1],
                    in1=tt[:, :], op0=ALU.mult, op1=ALU.add),
                    reads=[accSB, r2nB, ttB], writes=[attfB])
                R.op("dve", lambda e, sub=sub: e.scalar_tensor_tensor(
                    out=junk[:, :], in0=attf[:, sub, :], scalar=1.0, in1=attf[:, sub, :],
                    op0=ALU.mult, op1=ALU.mult, accum_out=ss[:, sub:sub + 1]),
                    reads=[attfB], writes=[junkB, ssB_])
            R.op("dve", lambda e: e.tensor_scalar(out=vv[:, :], in0=ss[:, :], scalar1=1.0 / 128,
                                                  scalar2=SUBLN_EPS, op0=ALU.mult, op1=ALU.add),
                 reads=[ssB_], writes=[vvB])
            R.op("pool", lambda e: e.tensor_tensor(out=rstd[:, :], in0=vv[:, :], in1=nhalf[:, :], op=ALU.pow),
                 reads=[vvB, nhB], writes=[rstdB])
            for sub in range(4):
                R.op("dve", lambda e, sub=sub: e.scalar_tensor_tensor(
                    out=attn[:, sub, :], in0=attf[:, sub, :], scalar=rstd[:, sub:sub + 1], in1=Gp[:, :],
                    op0=ALU.mult, op1=ALU.mult), reads=[attfB, rstdB, GpB], writes=[attnB])

            def stage2():
                for sub in range(4):
                    R.op("pe", lambda e, sub=sub: e.transpose(trb[:, sub * 128:(sub + 1) * 128], attn[:, sub, :],
                                                              ident[:, :]),
                         reads=[attnB, identB], writes=[trB], signal=(sub == 3))
                R.op("dve", lambda e: e.tensor_copy(out=mixb[i][:, h, :], in_=trb[:, :]),
                     reads=[trB], writes=[mixatt[i][h]])
            pending.append(stage2)

        def wo_block(qb):
            i = qb % 2
            tsl = slice(qb * TB, (qb + 1) * TB)
            for half in range(2):
                ocs = range(4 * half, 4 * half + 4)
                for oc in ocs:
                    p, m = (oc // 2) % 2, oc % 2
                    for kc in (4, 5, 6, 7, 0, 1, 2):
                        rd = [woB, mixatt[i][kc]] if kc < 4 else [woB, mixpc[i]]
                        R.op("pe", lambda e, p=p, m=m, kc=kc, oc=oc: e.matmul(
                            Sp[p][:, m, :], wo[:, kc, oc * 128:(oc + 1) * 128], mixb[i][:, kc, :],
                            start=(kc == 4), stop=False), reads=rd, writes=[WOB[p][m], SB_[p]], signal=False)
                for oc in ocs:
                    p, m = (oc // 2) % 2, oc % 2
                    R.op("pe", lambda e, p=p, m=m, oc=oc: e.matmul(
                        Sp[p][:, m, :], wo[:, 3, oc * 128:(oc + 1) * 128], mixb[i][:, 3, :],
                        start=False, stop=True), reads=[woB, mixatt[i][3]], writes=[WOB[p][m], SB_[p]])
                    R.op("dve", lambda e, p=p, m=m, oc=oc: e.tensor_tensor(
                        out=xb[i][:, oc, :], in0=Sp[p][:, m, :], in1=xb[i][:, oc, :], op=ALU.add),
                        reads=[WOB[p][m], xB[i][oc]], writes=[xB[i][oc]])
            R.dma("pool", f"xst{i}", lambda e: e.dma_start(out=xo[:, :, tsl], in_=xb[i][:, :, :]),
                  reads=xB[i])

        loads(0)
        kv_load(0)
        for qb in range(NTB):
            if qb + 1 < NTB:
                loads(qb + 1)
                kv_load(qb + 1)
            for h in range(NH):
                attn_head(qb, h)
            flush_pending()
            wo_block(qb)
        R.barrier()
        R.emit()


def phase_C(R, nc, G, l, xsrc, xdst, final):
    prm = G["prm"]
    pb = l * PL
    wg_d = G["w_gate"][l].rearrange("(kc p) n -> p kc n", p=128)
    wu_d = G["w_up"][l].rearrange("(kc p) n -> p kc n", p=128)
    wd_d = G["w_down"][l].rearrange("(f p) n -> p f n", p=128)
    with ExitStack() as st:
        def sb(name, shape, dt):
            return st.enter_context(nc.sbuf_tensor(f"{name}_L{l}", shape, dt))

        wg = sb("wgC", [128, 8, DFF], BF16)
        wu = sb("wuC", [128, 8, DFF], BF16)
        wd = sb("wdC", [128, NF, D], BF16)
        xb = [sb(f"xbC{i}", [128, 8, TB], F32) for i in range(2)]
        hT = sb("hTC", [128, 8, TB], BF16)
        aT = sb("aTC", [128, NF, TB], BF16)
        rstd = sb("rstdC", [128, TB], F32)
        sg = sb("sgC", [128, TB], F32)
        sqh = sb("sqhC", [128, 4, TB], BF16)
        banks = [st.enter_context(nc.psum_tensor(f"L{l}bkC{i}", [128, 512], F32)) for i in range(8)]
        bankB = [Buf() for _ in range(8)]
        wgB = [Buf() for _ in range(NF // 2)]
        wuB = [Buf() for _ in range(NF // 2)]
        wdB = [Buf() for _ in range(NF)]
        xB = [[Buf() for _ in range(8)] for _ in range(2)]
        hB, rstdB, sgB, sqhB = Buf(), Buf(), Buf(), Buf()
        aB = [Buf() for _ in range(NF)]

        for g2 in range(NF // 2):
            csl = slice(g2 * 256, (g2 + 1) * 256)
            R.dma("pool", f"wg{g2}", lambda e, csl=csl: e.dma_start(out=wg[:, :, csl], in_=wg_d[:, :, csl]),
                  writes=[wgB[g2]])
            R.dma("pool", f"wu{g2}", lambda e, csl=csl: e.dma_start(out=wu[:, :, csl], in_=wu_d[:, :, csl]),
                  writes=[wuB[g2]])
        for g2 in range(NF // 2):
            R.dma("pool", f"wd{g2}", lambda e, g2=g2: e.dma_start(out=wd[:, 2 * g2:2 * g2 + 2, :],
                                                                  in_=wd_d[:, 2 * g2:2 * g2 + 2, :]),
                  writes=[wdB[2 * g2], wdB[2 * g2 + 1]])

        xv = x_view(xsrc)
        xo = x_view(xdst)
        prm_g = pb + OFF_GF
        ones, onesB = G["ones"], G["onesB"]

        def load(tb):
            i = tb % 2
            R.dma("sp", f"xC{i}", lambda e: e.dma_start(out=xb[i][:, :, :], in_=xv[:, :, tb * TB:(tb + 1) * TB]),
                  writes=xB[i])

        def norm_rstd(i):
            for half in range(2):
                R.op("act", lambda e, half=half: e.activation(
                    out=sqh[:, :, :], in_=xb[i][:, 4 * half:4 * half + 4, :], func=AF.Square),
                    reads=xB[i][4 * half:4 * half + 4], writes=[sqhB])
                for k4 in range(4):
                    kc = 4 * half + k4
                    R.op("pe", lambda e, k4=k4, kc=kc: e.matmul(banks[7][:, :], ones[:, :], sqh[:, k4, :],
                                                                start=(kc == 0), stop=(kc == 7)),
                         reads=[sqhB, onesB], writes=[bankB[7]], signal=(k4 == 3))
            R.op("act", lambda e: e.activation(out=rstd[:, :], in_=banks[7][:, :], func=AF.Ln,
                                               bias=prm[:, OFF_EPS:OFF_EPS + 1], scale=1.0 / D),
                 reads=[bankB[7]], writes=[rstdB])
            R.op("act", lambda e: e.activation(out=rstd[:, :], in_=rstd[:, :], func=AF.Exp, scale=-0.5),
                 reads=[rstdB], writes=[rstdB])

        def norm_apply(i, dst, dstB_of, gcol0):
            for kc in range(8):
                R.op("dve", lambda e, kc=kc: e.scalar_tensor_tensor(
                    out=dst[:, kc, :], in0=xb[i][:, kc, :], scalar=prm[:, gcol0 + kc:gcol0 + kc + 1],
                    in1=rstd[:, :], op0=ALU.mult, op1=ALU.mult),
                    reads=[xB[i][kc], rstdB], writes=[dstB_of(kc)])

        pending = []

        def finish(tb):
            i = tb % 2
            tsl = slice(tb * TB, (tb + 1) * TB)
            if final:
                norm_rstd(i)
                norm_apply(i, xb[i], lambda kc: xB[i][kc], OFF_GFIN)
            R.dma("pool", f"xstC{i}", lambda e: e.dma_start(out=xo[:, :, tsl], in_=xb[i][:, :, :]),
                  reads=xB[i])

        def block(tb):
            i = tb % 2
            for f in range(NF):
                if f == 1:
                    while pending:
                        pending.pop(0)()
                    if tb + 1 < NTB:
                        load(tb + 1)
                if f == 5 and tb + 1 < NTB:
                    norm_rstd((tb + 1) % 2)
                bg, bgB = banks[(2 * f) % 6], bankB[(2 * f) % 6]
                bu, buB = banks[(2 * f + 1) % 6], bankB[(2 * f + 1) % 6]
                for kc in range(8):
                    R.op("pe", lambda e, kc=kc, f=f, bg=bg: e.matmul(
                        bg[:, :], wg[:, kc, f * 128:(f + 1) * 128], hT[:, kc, :],
                        start=(kc == 0), stop=(kc == 7)), reads=[wgB[f // 2], hB], writes=[bgB], signal=(kc == 7))
                for kc in range(8):
                    R.op("pe", lambda e, kc=kc, f=f, bu=bu: e.matmul(
                        bu[:, :], wu[:, kc, f * 128:(f + 1) * 128], hT[:, kc, :],
                        start=(kc == 0), stop=(kc == 7)), reads=[wuB[f // 2], hB], writes=[buB], signal=(kc == 7))
                R.op("act", lambda e, f=f, bg=bg: e.activation(out=sg[:, :], in_=bg[:, :], func=AF.Silu),
                     reads=[bgB], writes=[sgB])
                R.op("dve", lambda e, f=f, bu=bu: e.tensor_tensor(
                    out=aT[:, f, :], in0=bu[:, :], in1=sg[:, :], op=ALU.mult),
                    reads=[buB, sgB], writes=[aB[f]])
            if tb + 1 < NTB:
                norm_apply((tb + 1) % 2, hT, lambda kc: hB, prm_g)
            for oc in range(8):
                bk, bkB = banks[oc % 6], bankB[oc % 6]
                for f in range(NF):
                    R.op("pe", lambda e, f=f, oc=oc, bk=bk: e.matmul(
                        bk[:, :], wd[:, f, oc * 128:(oc + 1) * 128], aT[:, f, :],
                        start=(f == 0), stop=(f == NF - 1)), reads=[wdB[f], aB[f]], writes=[bkB],
                        signal=(f == NF - 1))
                R.op("dve", lambda e, oc=oc, bk=bk: e.tensor_tensor(
                    out=xb[i][:, oc, :], in0=bk[:, :], in1=xb[i][:, oc, :], op=ALU.add),
                    reads=[bkB, xB[i][oc]], writes=[xB[i][oc]])
            pending.append(lambda: finish(tb))

        load(0)
        norm_rstd(0)
        norm_apply(0, hT, lambda kc: hB, prm_g)
        for tb in range(NTB):
            block(tb)
        while pending:
            pending.pop(0)()
        R.barrier()
        R.emit()


def build_program(stop_after=None, debug=False):
    nc = bass.Bass("TRN2", target_bir_lowering=False)
    dk = "ExternalOutput" if debug else "Internal"
    G = {}
    xT = nc.dram_tensor("xT", [D, S], F32, kind="ExternalInput").ap()
    G["w_in"] = nc.dram_tensor("w_in", [DEPTH, D, INC], F32, kind="ExternalInput").ap()
    G["w_o"] = nc.dram_tensor("w_o", [DEPTH, D, D], F32, kind="ExternalInput").ap()
    G["w_gate"] = nc.dram_tensor("w_gate", [DEPTH, D, DFF], F32, kind="ExternalInput").ap()
    G["w_up"] = nc.dram_tensor("w_up", [DEPTH, D, DFF], F32, kind="ExternalInput").ap()
    G["w_down"] = nc.dram_tensor("w_down", [DEPTH, DFF, D], F32, kind="ExternalInput").ap()
    G["w_pool"] = nc.dram_tensor("w_pool", [DEPTH, 4, 64, 64], F32, kind="ExternalInput").ap()
    prm_d = nc.dram_tensor("prm", [128, NP], F32, kind="ExternalInput").ap()
    G["tbl"] = nc.dram_tensor("tbl", [128, 4, TW], F32, kind="ExternalInput").ap()
    G["ident"] = nc.dram_tensor("ident", [128, 128], F32, kind="ExternalInput").ap()
    yT = nc.dram_tensor("yT", [D, S], F32, kind="ExternalOutput").ap()
    xs = nc.dram_tensor("xs", [D, S], F32, kind=dk).ap()
    G["qT"] = nc.dram_tensor("qT", [4, 128, S], BF16, kind=dk).ap()
    G["kT"] = nc.dram_tensor("kT", [4, 128, S], BF16, kind=dk).ap()
    G["vS"] = nc.dram_tensor("vS", [128, 32, 516], BF16, kind=dk).ap()
    G["mixT"] = nc.dram_tensor("mixT", [D, S], BF16, kind=dk).ap()

    with ExitStack() as es:
        R = Rec(nc, es)
        prm = es.enter_context(nc.sbuf_tensor("prm_sb", [128, NP], F32))
        ones = es.enter_context(nc.sbuf_tensor("ones_sb", [128, 128], BF16))
        G["prm"], G["ones"], G["onesB"] = prm, ones, Buf()
        prmB = Buf()
        R.dma("sp", "prm", lambda e: e.dma_start(out=prm[:, :], in_=prm_d[:, :]), writes=[prmB])
        R.op("dve", lambda e: e.memset(ones[:, :], 1.0), writes=[G["onesB"]])
        R.barrier()
        done = False
        for l in range(DEPTH):
            xin = xT if l == 0 else xs
            phase_A(R, nc, G, l, xin)
            if stop_after == (l, "A"):
                done = True
                break
            phase_B(R, nc, G, l, xin, xs)
            if stop_after == (l, "B"):
                done = True
                break
            last = (l == DEPTH - 1)
            phase_C(R, nc, G, l, xs, yT if last else xs, last)
            if stop_after == (l, "C"):
                done = True
                break
    return nc


def _rel_bucket_np(d):
    n = np.maximum(d, 0)
    nf = np.maximum(n, 1).astype(np.float32)
    large = 16 + (np.log(nf / np.float32(16)) / np.float32(math.log(128 / 16)) * np.float32(16)).astype(np.int32)
    large = np.minimum(large, 31)
    return np.where(n < 16, n, large)


def host_prep(inputs):
    f32 = np.float32
    g = {k: np.asarray(v, dtype=f32) for k, v in inputs.items()}
    prm = np.zeros((128, NP), f32)
    for l in range(DEPTH):
        pb = l * PL
        prm[:, pb + OFF_GM:pb + OFF_GM + 8] = g["g_mix"][l].reshape(8, 128).T
        prm[:, pb + OFF_GF:pb + OFF_GF + 8] = g["g_ffn"][l].reshape(8, 128).T
        prm[:, pb + OFF_PS:pb + OFF_PS + 2] = g["pool_scale"][l].reshape(2, 128).T
        cw = g["conv_w"][l].reshape(3, 2, 128)
        prm[:, pb + OFF_CW:pb + OFF_CW + 6] = cw.transpose(2, 0, 1).reshape(128, 6)
        prm[:, pb + OFF_SG:pb + OFF_SG + 128] = np.broadcast_to(g["subln_g"][l][None, :], (128, 128))
        lo = pb + OFF_LAM
        prm[:, lo:lo + 64] = g["lambda_q1"][l][None, :]
        prm[:, lo + 64:lo + 128] = g["lambda_k1"][l][None, :]
        prm[:, lo + 128:lo + 192] = g["lambda_q2"][l][None, :]
        prm[:, lo + 192:lo + 256] = g["lambda_k2"][l][None, :]
    prm[:, OFF_GFIN:OFF_GFIN + 8] = g["g_final"].reshape(8, 128).T
    prm[:, OFF_CH:OFF_CH + 4] = g["rel_bias"][31][None, :]
    wins = [2.0, 4.0, 8.0, 16.0]
    t1 = np.arange(1, 17, dtype=f32)
    for c in range(2):
        for half in range(2):
            w = wins[c * 2 + half]
            prm[half * 64:(half + 1) * 64, OFF_ID0 + c * 16:OFF_ID0 + c * 16 + 16] = \
                (1.0 / np.minimum(t1, w)).astype(f32)[None, :]
    prm[:, OFF_EPS] = EPS
    prm[:, OFF_SEPS] = SUBLN_EPS
    kl = np.arange(128)[:, None]
    m = np.arange(TW)[None, :]
    d = m - kl
    bidx = _rel_bucket_np(d)
    tbl = np.empty((128, 4, TW), f32)
    for h in range(4):
        tbl[:, h, :] = np.where(d >= 0, g["rel_bias"][:, h][bidx], f32(MASKV))
    ident = np.eye(128, dtype=f32)
    common = {
        "w_in": g["w_in"], "w_o": g["w_o"], "w_gate": g["w_gate"], "w_up": g["w_up"],
        "w_down": g["w_down"], "w_pool": g["w_pool"], "prm": prm, "tbl": tbl, "ident": ident,
    }
    return g, common


_NC_CACHE = {}


def kernel(**inputs):
    g, common = host_prep(inputs)
    x = g["x"]
    B = x.shape[0]
    if "nc" not in _NC_CACHE:
        _NC_CACHE["nc"] = build_program()
    nc = _NC_CACHE["nc"]
    in_maps = []
    for b in range(B):
        m = dict(common)
        m["xT"] = np.ascontiguousarray(x[b].T)
        in_maps.append(m)
    res = run_bass_kernel_spmd(nc, in_maps, core_ids=list(range(B)))
    out = np.stack([np.ascontiguousarray(res.results[b]["yT"].T) for b in range(B)], axis=0)
    return out.astype(np.float32)
```

```python
import math
from contextlib import ExitStack

import numpy as np
import concourse.bass as bass
import concourse.mybir as mybir
from concourse.bass_utils import run_bass_kernel_spmd

F32 = mybir.dt.float32
BF16 = mybir.dt.bfloat16
AF = mybir.ActivationFunctionType
ALU = mybir.AluOpType

S = 4096
D = 1024
DEPTH = 2
NH = 4
DFF = 2816
NF = DFF // 128
TB = 512
NTB = S // TB
INC = 2560
EPS = 1e-6
SUBLN_EPS = 1e-5
MASKV = -30000.0
TW = 256
BCAST_TBL = True

PL = 8 + 8 + 2 + 6 + 128 + 256
OFF_GM, OFF_GF, OFF_PS, OFF_CW, OFF_SG, OFF_LAM = 0, 8, 16, 18, 24, 152
OFF_GFIN = DEPTH * PL
OFF_CH = OFF_GFIN + 8
OFF_ID0 = OFF_CH + 4
OFF_EPS = OFF_ID0 + 32
OFF_SEPS = OFF_EPS + 1
NP = OFF_SEPS + 1


class Buf:
    __slots__ = ("w", "r")

    def __init__(self):
        self.w = None
        self.r = []


class Rec:
    ENG = ("pe", "act", "dve", "pool", "sp")
    CE = ("pe", "act", "dve", "pool")

    def __init__(self, nc, es):
        self.nc = nc
        self.es = es
        self.sems = {}
        self.count = {}
        self.streams = {e: [] for e in self.ENG}
        self.waited = {e: {} for e in self.ENG}
        for e in self.CE:
            self._sem(e)

    def _sem(self, key):
        if key not in self.sems:
            self.sems[key] = self.es.enter_context(self.nc.semaphore("s_" + key))
            self.count[key] = 0
        return self.sems[key]

    def _collect(self, eng, reads, writes, extra):
        need = {}

        def add(tok, kind):
            if tok is None:
                return
            sem, val = tok
            if sem == eng:
                if eng == "pe":
                    return
                if val > self.count[eng]:
                    return
            if need.get(sem, 0) < val:
                need[sem] = val

        for b in reads:
            add(b.w, "raw")
        for b in writes:
            add(b.w, "waw")
            for t in b.r:
                add(t, "war")
        for t in extra:
            add(t, "raw")
        out = []
        wd = self.waited[eng]
        for sem, val in need.items():
            if wd.get(sem, 0) < val:
                wd[sem] = val
                out.append((sem, val))
        return out

    def op(self, eng, fn, reads=(), writes=(), signal=True, extra=()):
        waits = self._collect(eng, reads, writes, extra)
        if signal:
            self.count[eng] += 1
            tok = (eng, self.count[eng])
            inc = (eng, 1)
        else:
            tok = (eng, self.count[eng] + 1)
            inc = None
        for b in reads:
            b.r.append(tok)
        for b in writes:
            b.w = tok
            b.r = []
        self.streams[eng].append((waits, fn, inc))
        return tok

    def dma(self, q, semkey, fn, reads=(), writes=(), extra=()):
        self._sem(semkey)
        waits = self._collect(q, reads, writes, extra)
        self.count[semkey] += 16
        tok = (semkey, self.count[semkey])
        for b in reads:
            b.r.append(tok)
        for b in writes:
            b.w = tok
            b.r = []
        self.streams[q].append((waits, fn, (semkey, 16)))
        return tok

    def barrier(self):
        toks = [(k, v) for k, v in self.count.items() if v > 0]
        for eng in self.ENG:
            waits = []
            wd = self.waited[eng]
            for sem, val in toks:
                if sem == eng:
                    continue
                if wd.get(sem, 0) < val:
                    wd[sem] = val
                    waits.append((sem, val))
            if waits:
                self.streams[eng].append((waits, None, None))

    def emit(self):
        nc = self.nc
        with nc.Block() as block:
            def mk(name):
                def run(e):
                    for waits, fn, inc in self.streams[name]:
                        for sem, val in waits:
                            e.wait_ge(self.sems[sem], val)
                        if fn is not None:
                            ins = fn(e)
                            if inc is not None:
                                ins.then_inc(self.sems[inc[0]], inc[1])
                return run
            block.tensor(mk("pe"))
            block.scalar(mk("act"))
            block.vector(mk("dve"))
            block.gpsimd(mk("pool"))
            block.sync(mk("sp"))
        for k in self.streams:
            self.streams[k] = []


def emit_norm(R, G, xt, xB, sq, sqB, ssbank, ssB, lnv, lnvB, rstd, rstdB, out, outB, gcol0):
    prm, ones, onesB = G["prm"], G["ones"], G["onesB"]
    R.op("act", lambda e: e.activation(out=sq[:, :, :], in_=xt[:, :, :], func=AF.Square),
         reads=[xB], writes=sqB)
    for kc in range(8):
        R.op("pe", lambda e, kc=kc: e.matmul(ssbank[:, :], ones[:, :], sq[:, kc, :],
                                             start=(kc == 0), stop=(kc == 7)),
             reads=sqB + [onesB], writes=[ssB], signal=(kc == 7))
    R.op("act", lambda e: e.activation(out=lnv[:, :], in_=ssbank[:, :], func=AF.Ln,
                                       bias=prm[:, OFF_EPS:OFF_EPS + 1], scale=1.0 / D),
         reads=[ssB], writes=[lnvB])
    R.op("act", lambda e: e.activation(out=rstd[:, :], in_=lnv[:, :], func=AF.Exp, scale=-0.5),
         reads=[lnvB], writes=[rstdB])
    for kc in range(8):
        R.op("dve", lambda e, kc=kc: e.scalar_tensor_tensor(
            out=out[:, kc, :], in0=xt[:, kc, :], scalar=prm[:, gcol0 + kc:gcol0 + kc + 1],
            in1=rstd[:, :], op0=ALU.mult, op1=ALU.mult),
            reads=[xB, rstdB], writes=[outB])


def x_view(ap2d):
    return ap2d.rearrange("(c p) n -> p c n", p=128)


def phase_A(R, nc, G, l, xsrc):
    prm = G["prm"]
    pb = l * PL
    win_d = G["w_in"][l].rearrange("(kc p) n -> p kc n", p=128)
    with ExitStack() as st:
        def sb(name, shape, dt):
            return st.enter_context(nc.sbuf_tensor(f"{name}_L{l}", shape, dt))

        win = sb("winA", [128, 8, INC], BF16)
        xb = [sb(f"xbA{i}", [128, 8, TB], F32) for i in range(2)]
        sq = sb("sqA", [128, 8, TB], BF16)
        hT = [sb(f"hTA{i}", [128, 8, TB], BF16) for i in range(2)]
        lnv = sb("lnvA", [128, TB], F32)
        rstd = sb("rstdA", [128, TB], F32)
        qst = sb("qstA", [128, 4, TB], BF16)
        kst = sb("kstA", [128, 4, TB], BF16)
        vst = sb("vstA", [128, 4, 516], BF16)
        mixst = sb("mixstA", [128, 4, TB], BF16)
        Pb = sb("PbA", [128, 2, 528], F32)
        Ub = sb("UbA", [128, 2, 528], F32)
        gcS = sb("gcSA", [128, 2, TB], F32)
        s2 = sb("s2A", [128, 2, 528], F32)
        s4 = sb("s4A", [128, 2, 528], F32)
        s8 = sb("s8A", [128, 528], F32)
        s16 = sb("s16A", [128, 528], F32)
        tmpf = sb("tmpfA", [128, 16], F32)
        pooled = sb("pooledA", [128, 2, TB], BF16)
        yv = sb("yvA", [128, 2, TB], F32)
        wblk = sb("wblkA", [128, 2, 128], BF16)
        banks = [st.enter_context(nc.psum_tensor(f"L{l}bkA{i}", [128, 512], F32)) for i in range(8)]
        bankB = [Buf() for _ in range(8)]

        winB = [Buf() for _ in range(5)]
        xB = [Buf(), Buf()]
        sqB = [Buf()]
        hB = [Buf(), Buf()]
        lnvB, rstdB, qstB, kstB, vstB, mixB = Buf(), Buf(), Buf(), Buf(), Buf(), Buf()
        PbB, UbB, gcB, s2B, s4B, s8B, s16B, tmpB, poolB, yvB, wblkB = (Buf() for _ in range(11))

        for cg in (0, 1, 3, 4, 2):
            R.dma("pool", f"win{cg}", lambda e, cg=cg: e.dma_start(
                out=win[:, :, cg * 512:(cg + 1) * 512], in_=win_d[:, :, cg * 512:(cg + 1) * 512]),
                writes=[winB[cg]])
        R.op("dve", lambda e: e.memset(wblk[:, :, :], 0.0), writes=[wblkB])
        for g in range(4):
            r0 = (g % 2) * 64
            R.dma("pool", "wblk", lambda e, g=g, r0=r0: e.dma_start(
                out=wblk[r0:r0 + 64, g // 2, r0:r0 + 64], in_=G["w_pool"][l, g, :, :]),
                writes=[wblkB])
        R.op("dve", lambda e: e.memset(
            vst[:, :, :].rearrange("p t (h d) -> p t h d", h=4)[:, :, :, 128:129], 1.0), writes=[vstB])
        R.op("dve", lambda e: e.memset(Pb[:, :, 0:16], 0.0), writes=[PbB])
        R.op("dve", lambda e: e.memset(Ub[:, :, 0:16], 0.0), writes=[UbB])

        xv = x_view(xsrc)

        def load(tb):
            i = tb % 2
            R.dma("sp", f"xA{i}", lambda e: e.dma_start(out=xb[i][:, :, :], in_=xv[:, :, tb * TB:(tb + 1) * TB]),
                  writes=[xB[i]])

        def norm(tb):
            i = tb % 2
            emit_norm(R, G, xb[i], xB[i], sq, sqB, banks[7], bankB[7], lnv, lnvB, rstd, rstdB,
                      hT[i], hB[i], pb + OFF_GM)

        rr = [0]

        def nb():
            i = rr[0] % 5
            rr[0] += 1
            return banks[i], bankB[i]

        def mm_chain(bank, bB, h, hb, col, tsub=None):
            for kc in range(8):
                if tsub is None:
                    fn = lambda e, kc=kc: e.matmul(bank[:, :], win[:, kc, col:col + 128], h[:, kc, :],
                                                   start=(kc == 0), stop=(kc == 7))
                else:
                    fn = lambda e, kc=kc: e.matmul(bank[:, :], h[:, kc, tsub * 128:(tsub + 1) * 128],
                                                   win[:, kc, 1024:1536], start=(kc == 0), stop=(kc == 7))
                R.op("pe", fn, reads=[winB[2 if tsub is not None else col // 512], hb], writes=[bB],
                     signal=(kc == 7))

        load(0)
        load(1)
        norm(0)
        for tb in range(NTB):
            i = tb % 2
            h, hb = hT[i], hB[i]
            tsl = slice(tb * TB, (tb + 1) * TB)
            for hh in range(4):
                bank, bB = nb()
                mm_chain(bank, bB, h, hb, hh * 128)
                R.op("act", lambda e, bank=bank, hh=hh: e.mul(out=qst[:, hh, :], in_=bank[:, :], mul=0.125),
                     reads=[bB], writes=[qstB])
            for hh in range(4):
                bank, bB = nb()
                mm_chain(bank, bB, h, hb, 512 + hh * 128)
                R.op("act", lambda e, bank=bank, hh=hh: e.copy(out=kst[:, hh, :], in_=bank[:, :]),
                     reads=[bB], writes=[kstB])
            R.dma("pool", "qstore", lambda e, tsl=tsl: e.dma_start(
                out=G["qT"].rearrange("h p n -> p h n")[:, :, tsl], in_=qst[:, :, :]), reads=[qstB])
            R.dma("pool", "kstore", lambda e, tsl=tsl: e.dma_start(
                out=G["kT"].rearrange("h p n -> p h n")[:, :, tsl], in_=kst[:, :, :]), reads=[kstB])
            if tb + 1 < NTB:
                norm(tb + 1)
            for c in range(2):
                bank, bB = nb()
                mm_chain(bank, bB, h, hb, 1536 + c * 128)
                R.op("act", lambda e, bank=bank, c=c: e.copy(out=Pb[:, c, 16:528], in_=bank[:, :]),
                     reads=[bB], writes=[PbB])
            for c in range(2):
                bank, bB = nb()
                mm_chain(bank, bB, h, hb, 2048 + c * 128)
                R.op("act", lambda e, bank=bank, c=c: e.copy(out=gcS[:, c, :], in_=bank[:, :]),
                     reads=[bB], writes=[gcB])
            for c in range(2):
                bank, bB = nb()
                mm_chain(bank, bB, h, hb, 2304 + c * 128)
                R.op("dve", lambda e, bank=bank, c=c: e.tensor_tensor(
                    out=Ub[:, c, 16:528], in0=bank[:, :], in1=gcS[:, c, :], op=ALU.mult),
                    reads=[bB, gcB], writes=[UbB])
            gbb = []
            for c in range(2):
                bank, bB = banks[5 + c], bankB[5 + c]
                mm_chain(bank, bB, h, hb, 1792 + c * 128)
                gbb.append((bank, bB))
            for ts in range(4):
                bank, bB = nb()
                mm_chain(bank, bB, h, hb, 0, tsub=ts)
                R.op("act", lambda e, bank=bank, ts=ts: e.copy(
                    out=vst[:, ts, :].rearrange("p (h d) -> p h d", h=4)[:, :, 0:128],
                    in_=bank[:, :].rearrange("p (h d) -> p h d", h=4)),
                    reads=[bB], writes=[vstB])
            R.dma("pool", "vstore", lambda e, tb=tb: e.dma_start(
                out=G["vS"][:, tb * 4:(tb + 1) * 4, :], in_=vst[:, :, :]), reads=[vstB])
            for c in range(2):
                cw = lambda k, c=c: prm[:, pb + OFF_CW + k * 2 + c:pb + OFF_CW + k * 2 + c + 1]
                R.op("dve", lambda e, c=c, cw=cw: e.tensor_scalar(
                    out=yv[:, c, :], in0=Ub[:, c, 14:526], scalar1=cw(0), scalar2=None, op0=ALU.mult),
                    reads=[UbB], writes=[yvB])
                R.op("dve", lambda e, c=c, cw=cw: e.scalar_tensor_tensor(
                    out=yv[:, c, :], in0=Ub[:, c, 15:527], scalar=cw(1), in1=yv[:, c, :],
                    op0=ALU.mult, op1=ALU.add), reads=[UbB, yvB], writes=[yvB])
                R.op("dve", lambda e, c=c, cw=cw: e.scalar_tensor_tensor(
                    out=yv[:, c, :], in0=Ub[:, c, 16:528], scalar=cw(2), in1=yv[:, c, :],
                    op0=ALU.mult, op1=ALU.add), reads=[UbB, yvB], writes=[yvB])
                bank, bB = gbb[c]
                R.op("dve", lambda e, c=c, bank=bank: e.tensor_tensor(
                    out=mixst[:, 2 + c, :], in0=bank[:, :], in1=yv[:, c, :], op=ALU.mult),
                    reads=[bB, yvB], writes=[mixB])
            R.op("dve", lambda e: e.tensor_copy(out=Ub[:, :, 14:16], in_=Ub[:, :, 526:528]),
                 reads=[UbB], writes=[UbB])
            R.op("dve", lambda e: e.tensor_tensor(out=s2[:, :, 1:528], in0=Pb[:, :, 1:528], in1=Pb[:, :, 0:527],
                                                  op=ALU.add), reads=[PbB], writes=[s2B])
            R.op("dve", lambda e: e.tensor_tensor(out=s4[:, :, 3:528], in0=s2[:, :, 3:528], in1=s2[:, :, 1:526],
                                                  op=ALU.add), reads=[s2B], writes=[s4B])
            R.op("dve", lambda e: e.tensor_tensor(out=s8[:, 7:528], in0=s4[:, 1, 7:528], in1=s4[:, 1, 3:524],
                                                  op=ALU.add), reads=[s4B], writes=[s8B])
            R.op("dve", lambda e: e.tensor_tensor(out=s16[64:128, 15:528], in0=s8[64:128, 15:528],
                                                  in1=s8[64:128, 7:520], op=ALU.add), reads=[s8B], writes=[s16B])
            grp = [(s2, lambda r: s2[r, 0, 16:528], 0, 0, 2.0, s2B),
                   (s4, lambda r: s4[r, 0, 16:528], 64, 0, 4.0, s4B),
                   (s8, lambda r: s8[r, 16:528], 0, 1, 8.0, s8B),
                   (s16, lambda r: s16[r, 16:528], 64, 1, 16.0, s16B)]
            for (_, src, r0, c, w, sB) in grp:
                rs = slice(r0, r0 + 64)
                R.op("dve", lambda e, src=src, rs=rs, c=c, w=w: e.scalar_tensor_tensor(
                    out=pooled[rs, c, :], in0=src(rs), scalar=1.0 / w, in1=Pb[rs, c, 16:528],
                    op0=ALU.mult, op1=ALU.subtract), reads=[sB, PbB], writes=[poolB])
            if tb == 0:
                for (_, src, r0, c, w, sB) in grp:
                    rs = slice(r0, r0 + 64)
                    R.op("dve", lambda e, src=src, rs=rs, c=c: e.tensor_tensor(
                        out=tmpf[rs, :], in0=src(rs)[:, 0:16],
                        in1=prm[rs, OFF_ID0 + c * 16:OFF_ID0 + c * 16 + 16], op=ALU.mult),
                        reads=[sB], writes=[tmpB])
                    R.op("dve", lambda e, rs=rs, c=c: e.tensor_tensor(
                        out=pooled[rs, c, 0:16], in0=tmpf[rs, :], in1=Pb[rs, c, 16:32], op=ALU.subtract),
                        reads=[tmpB, PbB], writes=[poolB])
            R.op("dve", lambda e: e.tensor_copy(out=Pb[:, :, 0:16], in_=Pb[:, :, 512:528]),
                 reads=[PbB], writes=[PbB])
            for c in range(2):
                bank, bB = nb()
                R.op("pe", lambda e, bank=bank, c=c: e.matmul(bank[:, :], wblk[:, c, :], pooled[:, c, :],
                                                              start=True, stop=True),
                     reads=[wblkB, poolB], writes=[bB])
                R.op("dve", lambda e, bank=bank, c=c: e.tensor_scalar(
                    out=mixst[:, c, :], in0=bank[:, :], scalar1=prm[:, pb + OFF_PS + c:pb + OFF_PS + c + 1],
                    scalar2=None, op0=ALU.mult), reads=[bB], writes=[mixB])
            R.dma("pool", "mixstore", lambda e, tsl=tsl: e.dma_start(
                out=x_view(G["mixT"])[:, 4:8, tsl], in_=mixst[:, :, :]), reads=[mixB])
            if tb + 2 < NTB:
                load(tb + 2)
        R.barrier()
        R.emit()


def phase_B(R, nc, G, l, xsrc, xdst):
    prm = G["prm"]
    pb = l * PL
    lam_init = 0.8 - 0.6 * math.exp(-0.3 * l)
    wo_d = G["w_o"][l].rearrange("(kc p) n -> p kc n", p=128)
    with ExitStack() as st:
        def sb(name, shape, dt):
            return st.enter_context(nc.sbuf_tensor(f"{name}_L{l}", shape, dt))

        KT = sb("KTB", [128, 4, S], BF16)
        V = sb("VB", [128, 32, 516], BF16)
        wo = sb("woB", [128, 8, D], BF16)
        tbl = sb("tblB", [128, 4, TW], F32)
        tblh = sb("tblhB", [128, 4, TW], BF16)
        tbll = sb("tbllB", [128, 4, TW], BF16)
        tblhB, tbllB = Buf(), Buf()
        Qb = [sb(f"QbB{i}", [128, 4, TB], BF16) for i in range(2)]
        PT = [sb(f"PTB{i}", [128, 2, TB], BF16) for i in range(3)]
        mixb = [sb(f"mixbB{i}", [128, 8, TB], BF16) for i in range(2)]
        xb = [sb(f"xbB{i}", [128, 8, TB], F32) for i in range(2)]
        accS = sb("accSB", [128, 8, 129], F32)
        rl = sb("rlB", [128, 2, 4], F32)
        r2n = sb("r2nB", [128, 4], F32)
        tt = sb("ttB", [128, 128], F32)
        attf = sb("attfB", [128, 4, 128], F32)
        junk = sb("junkB", [128, 128], F32)
        ss = sb("ssB", [128, 4], F32)
        vv = sb("vvB", [128, 4], F32)
        nhalf = sb("nhalfB", [128, 4], F32)
        rstd = sb("rstdB", [128, 4], F32)
        attn = sb("attnB", [128, 4, 128], BF16)
        ident = sb("identB", [128, 128], BF16)
        Gp = sb("GpB", [128, 128], F32)
        lamt = sb("lamtB", [128, 8], F32)
        lprod = sb("lprodB", [128, 64], F32)
        Sp = [st.enter_context(nc.psum_tensor(f"L{l}SpB{i}", [128, 2, 512], F32)) for i in range(2)]
        accb = [st.enter_context(nc.psum_tensor(f"L{l}accB{i}", [128, 3, 129], F32)) for i in range(3)]
        trb = st.enter_context(nc.psum_tensor(f"L{l}trB", [128, 512], BF16))

        SB_ = [Buf(), Buf()]
        WOB = [[Buf(), Buf()], [Buf(), Buf()]]
        PTB = [Buf(), Buf(), Buf()]
        accB = [Buf() for _ in range(8)]
        trB = Buf()
        KTB = [Buf() for _ in range(NTB)]
        VBf = [Buf() for _ in range(NTB)]
        QB = [Buf(), Buf()]
        mixatt = [[Buf() for _ in range(4)] for _ in range(2)]
        mixpc = [Buf(), Buf()]
        xB = [[Buf() for _ in range(8)] for _ in range(2)]
        woB, tblB, identB, GpB, lamB, lprodB, nhB = (Buf() for _ in range(7))
        accSB, rlB, r2nB, ttB, attfB, junkB, ssB_, vvB, rstdB, attnB = (Buf() for _ in range(10))

        def acc(idx):
            return accb[idx // 3][:, idx % 3, :]

        R.dma("sp", "tblB", lambda e: e.dma_start(out=tbl[:, :, :], in_=G["tbl"][:, :, :]), writes=[tblB])
        for hh in range(NH):
            R.op("dve", lambda e, hh=hh: e.tensor_scalar(
                out=tbl[:, hh, :], in0=tbl[:, hh, :], scalar1=prm[:, OFF_CH + hh:OFF_CH + hh + 1],
                scalar2=None, op0=ALU.subtract), reads=[tblB], writes=[tblB])
        R.op("dve", lambda e: e.tensor_copy(out=tblh[:, :, :], in_=tbl[:, :, :]), reads=[tblB], writes=[tblhB])
        R.op("dve", lambda e: e.tensor_tensor(out=tbl[:, :, :], in0=tbl[:, :, :], in1=tblh[:, :, :],
                                              op=ALU.subtract), reads=[tblB, tblhB], writes=[tblB])
        R.op("dve", lambda e: e.tensor_copy(out=tbll[:, :, :], in_=tbl[:, :, :]), reads=[tblB], writes=[tbllB])
        R.op("dve", lambda e: e.memset(nhalf[:, :], -0.5), writes=[nhB])
        kT_v = G["kT"].rearrange("h p n -> p h n")

        def kv_load(tb):
            R.dma("sp", f"kld{tb}", lambda e: e.dma_start(
                out=KT[:, :, tb * TB:(tb + 1) * TB], in_=kT_v[:, :, tb * TB:(tb + 1) * TB]), writes=[KTB[tb]])
            R.dma("sp", f"vld{tb}", lambda e: e.dma_start(
                out=V[:, tb * 4:(tb + 1) * 4, :], in_=G["vS"][:, tb * 4:(tb + 1) * 4, :]), writes=[VBf[tb]])
        R.dma("pool", "identB", lambda e: e.dma_start(out=ident[:, :], in_=G["ident"][:, :]), writes=[identB])
        R.dma("pool", "woB", lambda e: e.dma_start(out=wo[:, :, :], in_=wo_d[:, :, :]), writes=[woB])
        lo = pb + OFF_LAM
        for j in range(2):
            R.op("dve", lambda e, j=j: e.tensor_tensor(
                out=lprod[:, :], in0=prm[:, lo + j * 128:lo + j * 128 + 64],
                in1=prm[:, lo + j * 128 + 64:lo + j * 128 + 128], op=ALU.mult), writes=[lprodB])
            R.op("dve", lambda e, j=j: e.reduce_sum(out=lamt[:, j:j + 1], in_=lprod[:, :],
                                                   axis=mybir.AxisListType.X),
                 reads=[lprodB], writes=[lamB])
        R.op("act", lambda e: e.activation(out=lamt[:, 2:4], in_=lamt[:, 0:2], func=AF.Exp),
             reads=[lamB], writes=[lamB])
        R.op("dve", lambda e: e.tensor_tensor(out=lamt[:, 4:5], in0=lamt[:, 2:3], in1=lamt[:, 3:4],
                                              op=ALU.subtract), reads=[lamB], writes=[lamB])
        R.op("dve", lambda e: e.tensor_scalar(out=lamt[:, 5:6], in0=lamt[:, 4:5], scalar1=lam_init,
                                              scalar2=-1.0, op0=ALU.add, op1=ALU.mult),
             reads=[lamB], writes=[lamB])
        R.op("dve", lambda e: e.tensor_scalar(out=Gp[:, :], in0=prm[:, pb + OFF_SG:pb + OFF_SG + 128],
                                              scalar1=1.0 - lam_init, scalar2=None, op0=ALU.mult),
             writes=[GpB])
        nlam = lamt[:, 5:6]

        xv = x_view(xsrc)
        xo = x_view(xdst)
        qT_v = G["qT"].rearrange("h p n -> p h n")
        mix_v = x_view(G["mixT"])

        def loads(qb):
            i = qb % 2
            tsl = slice(qb * TB, (qb + 1) * TB)
            R.dma("sp", f"qld{i}", lambda e: e.dma_start(out=Qb[i][:, :, :], in_=qT_v[:, :, tsl]), writes=[QB[i]])
            R.dma("sp", f"mld{i}", lambda e: e.dma_start(out=mixb[i][:, 4:8, :], in_=mix_v[:, 4:8, tsl]),
                  writes=[mixpc[i]])
            R.dma("sp", f"xld{i}", lambda e: e.dma_start(out=xb[i][:, :, :], in_=xv[:, :, tsl]), writes=xB[i])

        pending = []

        def flush_pending():
            while pending:
                pending.pop(0)()

        def attn_head(qb, h):
            i = qb % 2
            nk = 4 * (qb + 1)

            def qk(kt):
                j = kt - 4 * qb
                qlo = max(0, 128 * j)
                p = kt % 2
                near = (j >= -1)
                for m in range(2):
                    R.op("pe", lambda e, m=m, p=p, kt=kt, qlo=qlo, near=near: e.matmul(
                        Sp[p][:, m, qlo:512], KT[m * 64:(m + 1) * 64, h, kt * 128:(kt + 1) * 128],
                        Qb[i][m * 64:(m + 1) * 64, h, qlo:512], start=True, stop=(not near)),
                        reads=[KTB[kt // 4], QB[i]], writes=[SB_[p], WOB[p][0], WOB[p][1]],
                        signal=(m == 1 and not near))
                if near:
                    mlo = 128 if j < 0 else 0
                    mhi = min(256, mlo + 512 - qlo)
                    c0, c1 = mlo + 128 * j, mhi + 128 * j
                    for m in range(2):
                        for part, tb_ in enumerate((tblh, tbll)):
                            R.op("pe", lambda e, m=m, p=p, c0=c0, c1=c1, mlo=mlo, mhi=mhi, tb_=tb_, part=part: e.matmul(
                                Sp[p][:, m, c0:c1], ident[:, :], tb_[:, h, mlo:mhi], start=False,
                                stop=(part == 1)),
                                reads=[identB, tblhB, tbllB], writes=[SB_[p]], signal=(m == 1 and part == 1))

            def soft(kt):
                j = kt - 4 * qb
                qlo = max(0, 128 * j)
                p = kt % 2
                p3 = kt % 3
                R.op("act", lambda e, p=p, p3=p3, qlo=qlo: e.activation(
                    out=PT[p3][:, :, qlo:512], in_=Sp[p][:, :, qlo:512], func=AF.Exp),
                    reads=[SB_[p]], writes=[PTB[p3]])

            def pv(kt):
                j = kt - 4 * qb
                p = kt % 3
                s0 = max(j, 0)
                seen = set()
                for m in range(2):
                    for sub in range(s0, 4):
                        idx = sub * 2 + m
                        last = (kt == 4 * qb + sub)
                        bnk = idx // 3
                        st_ = (kt == 0 and bnk not in seen)
                        seen.add(bnk)
                        R.op("pe", lambda e, m=m, p=p, sub=sub, kt=kt, last=last, st_=st_, idx=idx: e.matmul(
                            acc(idx), PT[p][:, m, sub * 128:(sub + 1) * 128], V[:, kt, h * 129:(h + 1) * 129],
                            start=st_, stop=last, skip_group_check=True),
                            reads=[PTB[p], VBf[kt // 4]], writes=[accB[idx]],
                            signal=(last or (m == 1 and sub == 3)))

            defer_at = min(nk - 1, 7)
            qk(0)
            qk(1)
            for kt in range(nk):
                soft(kt)
                if kt + 2 < nk:
                    qk(kt + 2)
                pv(kt)
                if kt == defer_at:
                    flush_pending()
            for bnk in range(3):
                n = 3 if bnk < 2 else 2
                R.op("dve", lambda e, bnk=bnk, n=n: e.tensor_copy(
                    out=accS[:, 3 * bnk:3 * bnk + n, :], in_=accb[bnk][:, 0:n, :]),
                    reads=accB[3 * bnk:3 * bnk + n], writes=[accSB])
            accS4 = accS[:, :, :].rearrange("p (s m) c -> p s m c", m=2)
            for m in range(2):
                R.op("dve", lambda e, m=m: e.reciprocal(out=rl[:, m, :], in_=accS4[:, :, m, 128]),
                     reads=[accSB], writes=[rlB])
            R.op("dve", lambda e: e.tensor_scalar(out=r2n[:, :], in0=rl[:, 1, :], scalar1=nlam,
                                                  scalar2=None, op0=ALU.mult),
                 reads=[rlB, lamB], writes=[r2nB])
            for sub in range(4):
                R.op("dve", lambda e, sub=sub: e.tensor_scalar(
                    out=tt[:, :], in0=accS[:, 2 * sub, 0:128], scalar1=rl[:, 0, sub:sub + 1], scalar2=None,
                    op0=ALU.mult), reads=[accSB, rlB], writes=[ttB])
                R.op("dve", lambda e, sub=sub: e.scalar_tensor_tensor(
                    out=attf[:, sub, :], in0=accS[:, 2 * sub + 1, 0:128], scalar=r2n[:, sub:sub + 1],
                    in1=tt[:, :], op0=ALU.mult, op1=ALU.add),
                    reads=[accSB, r2nB, ttB], writes=[attfB])
                R.op("dve", lambda e, sub=sub: e.scalar_tensor_tensor(
                    out=junk[:, :], in0=attf[:, sub, :], scalar=1.0, in1=attf[:, sub, :],
                    op0=ALU.mult, op1=ALU.mult, accum_out=ss[:, sub:sub + 1]),
                    reads=[attfB], writes=[junkB, ssB_])
            R.op("dve", lambda e: e.tensor_scalar(out=vv[:, :], in0=ss[:, :], scalar1=1.0 / 128,
                                                  scalar2=SUBLN_EPS, op0=ALU.mult, op1=ALU.add),
                 reads=[ssB_], writes=[vvB])
            R.op("pool", lambda e: e.tensor_tensor(out=rstd[:, :], in0=vv[:, :], in1=nhalf[:, :], op=ALU.pow),
                 reads=[vvB, nhB], writes=[rstdB])
            for sub in range(4):
                R.op("dve", lambda e, sub=sub: e.scalar_tensor_tensor(
                    out=attn[:, sub, :], in0=attf[:, sub, :], scalar=rstd[:, sub:sub + 1], in1=Gp[:, :],
                    op0=ALU.mult, op1=ALU.mult), reads=[attfB, rstdB, GpB], writes=[attnB])

            def stage2():
                for sub in range(4):
                    R.op("pe", lambda e, sub=sub: e.transpose(trb[:, sub * 128:(sub + 1) * 128], attn[:, sub, :],
                                                              ident[:, :]),
                         reads=[attnB, identB], writes=[trB], signal=(sub == 3))
                R.op("dve", lambda e: e.tensor_copy(out=mixb[i][:, h, :], in_=trb[:, :]),
                     reads=[trB], writes=[mixatt[i][h]])
            pending.append(stage2)

        def wo_block(qb):
            i = qb % 2
            tsl = slice(qb * TB, (qb + 1) * TB)
            for half in range(2):
                ocs = range(4 * half, 4 * half + 4)
                for oc in ocs:
                    p, m = (oc // 2) % 2, oc % 2
                    for kc in (4, 5, 6, 7, 0, 1, 2):
                        rd = [woB, mixatt[i][kc]] if kc < 4 else [woB, mixpc[i]]
                        R.op("pe", lambda e, p=p, m=m, kc=kc, oc=oc: e.matmul(
                            Sp[p][:, m, :], wo[:, kc, oc * 128:(oc + 1) * 128], mixb[i][:, kc, :],
                            start=(kc == 4), stop=False), reads=rd, writes=[WOB[p][m], SB_[p]], signal=False)
                for oc in ocs:
                    p, m = (oc // 2) % 2, oc % 2
                    R.op("pe", lambda e, p=p, m=m, oc=oc: e.matmul(
                        Sp[p][:, m, :], wo[:, 3, oc * 128:(oc + 1) * 128], mixb[i][:, 3, :],
                        start=False, stop=True), reads=[woB, mixatt[i][3]], writes=[WOB[p][m], SB_[p]])
                    R.op("dve", lambda e, p=p, m=m, oc=oc: e.tensor_tensor(
                        out=xb[i][:, oc, :], in0=Sp[p][:, m, :], in1=xb[i][:, oc, :], op=ALU.add),
                        reads=[WOB[p][m], xB[i][oc]], writes=[xB[i][oc]])
            R.dma("pool", f"xst{i}", lambda e: e.dma_start(out=xo[:, :, tsl], in_=xb[i][:, :, :]),
                  reads=xB[i])

        loads(0)
        kv_load(0)
        for qb in range(NTB):
            if qb + 1 < NTB:
                loads(qb + 1)
                kv_load(qb + 1)
            for h in range(NH):
                attn_head(qb, h)
            flush_pending()
            wo_block(qb)
        R.barrier()
        R.emit()


def phase_C(R, nc, G, l, xsrc, xdst, final):
    prm = G["prm"]
    pb = l * PL
    wg_d = G["w_gate"][l].rearrange("(kc p) n -> p kc n", p=128)
    wu_d = G["w_up"][l].rearrange("(kc p) n -> p kc n", p=128)
    wd_d = G["w_down"][l].rearrange("(f p) n -> p f n", p=128)
    with ExitStack() as st:
        def sb(name, shape, dt):
            return st.enter_context(nc.sbuf_tensor(f"{name}_L{l}", shape, dt))

        wg = sb("wgC", [128, 8, DFF], BF16)
        wu = sb("wuC", [128, 8, DFF], BF16)
        wd = sb("wdC", [128, NF, D], BF16)
        xb = [sb(f"xbC{i}", [128, 8, TB], F32) for i in range(2)]
        hT = sb("hTC", [128, 8, TB], BF16)
        aT = sb("aTC", [128, NF, TB], BF16)
        rstd = sb("rstdC", [128, TB], F32)
        sg = sb("sgC", [128, TB], F32)
        sqh = sb("sqhC", [128, 4, TB], BF16)
        banks = [st.enter_context(nc.psum_tensor(f"L{l}bkC{i}", [128, 512], F32)) for i in range(8)]
        bankB = [Buf() for _ in range(8)]
        wgB = [Buf() for _ in range(NF // 2)]
        wuB = [Buf() for _ in range(NF // 2)]
        wdB = [Buf() for _ in range(NF)]
        xB = [[Buf() for _ in range(8)] for _ in range(2)]
        hB, rstdB, sgB, sqhB = Buf(), Buf(), Buf(), Buf()
        aB = [Buf() for _ in range(NF)]

        for g2 in range(NF // 2):
            csl = slice(g2 * 256, (g2 + 1) * 256)
            R.dma("pool", f"wg{g2}", lambda e, csl=csl: e.dma_start(out=wg[:, :, csl], in_=wg_d[:, :, csl]),
                  writes=[wgB[g2]])
            R.dma("pool", f"wu{g2}", lambda e, csl=csl: e.dma_start(out=wu[:, :, csl], in_=wu_d[:, :, csl]),
                  writes=[wuB[g2]])
        for g2 in range(NF // 2):
            R.dma("pool", f"wd{g2}", lambda e, g2=g2: e.dma_start(out=wd[:, 2 * g2:2 * g2 + 2, :],
                                                                  in_=wd_d[:, 2 * g2:2 * g2 + 2, :]),
                  writes=[wdB[2 * g2], wdB[2 * g2 + 1]])

        xv = x_view(xsrc)
        xo = x_view(xdst)
        prm_g = pb + OFF_GF
        ones, onesB = G["ones"], G["onesB"]

        def load(tb):
            i = tb % 2
            R.dma("sp", f"xC{i}", lambda e: e.dma_start(out=xb[i][:, :, :], in_=xv[:, :, tb * TB:(tb + 1) * TB]),
                  writes=xB[i])

        def norm_steps(i):
            def sq(half):
                R.op("act", lambda e: e.activation(
                    out=sqh[:, :, :], in_=xb[i][:, 4 * half:4 * half + 4, :], func=AF.Square),
                    reads=xB[i][4 * half:4 * half + 4], writes=[sqhB])

            def mm(half):
                for k4 in range(4):
                    kc = 4 * half + k4
                    R.op("pe", lambda e, k4=k4, kc=kc: e.matmul(banks[7][:, :], ones[:, :], sqh[:, k4, :],
                                                                start=(kc == 0), stop=(kc == 7)),
                         reads=[sqhB, onesB], writes=[bankB[7]], signal=(k4 == 3))

            def lnexp():
                R.op("act", lambda e: e.activation(out=rstd[:, :], in_=banks[7][:, :], func=AF.Ln,
                                                   bias=prm[:, OFF_EPS:OFF_EPS + 1], scale=1.0 / D),
                     reads=[bankB[7]], writes=[rstdB])
                R.op("act", lambda e: e.activation(out=rstd[:, :], in_=rstd[:, :], func=AF.Exp, scale=-0.5),
                     reads=[rstdB], writes=[rstdB])
            return [lambda: sq(0), lambda: (mm(0), sq(1)), lambda: (mm(1), lnexp())]

        def norm_rstd(i):
            for st_ in norm_steps(i):
                st_()

        def norm_apply(i, dst, dstB_of, gcol0):
            for kc in range(8):
                R.op("dve", lambda e, kc=kc: e.scalar_tensor_tensor(
                    out=dst[:, kc, :], in0=xb[i][:, kc, :], scalar=prm[:, gcol0 + kc:gcol0 + kc + 1],
                    in1=rstd[:, :], op0=ALU.mult, op1=ALU.mult),
                    reads=[xB[i][kc], rstdB], writes=[dstB_of(kc)])

        pending = []
        nxt = []

        def finish_steps(tb):
            i = tb % 2
            tsl = slice(tb * TB, (tb + 1) * TB)

            def store():
                R.dma("pool", f"xstC{i}", lambda e: e.dma_start(out=xo[:, :, tsl], in_=xb[i][:, :, :]),
                      reads=xB[i])
            if not final:
                return [store]
            ns = norm_steps(i)
            return [ns[0], ns[1], lambda: (ns[2](), norm_apply(i, xb[i], lambda kc: xB[i][kc], OFF_GFIN), store())]

        def block(tb):
            i = tb % 2
            sched = False
            for f in range(NF):
                if f >= 1:
                    if pending:
                        pending.pop(0)()
                    if not pending and not sched and tb + 1 < NTB:
                        load(tb + 1)
                        nxt.extend(norm_steps((tb + 1) % 2))
                        sched = True
                    elif nxt and f >= 6:
                        nxt.pop(0)()
                bg, bgB = banks[(2 * f) % 6], bankB[(2 * f) % 6]
                bu, buB = banks[(2 * f + 1) % 6], bankB[(2 * f + 1) % 6]
                for kc in range(8):
                    R.op("pe", lambda e, kc=kc, f=f, bg=bg: e.matmul(
                        bg[:, :], wg[:, kc, f * 128:(f + 1) * 128], hT[:, kc, :],
                        start=(kc == 0), stop=(kc == 7)), reads=[wgB[f // 2], hB], writes=[bgB], signal=(kc == 7))
                for kc in range(8):
                    R.op("pe", lambda e, kc=kc, f=f, bu=bu: e.matmul(
                        bu[:, :], wu[:, kc, f * 128:(f + 1) * 128], hT[:, kc, :],
                        start=(kc == 0), stop=(kc == 7)), reads=[wuB[f // 2], hB], writes=[buB], signal=(kc == 7))
                R.op("act", lambda e, f=f, bg=bg: e.activation(out=sg[:, :], in_=bg[:, :], func=AF.Silu),
                     reads=[bgB], writes=[sgB])
                R.op("dve", lambda e, f=f, bu=bu: e.tensor_tensor(
                    out=aT[:, f, :], in0=bu[:, :], in1=sg[:, :], op=ALU.mult),
                    reads=[buB, sgB], writes=[aB[f]])
            while nxt:
                nxt.pop(0)()
            if tb + 1 < NTB:
                norm_apply((tb + 1) % 2, hT, lambda kc: hB, prm_g)
            for oc in range(8):
                bk, bkB = banks[oc % 6], bankB[oc % 6]
                for f in range(NF):
                    R.op("pe", lambda e, f=f, oc=oc, bk=bk: e.matmul(
                        bk[:, :], wd[:, f, oc * 128:(oc + 1) * 128], aT[:, f, :],
                        start=(f == 0), stop=(f == NF - 1)), reads=[wdB[f], aB[f]], writes=[bkB],
                        signal=(f == NF - 1))
                R.op("dve", lambda e, oc=oc, bk=bk: e.tensor_tensor(
                    out=xb[i][:, oc, :], in0=bk[:, :], in1=xb[i][:, oc, :], op=ALU.add),
                    reads=[bkB, xB[i][oc]], writes=[xB[i][oc]])
            pending.extend(finish_steps(tb))

        load(0)
        norm_rstd(0)
        norm_apply(0, hT, lambda kc: hB, prm_g)
        for tb in range(NTB):
            block(tb)
        while pending:
            pending.pop(0)()
        R.barrier()
        R.emit()


def build_program(stop_after=None, debug=False):
    nc = bass.Bass("TRN2", target_bir_lowering=False)
    dk = "ExternalOutput" if debug else "Internal"
    G = {}
    xT = nc.dram_tensor("xT", [D, S], F32, kind="ExternalInput").ap()
    G["w_in"] = nc.dram_tensor("w_in", [DEPTH, D, INC], F32, kind="ExternalInput").ap()
    G["w_o"] = nc.dram_tensor("w_o", [DEPTH, D, D], F32, kind="ExternalInput").ap()
    G["w_gate"] = nc.dram_tensor("w_gate", [DEPTH, D, DFF], F32, kind="ExternalInput").ap()
    G["w_up"] = nc.dram_tensor("w_up", [DEPTH, D, DFF], F32, kind="ExternalInput").ap()
    G["w_down"] = nc.dram_tensor("w_down", [DEPTH, DFF, D], F32, kind="ExternalInput").ap()
    G["w_pool"] = nc.dram_tensor("w_pool", [DEPTH, 4, 64, 64], F32, kind="ExternalInput").ap()
    prm_d = nc.dram_tensor("prm", [128, NP], F32, kind="ExternalInput").ap()
    G["tbl"] = nc.dram_tensor("tbl", [128, 4, TW], F32, kind="ExternalInput").ap()
    G["ident"] = nc.dram_tensor("ident", [128, 128], F32, kind="ExternalInput").ap()
    yT = nc.dram_tensor("yT", [D, S], F32, kind="ExternalOutput").ap()
    xs = nc.dram_tensor("xs", [D, S], F32, kind=dk).ap()
    G["qT"] = nc.dram_tensor("qT", [4, 128, S], BF16, kind=dk).ap()
    G["kT"] = nc.dram_tensor("kT", [4, 128, S], BF16, kind=dk).ap()
    G["vS"] = nc.dram_tensor("vS", [128, 32, 516], BF16, kind=dk).ap()
    G["mixT"] = nc.dram_tensor("mixT", [D, S], BF16, kind=dk).ap()

    with ExitStack() as es:
        R = Rec(nc, es)
        prm = es.enter_context(nc.sbuf_tensor("prm_sb", [128, NP], F32))
        ones = es.enter_context(nc.sbuf_tensor("ones_sb", [128, 128], BF16))
        G["prm"], G["ones"], G["onesB"] = prm, ones, Buf()
        prmB = Buf()
        R.dma("sp", "prm", lambda e: e.dma_start(out=prm[:, :], in_=prm_d[:, :]), writes=[prmB])
        R.op("dve", lambda e: e.memset(ones[:, :], 1.0), writes=[G["onesB"]])
        R.barrier()
        done = False
        for l in range(DEPTH):
            xin = xT if l == 0 else xs
            phase_A(R, nc, G, l, xin)
            if stop_after == (l, "A"):
                done = True
                break
            phase_B(R, nc, G, l, xin, xs)
            if stop_after == (l, "B"):
                done = True
                break
            last = (l == DEPTH - 1)
            phase_C(R, nc, G, l, xs, yT if last else xs, last)
            if stop_after == (l, "C"):
                done = True
                break
    return nc


def _rel_bucket_np(d):
    n = np.maximum(d, 0)
    nf = np.maximum(n, 1).astype(np.float32)
    large = 16 + (np.log(nf / np.float32(16)) / np.float32(math.log(128 / 16)) * np.float32(16)).astype(np.int32)
    large = np.minimum(large, 31)
    return np.where(n < 16, n, large)


def host_prep(inputs):
    f32 = np.float32
    g = {k: np.asarray(v, dtype=f32) for k, v in inputs.items()}
    prm = np.zeros((128, NP), f32)
    for l in range(DEPTH):
        pb = l * PL
        prm[:, pb + OFF_GM:pb + OFF_GM + 8] = g["g_mix"][l].reshape(8, 128).T
        prm[:, pb + OFF_GF:pb + OFF_GF + 8] = g["g_ffn"][l].reshape(8, 128).T
        prm[:, pb + OFF_PS:pb + OFF_PS + 2] = g["pool_scale"][l].reshape(2, 128).T
        cw = g["conv_w"][l].reshape(3, 2, 128)
        prm[:, pb + OFF_CW:pb + OFF_CW + 6] = cw.transpose(2, 0, 1).reshape(128, 6)
        prm[:, pb + OFF_SG:pb + OFF_SG + 128] = np.broadcast_to(g["subln_g"][l][None, :], (128, 128))
        lo = pb + OFF_LAM
        prm[:, lo:lo + 64] = g["lambda_q1"][l][None, :]
        prm[:, lo + 64:lo + 128] = g["lambda_k1"][l][None, :]
        prm[:, lo + 128:lo + 192] = g["lambda_q2"][l][None, :]
        prm[:, lo + 192:lo + 256] = g["lambda_k2"][l][None, :]
    prm[:, OFF_GFIN:OFF_GFIN + 8] = g["g_final"].reshape(8, 128).T
    prm[:, OFF_CH:OFF_CH + 4] = g["rel_bias"][31][None, :]
    wins = [2.0, 4.0, 8.0, 16.0]
    t1 = np.arange(1, 17, dtype=f32)
    for c in range(2):
        for half in range(2):
            w = wins[c * 2 + half]
            prm[half * 64:(half + 1) * 64, OFF_ID0 + c * 16:OFF_ID0 + c * 16 + 16] = \
                (1.0 / np.minimum(t1, w)).astype(f32)[None, :]
    prm[:, OFF_EPS] = EPS
    prm[:, OFF_SEPS] = SUBLN_EPS
    kl = np.arange(128)[:, None]
    m = np.arange(TW)[None, :]
    d = m - kl
    bidx = _rel_bucket_np(d)
    tbl = np.empty((128, 4, TW), f32)
    for h in range(4):
        tbl[:, h, :] = np.where(d >= 0, g["rel_bias"][:, h][bidx], f32(MASKV))
    ident = np.eye(128, dtype=f32)
    common = {
        "w_in": g["w_in"], "w_o": g["w_o"], "w_gate": g["w_gate"], "w_up": g["w_up"],
        "w_down": g["w_down"], "w_pool": g["w_pool"], "prm": prm, "tbl": tbl, "ident": ident,
    }
    return g, common


_NC_CACHE = {}


def kernel(**inputs):
    g, common = host_prep(inputs)
    x = g["x"]
    B = x.shape[0]
    if "nc" not in _NC_CACHE:
        _NC_CACHE["nc"] = build_program()
    nc = _NC_CACHE["nc"]
    in_maps = []
    for b in range(B):
        m = dict(common)
        m["xT"] = np.ascontiguousarray(x[b].T)
        in_maps.append(m)
    res = run_bass_kernel_spmd(nc, in_maps, core_ids=list(range(B)))
    out = np.stack([np.ascontiguousarray(res.results[b]["yT"].T) for b in range(B)], axis=0)
    return out.astype(np.float32)
```

```python
import math
from contextlib import ExitStack

import numpy as np
import concourse.bass as bass
import concourse.mybir as mybir
from concourse.bass_utils import run_bass_kernel_spmd

F32 = mybir.dt.float32
BF16 = mybir.dt.bfloat16
AF = mybir.ActivationFunctionType
ALU = mybir.AluOpType

S = 4096
D = 1024
DEPTH = 2
NH = 4
DFF = 2816
NF = DFF // 128
TB = 512
NTB = S // TB
INC = 2560
EPS = 1e-6
SUBLN_EPS = 1e-5
MASKV = -30000.0
TW = 256
BCAST_TBL = True

PL = 8 + 8 + 2 + 6 + 128 + 256
OFF_GM, OFF_GF, OFF_PS, OFF_CW, OFF_SG, OFF_LAM = 0, 8, 16, 18, 24, 152
OFF_GFIN = DEPTH * PL
OFF_CH = OFF_GFIN + 8
OFF_ID0 = OFF_CH + 4
OFF_EPS = OFF_ID0 + 32
OFF_SEPS = OFF_EPS + 1
NP = OFF_SEPS + 1


class Buf:
    __slots__ = ("w", "r")

    def __init__(self):
        self.w = None
        self.r = []


class Rec:
    ENG = ("pe", "act", "dve", "pool", "sp")
    CE = ("pe", "act", "dve", "pool")

    def __init__(self, nc, es):
        self.nc = nc
        self.es = es
        self.sems = {}
        self.count = {}
        self.streams = {e: [] for e in self.ENG}
        self.waited = {e: {} for e in self.ENG}
        for e in self.CE:
            self._sem(e)

    def _sem(self, key):
        if key not in self.sems:
            self.sems[key] = self.es.enter_context(self.nc.semaphore("s_" + key))
            self.count[key] = 0
        return self.sems[key]

    def _collect(self, eng, reads, writes, extra):
        need = {}

        def add(tok, kind):
            if tok is None:
                return
            sem, val = tok
            if sem == eng:
                if eng == "pe":
                    return
                if val > self.count[eng]:
                    return
            if need.get(sem, 0) < val:
                need[sem] = val

        for b in reads:
            add(b.w, "raw")
        for b in writes:
            add(b.w, "waw")
            for t in b.r:
                add(t, "war")
        for t in extra:
            add(t, "raw")
        out = []
        wd = self.waited[eng]
        for sem, val in need.items():
            if wd.get(sem, 0) < val:
                wd[sem] = val
                out.append((sem, val))
        return out

    def op(self, eng, fn, reads=(), writes=(), signal=True, extra=()):
        waits = self._collect(eng, reads, writes, extra)
        if signal:
            self.count[eng] += 1
            tok = (eng, self.count[eng])
            inc = (eng, 1)
        else:
            tok = (eng, self.count[eng] + 1)
            inc = None
        for b in reads:
            b.r.append(tok)
        for b in writes:
            b.w = tok
            b.r = []
        self.streams[eng].append((waits, fn, inc))
        return tok

    def dma(self, q, semkey, fn, reads=(), writes=(), extra=()):
        self._sem(semkey)
        waits = self._collect(q, reads, writes, extra)
        self.count[semkey] += 16
        tok = (semkey, self.count[semkey])
        for b in reads:
            b.r.append(tok)
        for b in writes:
            b.w = tok
            b.r = []
        self.streams[q].append((waits, fn, (semkey, 16)))
        return tok

    def barrier(self):
        toks = [(k, v) for k, v in self.count.items() if v > 0]
        for eng in self.ENG:
            waits = []
            wd = self.waited[eng]
            for sem, val in toks:
                if sem == eng:
                    continue
                if wd.get(sem, 0) < val:
                    wd[sem] = val
                    waits.append((sem, val))
            if waits:
                self.streams[eng].append((waits, None, None))

    def emit(self):
        nc = self.nc
        with nc.Block() as block:
            def mk(name):
                def run(e):
                    for waits, fn, inc in self.streams[name]:
                        for sem, val in waits:
                            e.wait_ge(self.sems[sem], val)
                        if fn is not None:
                            ins = fn(e)
                            if inc is not None:
                                ins.then_inc(self.sems[inc[0]], inc[1])
                return run
            block.tensor(mk("pe"))
            block.scalar(mk("act"))
            block.vector(mk("dve"))
            block.gpsimd(mk("pool"))
            block.sync(mk("sp"))
        for k in self.streams:
            self.streams[k] = []


def emit_norm_sq(R, xt, xB, sq, sqB):
    R.op("act", lambda e: e.activation(out=sq[:, :, :], in_=xt[:, :, :], func=AF.Square),
         reads=[xB], writes=sqB)


def emit_norm(R, G, xt, xB, sq, sqB, ssbank, ssB, lnv, lnvB, rstd, rstdB, out, outB, gcol0):
    prm, ones, onesB = G["prm"], G["ones"], G["onesB"]
    for kc in range(8):
        R.op("pe", lambda e, kc=kc: e.matmul(ssbank[:, :], ones[:, :], sq[:, kc, :],
                                             start=(kc == 0), stop=(kc == 7)),
             reads=sqB + [onesB], writes=[ssB], signal=(kc == 7))
    R.op("act", lambda e: e.activation(out=lnv[:, :], in_=ssbank[:, :], func=AF.Ln,
                                       bias=prm[:, OFF_EPS:OFF_EPS + 1], scale=1.0 / D),
         reads=[ssB], writes=[lnvB])
    R.op("act", lambda e: e.activation(out=rstd[:, :], in_=lnv[:, :], func=AF.Exp, scale=-0.5),
         reads=[lnvB], writes=[rstdB])
    for kc in range(8):
        R.op("dve", lambda e, kc=kc: e.scalar_tensor_tensor(
            out=out[:, kc, :], in0=xt[:, kc, :], scalar=prm[:, gcol0 + kc:gcol0 + kc + 1],
            in1=rstd[:, :], op0=ALU.mult, op1=ALU.mult),
            reads=[xB, rstdB], writes=[outB])


def x_view(ap2d):
    return ap2d.rearrange("(c p) n -> p c n", p=128)


def phase_A(R, nc, G, l, xsrc):
    prm = G["prm"]
    pb = l * PL
    win_d = G["w_in"][l].rearrange("(kc p) n -> p kc n", p=128)
    with ExitStack() as st:
        def sb(name, shape, dt):
            return st.enter_context(nc.sbuf_tensor(f"{name}_L{l}", shape, dt))

        win = sb("winA", [128, 8, INC], BF16)
        xb = [sb(f"xbA{i}", [128, 8, TB], F32) for i in range(2)]
        sq = sb("sqA", [128, 8, TB], BF16)
        hT = [sb(f"hTA{i}", [128, 8, TB], BF16) for i in range(2)]
        lnv = sb("lnvA", [128, TB], F32)
        rstd = sb("rstdA", [128, TB], F32)
        qst = sb("qstA", [128, 4, TB], BF16)
        kst = sb("kstA", [128, 4, TB], BF16)
        vst = sb("vstA", [128, 4, 516], BF16)
        mixst = sb("mixstA", [128, 4, TB], BF16)
        Pb = sb("PbA", [128, 2, 528], F32)
        Ub = sb("UbA", [128, 2, 528], F32)
        gcS = sb("gcSA", [128, 2, TB], F32)
        s2 = sb("s2A", [128, 2, 528], F32)
        s4 = sb("s4A", [128, 2, 528], F32)
        s8 = sb("s8A", [128, 528], F32)
        s16 = sb("s16A", [128, 528], F32)
        tmpf = sb("tmpfA", [128, 16], F32)
        pooled = sb("pooledA", [128, 2, TB], BF16)
        yv = sb("yvA", [128, 2, TB], F32)
        wblk = sb("wblkA", [128, 2, 128], BF16)
        banks = [st.enter_context(nc.psum_tensor(f"L{l}bkA{i}", [128, 512], F32)) for i in range(8)]
        bankB = [Buf() for _ in range(8)]

        winB = [Buf() for _ in range(5)]
        xB = [Buf(), Buf()]
        sqB = [Buf()]
        hB = [Buf(), Buf()]
        lnvB, rstdB, qstB, kstB, vstB, mixB = Buf(), Buf(), Buf(), Buf(), Buf(), Buf()
        PbB, UbB, gcB, s2B, s4B, s8B, s16B, tmpB, poolB, yvB, wblkB = (Buf() for _ in range(11))

        for cg in (0, 1, 3, 4, 2):
            R.dma("pool", f"win{cg}", lambda e, cg=cg: e.dma_start(
                out=win[:, :, cg * 512:(cg + 1) * 512], in_=win_d[:, :, cg * 512:(cg + 1) * 512]),
                writes=[winB[cg]])
        R.op("dve", lambda e: e.memset(wblk[:, :, :], 0.0), writes=[wblkB])
        for g in range(4):
            r0 = (g % 2) * 64
            R.dma("pool", "wblk", lambda e, g=g, r0=r0: e.dma_start(
                out=wblk[r0:r0 + 64, g // 2, r0:r0 + 64], in_=G["w_pool"][l, g, :, :]),
                writes=[wblkB])
        R.op("dve", lambda e: e.memset(
            vst[:, :, :].rearrange("p t (h d) -> p t h d", h=4)[:, :, :, 128:129], 1.0), writes=[vstB])
        R.op("dve", lambda e: e.memset(Pb[:, :, 0:16], 0.0), writes=[PbB])
        R.op("dve", lambda e: e.memset(Ub[:, :, 0:16], 0.0), writes=[UbB])

        xv = x_view(xsrc)

        def load(tb):
            i = tb % 2
            R.dma("sp", f"xA{i}", lambda e: e.dma_start(out=xb[i][:, :, :], in_=xv[:, :, tb * TB:(tb + 1) * TB]),
                  writes=[xB[i]])

        def norm(tb):
            i = tb % 2
            emit_norm(R, G, xb[i], xB[i], sq, sqB, banks[7], bankB[7], lnv, lnvB, rstd, rstdB,
                      hT[i], hB[i], pb + OFF_GM)

        rr = [0]

        def nb():
            i = rr[0] % 5
            rr[0] += 1
            return banks[i], bankB[i]

        def mm_chain(bank, bB, h, hb, col, tsub=None):
            for kc in range(8):
                if tsub is None:
                    fn = lambda e, kc=kc: e.matmul(bank[:, :], win[:, kc, col:col + 128], h[:, kc, :],
                                                   start=(kc == 0), stop=(kc == 7))
                else:
                    fn = lambda e, kc=kc: e.matmul(bank[:, :], h[:, kc, tsub * 128:(tsub + 1) * 128],
                                                   win[:, kc, 1024:1536], start=(kc == 0), stop=(kc == 7))
                R.op("pe", fn, reads=[winB[2 if tsub is not None else col // 512], hb], writes=[bB],
                     signal=(kc == 7))

        def norm_sq(tb):
            emit_norm_sq(R, xb[tb % 2], xB[tb % 2], sq, sqB)

        deferred = []

        load(0)
        load(1)
        norm_sq(0)
        norm(0)
        for tb in range(NTB):
            i = tb % 2
            h, hb = hT[i], hB[i]
            tsl = slice(tb * TB, (tb + 1) * TB)
            if tb + 1 < NTB:
                norm_sq(tb + 1)
            for hh in range(4):
                bank, bB = nb()
                mm_chain(bank, bB, h, hb, hh * 128)
                R.op("act", lambda e, bank=bank, hh=hh: e.mul(out=qst[:, hh, :], in_=bank[:, :], mul=0.125),
                     reads=[bB], writes=[qstB])
            while deferred:
                deferred.pop(0)()
            for hh in range(4):
                bank, bB = nb()
                mm_chain(bank, bB, h, hb, 512 + hh * 128)
                R.op("act", lambda e, bank=bank, hh=hh: e.copy(out=kst[:, hh, :], in_=bank[:, :]),
                     reads=[bB], writes=[kstB])
            R.dma("pool", "qstore", lambda e, tsl=tsl: e.dma_start(
                out=G["qT"].rearrange("h p n -> p h n")[:, :, tsl], in_=qst[:, :, :]), reads=[qstB])
            R.dma("pool", "kstore", lambda e, tsl=tsl: e.dma_start(
                out=G["kT"].rearrange("h p n -> p h n")[:, :, tsl], in_=kst[:, :, :]), reads=[kstB])
            if tb + 1 < NTB:
                norm(tb + 1)
            for c in range(2):
                bank, bB = nb()
                mm_chain(bank, bB, h, hb, 1536 + c * 128)
                R.op("act", lambda e, bank=bank, c=c: e.copy(out=Pb[:, c, 16:528], in_=bank[:, :]),
                     reads=[bB], writes=[PbB])
            for c in range(2):
                bank, bB = nb()
                mm_chain(bank, bB, h, hb, 2048 + c * 128)
                R.op("act", lambda e, bank=bank, c=c: e.copy(out=gcS[:, c, :], in_=bank[:, :]),
                     reads=[bB], writes=[gcB])
            for c in range(2):
                bank, bB = nb()
                mm_chain(bank, bB, h, hb, 2304 + c * 128)
                R.op("dve", lambda e, bank=bank, c=c: e.tensor_tensor(
                    out=Ub[:, c, 16:528], in0=bank[:, :], in1=gcS[:, c, :], op=ALU.mult),
                    reads=[bB, gcB], writes=[UbB])
            gbb = []
            for c in range(2):
                bank, bB = banks[5 + c], bankB[5 + c]
                mm_chain(bank, bB, h, hb, 1792 + c * 128)
                gbb.append((bank, bB))
            for ts in range(4):
                bank, bB = nb()
                mm_chain(bank, bB, h, hb, 0, tsub=ts)
                R.op("act", lambda e, bank=bank, ts=ts: e.copy(
                    out=vst[:, ts, :].rearrange("p (h d) -> p h d", h=4)[:, :, 0:128],
                    in_=bank[:, :].rearrange("p (h d) -> p h d", h=4)),
                    reads=[bB], writes=[vstB])
            R.dma("pool", "vstore", lambda e, tb=tb: e.dma_start(
                out=G["vS"][:, tb * 4:(tb + 1) * 4, :], in_=vst[:, :, :]), reads=[vstB])
            for c in range(2):
                cw = lambda k, c=c: prm[:, pb + OFF_CW + k * 2 + c:pb + OFF_CW + k * 2 + c + 1]
                R.op("dve", lambda e, c=c, cw=cw: e.tensor_scalar(
                    out=yv[:, c, :], in0=Ub[:, c, 14:526], scalar1=cw(0), scalar2=None, op0=ALU.mult),
                    reads=[UbB], writes=[yvB])
                R.op("dve", lambda e, c=c, cw=cw: e.scalar_tensor_tensor(
                    out=yv[:, c, :], in0=Ub[:, c, 15:527], scalar=cw(1), in1=yv[:, c, :],
                    op0=ALU.mult, op1=ALU.add), reads=[UbB, yvB], writes=[yvB])
                R.op("dve", lambda e, c=c, cw=cw: e.scalar_tensor_tensor(
                    out=yv[:, c, :], in0=Ub[:, c, 16:528], scalar=cw(2), in1=yv[:, c, :],
                    op0=ALU.mult, op1=ALU.add), reads=[UbB, yvB], writes=[yvB])
                bank, bB = gbb[c]
                R.op("dve", lambda e, c=c, bank=bank: e.tensor_tensor(
                    out=mixst[:, 2 + c, :], in0=bank[:, :], in1=yv[:, c, :], op=ALU.mult),
                    reads=[bB, yvB], writes=[mixB])
            R.op("dve", lambda e: e.tensor_copy(out=Ub[:, :, 14:16], in_=Ub[:, :, 526:528]),
                 reads=[UbB], writes=[UbB])
            R.op("dve", lambda e: e.tensor_tensor(out=s2[:, :, 1:528], in0=Pb[:, :, 1:528], in1=Pb[:, :, 0:527],
                                                  op=ALU.add), reads=[PbB], writes=[s2B])
            R.op("dve", lambda e: e.tensor_tensor(out=s4[:, :, 3:528], in0=s2[:, :, 3:528], in1=s2[:, :, 1:526],
                                                  op=ALU.add), reads=[s2B], writes=[s4B])
            R.op("dve", lambda e: e.tensor_tensor(out=s8[:, 7:528], in0=s4[:, 1, 7:528], in1=s4[:, 1, 3:524],
                                                  op=ALU.add), reads=[s4B], writes=[s8B])
            R.op("dve", lambda e: e.tensor_tensor(out=s16[64:128, 15:528], in0=s8[64:128, 15:528],
                                                  in1=s8[64:128, 7:520], op=ALU.add), reads=[s8B], writes=[s16B])
            grp = [(s2, lambda r: s2[r, 0, 16:528], 0, 0, 2.0, s2B),
                   (s4, lambda r: s4[r, 0, 16:528], 64, 0, 4.0, s4B),
                   (s8, lambda r: s8[r, 16:528], 0, 1, 8.0, s8B),
                   (s16, lambda r: s16[r, 16:528], 64, 1, 16.0, s16B)]
            for (_, src, r0, c, w, sB) in grp:
                rs = slice(r0, r0 + 64)
                R.op("dve", lambda e, src=src, rs=rs, c=c, w=w: e.scalar_tensor_tensor(
                    out=pooled[rs, c, :], in0=src(rs), scalar=1.0 / w, in1=Pb[rs, c, 16:528],
                    op0=ALU.mult, op1=ALU.subtract), reads=[sB, PbB], writes=[poolB])
            if tb == 0:
                for (_, src, r0, c, w, sB) in grp:
                    rs = slice(r0, r0 + 64)
                    R.op("dve", lambda e, src=src, rs=rs, c=c: e.tensor_tensor(
                        out=tmpf[rs, :], in0=src(rs)[:, 0:16],
                        in1=prm[rs, OFF_ID0 + c * 16:OFF_ID0 + c * 16 + 16], op=ALU.mult),
                        reads=[sB], writes=[tmpB])
                    R.op("dve", lambda e, rs=rs, c=c: e.tensor_tensor(
                        out=pooled[rs, c, 0:16], in0=tmpf[rs, :], in1=Pb[rs, c, 16:32], op=ALU.subtract),
                        reads=[tmpB, PbB], writes=[poolB])
            R.op("dve", lambda e: e.tensor_copy(out=Pb[:, :, 0:16], in_=Pb[:, :, 512:528]),
                 reads=[PbB], writes=[PbB])
            def pool_tail(tsl=tsl):
                for c in range(2):
                    bank, bB = nb()
                    R.op("pe", lambda e, bank=bank, c=c: e.matmul(bank[:, :], wblk[:, c, :], pooled[:, c, :],
                                                                  start=True, stop=True),
                         reads=[wblkB, poolB], writes=[bB])
                    R.op("dve", lambda e, bank=bank, c=c: e.tensor_scalar(
                        out=mixst[:, c, :], in0=bank[:, :], scalar1=prm[:, pb + OFF_PS + c:pb + OFF_PS + c + 1],
                        scalar2=None, op0=ALU.mult), reads=[bB], writes=[mixB])
                R.dma("pool", "mixstore", lambda e: e.dma_start(
                    out=x_view(G["mixT"])[:, 4:8, tsl], in_=mixst[:, :, :]), reads=[mixB])
            deferred.append(pool_tail)
            if tb + 2 < NTB:
                load(tb + 2)
        while deferred:
            deferred.pop(0)()
        R.barrier()
        R.emit()


def phase_B(R, nc, G, l, xsrc, xdst):
    prm = G["prm"]
    pb = l * PL
    lam_init = 0.8 - 0.6 * math.exp(-0.3 * l)
    wo_d = G["w_o"][l].rearrange("(kc p) n -> p kc n", p=128)
    with ExitStack() as st:
        def sb(name, shape, dt):
            return st.enter_context(nc.sbuf_tensor(f"{name}_L{l}", shape, dt))

        KT = sb("KTB", [128, 4, S], BF16)
        V = sb("VB", [128, 32, 516], BF16)
        wo = sb("woB", [128, 8, D], BF16)
        tbl = sb("tblB", [128, 4, TW], F32)
        tblh = sb("tblhB", [128, 4, TW], BF16)
        tbll = sb("tbllB", [128, 4, TW], BF16)
        tblhB, tbllB = Buf(), Buf()
        Qb = [sb(f"QbB{i}", [128, 4, TB], BF16) for i in range(2)]
        PT = [sb(f"PTB{i}", [128, 2, TB], BF16) for i in range(3)]
        mixb = [sb(f"mixbB{i}", [128, 8, TB], BF16) for i in range(2)]
        xb = [sb(f"xbB{i}", [128, 8, TB], F32) for i in range(2)]
        accS = sb("accSB", [128, 8, 129], F32)
        rl = sb("rlB", [128, 2, 4], F32)
        r2n = sb("r2nB", [128, 4], F32)
        tt = sb("ttB", [128, 128], F32)
        attf = sb("attfB", [128, 4, 128], F32)
        junk = sb("junkB", [128, 128], F32)
        ss = sb("ssB", [128, 4], F32)
        vv = sb("vvB", [128, 4], F32)
        nhalf = sb("nhalfB", [128, 4], F32)
        rstd = sb("rstdB", [128, 4], F32)
        attn = sb("attnB", [128, 4, 128], BF16)
        ident = sb("identB", [128, 128], BF16)
        Gp = sb("GpB", [128, 128], F32)
        lamt = sb("lamtB", [128, 8], F32)
        lprod = sb("lprodB", [128, 64], F32)
        Sp = [st.enter_context(nc.psum_tensor(f"L{l}SpB{i}", [128, 2, 512], F32)) for i in range(2)]
        accb = [st.enter_context(nc.psum_tensor(f"L{l}accB{i}", [128, 3, 129], F32)) for i in range(3)]
        trb = st.enter_context(nc.psum_tensor(f"L{l}trB", [128, 512], BF16))

        SB_ = [Buf(), Buf()]
        WOB = [[Buf(), Buf()], [Buf(), Buf()]]
        PTB = [Buf(), Buf(), Buf()]
        accB = [Buf() for _ in range(8)]
        trB = Buf()
        KTB = [Buf() for _ in range(NTB)]
        VBf = [Buf() for _ in range(NTB)]
        QB = [Buf(), Buf()]
        mixatt = [[Buf() for _ in range(4)] for _ in range(2)]
        mixpc = [Buf(), Buf()]
        xB = [[Buf() for _ in range(8)] for _ in range(2)]
        woB, tblB, identB, GpB, lamB, lprodB, nhB = (Buf() for _ in range(7))
        accSB, rlB, r2nB, ttB, attfB, junkB, ssB_, vvB, rstdB, attnB = (Buf() for _ in range(10))

        def acc(idx):
            return accb[idx // 3][:, idx % 3, :]

        R.dma("sp", "tblB", lambda e: e.dma_start(out=tbl[:, :, :], in_=G["tbl"][:, :, :]), writes=[tblB])
        for hh in range(NH):
            R.op("dve", lambda e, hh=hh: e.tensor_scalar(
                out=tbl[:, hh, :], in0=tbl[:, hh, :], scalar1=prm[:, OFF_CH + hh:OFF_CH + hh + 1],
                scalar2=None, op0=ALU.subtract), reads=[tblB], writes=[tblB])
        R.op("dve", lambda e: e.tensor_copy(out=tblh[:, :, :], in_=tbl[:, :, :]), reads=[tblB], writes=[tblhB])
        R.op("dve", lambda e: e.tensor_tensor(out=tbl[:, :, :], in0=tbl[:, :, :], in1=tblh[:, :, :],
                                              op=ALU.subtract), reads=[tblB, tblhB], writes=[tblB])
        R.op("dve", lambda e: e.tensor_copy(out=tbll[:, :, :], in_=tbl[:, :, :]), reads=[tblB], writes=[tbllB])
        R.op("dve", lambda e: e.memset(nhalf[:, :], -0.5), writes=[nhB])
        kT_v = G["kT"].rearrange("h p n -> p h n")

        def kv_load(tb):
            R.dma("sp", f"kld{tb}", lambda e: e.dma_start(
                out=KT[:, :, tb * TB:(tb + 1) * TB], in_=kT_v[:, :, tb * TB:(tb + 1) * TB]), writes=[KTB[tb]])
            R.dma("sp", f"vld{tb}", lambda e: e.dma_start(
                out=V[:, tb * 4:(tb + 1) * 4, :], in_=G["vS"][:, tb * 4:(tb + 1) * 4, :]), writes=[VBf[tb]])
        R.dma("pool", "identB", lambda e: e.dma_start(out=ident[:, :], in_=G["ident"][:, :]), writes=[identB])
        R.dma("pool", "woB", lambda e: e.dma_start(out=wo[:, :, :], in_=wo_d[:, :, :]), writes=[woB])
        lo = pb + OFF_LAM
        for j in range(2):
            R.op("dve", lambda e, j=j: e.tensor_tensor(
                out=lprod[:, :], in0=prm[:, lo + j * 128:lo + j * 128 + 64],
                in1=prm[:, lo + j * 128 + 64:lo + j * 128 + 128], op=ALU.mult), writes=[lprodB])
            R.op("dve", lambda e, j=j: e.reduce_sum(out=lamt[:, j:j + 1], in_=lprod[:, :],
                                                   axis=mybir.AxisListType.X),
                 reads=[lprodB], writes=[lamB])
        R.op("act", lambda e: e.activation(out=lamt[:, 2:4], in_=lamt[:, 0:2], func=AF.Exp),
             reads=[lamB], writes=[lamB])
        R.op("dve", lambda e: e.tensor_tensor(out=lamt[:, 4:5], in0=lamt[:, 2:3], in1=lamt[:, 3:4],
                                              op=ALU.subtract), reads=[lamB], writes=[lamB])
        R.op("dve", lambda e: e.tensor_scalar(out=lamt[:, 5:6], in0=lamt[:, 4:5], scalar1=lam_init,
                                              scalar2=-1.0, op0=ALU.add, op1=ALU.mult),
             reads=[lamB], writes=[lamB])
        R.op("dve", lambda e: e.tensor_scalar(out=Gp[:, :], in0=prm[:, pb + OFF_SG:pb + OFF_SG + 128],
                                              scalar1=1.0 - lam_init, scalar2=None, op0=ALU.mult),
             writes=[GpB])
        nlam = lamt[:, 5:6]

        xv = x_view(xsrc)
        xo = x_view(xdst)
        qT_v = G["qT"].rearrange("h p n -> p h n")
        mix_v = x_view(G["mixT"])

        def loads(qb):
            i = qb % 2
            tsl = slice(qb * TB, (qb + 1) * TB)
            R.dma("sp", f"qld{i}", lambda e: e.dma_start(out=Qb[i][:, :, :], in_=qT_v[:, :, tsl]), writes=[QB[i]])
            R.dma("sp", f"mld{i}", lambda e: e.dma_start(out=mixb[i][:, 4:8, :], in_=mix_v[:, 4:8, tsl]),
                  writes=[mixpc[i]])
            R.dma("sp", f"xld{i}", lambda e: e.dma_start(out=xb[i][:, :, :], in_=xv[:, :, tsl]), writes=xB[i])

        pending = []

        def flush_pending():
            while pending:
                pending.pop(0)()

        def attn_head(qb, h):
            i = qb % 2
            nk = 4 * (qb + 1)

            def qk(kt):
                j = kt - 4 * qb
                qlo = max(0, 128 * j)
                p = kt % 2
                near = (j >= -1)
                for m in range(2):
                    R.op("pe", lambda e, m=m, p=p, kt=kt, qlo=qlo, near=near: e.matmul(
                        Sp[p][:, m, qlo:512], KT[m * 64:(m + 1) * 64, h, kt * 128:(kt + 1) * 128],
                        Qb[i][m * 64:(m + 1) * 64, h, qlo:512], start=True, stop=(not near)),
                        reads=[KTB[kt // 4], QB[i]], writes=[SB_[p], WOB[p][0], WOB[p][1]],
                        signal=(m == 1 and not near))
                if near:
                    mlo = 128 if j < 0 else 0
                    mhi = min(256, mlo + 512 - qlo)
                    c0, c1 = mlo + 128 * j, mhi + 128 * j
                    for m in range(2):
                        for part, tb_ in enumerate((tblh, tbll)):
                            R.op("pe", lambda e, m=m, p=p, c0=c0, c1=c1, mlo=mlo, mhi=mhi, tb_=tb_, part=part: e.matmul(
                                Sp[p][:, m, c0:c1], ident[:, :], tb_[:, h, mlo:mhi], start=False,
                                stop=(part == 1)),
                                reads=[identB, tblhB, tbllB], writes=[SB_[p]], signal=(m == 1 and part == 1))

            def soft(kt):
                j = kt - 4 * qb
                qlo = max(0, 128 * j)
                p = kt % 2
                p3 = kt % 3
                R.op("act", lambda e, p=p, p3=p3, qlo=qlo: e.activation(
                    out=PT[p3][:, :, qlo:512], in_=Sp[p][:, :, qlo:512], func=AF.Exp),
                    reads=[SB_[p]], writes=[PTB[p3]])

            def pv(kt):
                j = kt - 4 * qb
                p = kt % 3
                s0 = max(j, 0)
                seen = set()
                for m in range(2):
                    for sub in range(s0, 4):
                        idx = sub * 2 + m
                        last = (kt == 4 * qb + sub)
                        bnk = idx // 3
                        st_ = (kt == 0 and bnk not in seen)
                        seen.add(bnk)
                        R.op("pe", lambda e, m=m, p=p, sub=sub, kt=kt, last=last, st_=st_, idx=idx: e.matmul(
                            acc(idx), PT[p][:, m, sub * 128:(sub + 1) * 128], V[:, kt, h * 129:(h + 1) * 129],
                            start=st_, stop=last, skip_group_check=True),
                            reads=[PTB[p], VBf[kt // 4]], writes=[accB[idx]],
                            signal=(last or (m == 1 and sub == 3)))

            defer_at = min(nk - 1, 7)
            qk(0)
            qk(1)
            for kt in range(nk):
                soft(kt)
                if kt + 2 < nk:
                    qk(kt + 2)
                pv(kt)
                if kt == defer_at:
                    flush_pending()
            for bnk in range(3):
                n = 3 if bnk < 2 else 2
                R.op("dve", lambda e, bnk=bnk, n=n: e.tensor_copy(
                    out=accS[:, 3 * bnk:3 * bnk + n, :], in_=accb[bnk][:, 0:n, :]),
                    reads=accB[3 * bnk:3 * bnk + n], writes=[accSB])
            accS4 = accS[:, :, :].rearrange("p (s m) c -> p s m c", m=2)
            for m in range(2):
                R.op("dve", lambda e, m=m: e.reciprocal(out=rl[:, m, :], in_=accS4[:, :, m, 128]),
                     reads=[accSB], writes=[rlB])
            R.op("dve", lambda e: e.tensor_scalar(out=r2n[:, :], in0=rl[:, 1, :], scalar1=nlam,
                                                  scalar2=None, op0=ALU.mult),
                 reads=[rlB, lamB], writes=[r2nB])
            for sub in range(4):
                R.op("dve", lambda e, sub=sub: e.tensor_scalar(
                    out=tt[:, :], in0=accS[:, 2 * sub, 0:128], scalar1=rl[:, 0, sub:sub + 1], scalar2=None,
                    op0=ALU.mult), reads=[accSB, rlB], writes=[ttB])
                R.op("dve", lambda e, sub=sub: e.scalar_tensor_tensor(
                    out=attf[:, sub, :], in0=accS[:, 2 * sub + 1, 0:128], scalar=r2n[:, sub:sub + 1],
                    in1=tt[:, :], op0=ALU.mult, op1=ALU.add),
                    reads=[accSB, r2nB, ttB], writes=[attfB])
                R.op("dve", lambda e, sub=sub: e.scalar_tensor_tensor(
                    out=junk[:, :], in0=attf[:, sub, :], scalar=1.0, in1=attf[:, sub, :],
                    op0=ALU.mult, op1=ALU.mult, accum_out=ss[:, sub:sub + 1]),
                    reads=[attfB], writes=[junkB, ssB_])
            R.op("dve", lambda e: e.tensor_scalar(out=vv[:, :], in0=ss[:, :], scalar1=1.0 / 128,
                                                  scalar2=SUBLN_EPS, op0=ALU.mult, op1=ALU.add),
                 reads=[ssB_], writes=[vvB])
            R.op("pool", lambda e: e.tensor_tensor(out=rstd[:, :], in0=vv[:, :], in1=nhalf[:, :], op=ALU.pow),
                 reads=[vvB, nhB], writes=[rstdB])
            for sub in range(4):
                R.op("dve", lambda e, sub=sub: e.scalar_tensor_tensor(
                    out=attn[:, sub, :], in0=attf[:, sub, :], scalar=rstd[:, sub:sub + 1], in1=Gp[:, :],
                    op0=ALU.mult, op1=ALU.mult), reads=[attfB, rstdB, GpB], writes=[attnB])

            def stage2():
                for sub in range(4):
                    R.op("pe", lambda e, sub=sub: e.transpose(trb[:, sub * 128:(sub + 1) * 128], attn[:, sub, :],
                                                              ident[:, :]),
                         reads=[attnB, identB], writes=[trB], signal=(sub == 3))
                R.op("dve", lambda e: e.tensor_copy(out=mixb[i][:, h, :], in_=trb[:, :]),
                     reads=[trB], writes=[mixatt[i][h]])
            pending.append(stage2)

        def wo_block(qb):
            i = qb % 2
            tsl = slice(qb * TB, (qb + 1) * TB)
            for half in range(2):
                ocs = range(4 * half, 4 * half + 4)
                for oc in ocs:
                    p, m = (oc // 2) % 2, oc % 2
                    for kc in (4, 5, 6, 7, 0, 1, 2):
                        rd = [woB, mixatt[i][kc]] if kc < 4 else [woB, mixpc[i]]
                        R.op("pe", lambda e, p=p, m=m, kc=kc, oc=oc: e.matmul(
                            Sp[p][:, m, :], wo[:, kc, oc * 128:(oc + 1) * 128], mixb[i][:, kc, :],
                            start=(kc == 4), stop=False), reads=rd, writes=[WOB[p][m], SB_[p]], signal=False)
                for oc in ocs:
                    p, m = (oc // 2) % 2, oc % 2
                    R.op("pe", lambda e, p=p, m=m, oc=oc: e.matmul(
                        Sp[p][:, m, :], wo[:, 3, oc * 128:(oc + 1) * 128], mixb[i][:, 3, :],
                        start=False, stop=True), reads=[woB, mixatt[i][3]], writes=[WOB[p][m], SB_[p]])
                    R.op("dve", lambda e, p=p, m=m, oc=oc: e.tensor_tensor(
                        out=xb[i][:, oc, :], in0=Sp[p][:, m, :], in1=xb[i][:, oc, :], op=ALU.add),
                        reads=[WOB[p][m], xB[i][oc]], writes=[xB[i][oc]])
            R.dma("pool", f"xst{i}", lambda e: e.dma_start(out=xo[:, :, tsl], in_=xb[i][:, :, :]),
                  reads=xB[i])

        loads(0)
        kv_load(0)
        for qb in range(NTB):
            if qb + 1 < NTB:
                loads(qb + 1)
                kv_load(qb + 1)
            for h in range(NH):
                attn_head(qb, h)
            flush_pending()
            wo_block(qb)
        R.barrier()
        R.emit()


def phase_C(R, nc, G, l, xsrc, xdst, final):
    prm = G["prm"]
    pb = l * PL
    wg_d = G["w_gate"][l].rearrange("(kc p) n -> p kc n", p=128)
    wu_d = G["w_up"][l].rearrange("(kc p) n -> p kc n", p=128)
    wd_d = G["w_down"][l].rearrange("(f p) n -> p f n", p=128)
    with ExitStack() as st:
        def sb(name, shape, dt):
            return st.enter_context(nc.sbuf_tensor(f"{name}_L{l}", shape, dt))

        wg = sb("wgC", [128, 8, DFF], BF16)
        wu = sb("wuC", [128, 8, DFF], BF16)
        wd = sb("wdC", [128, NF, D], BF16)
        xb = [sb(f"xbC{i}", [128, 8, TB], F32) for i in range(2)]
        hT = sb("hTC", [128, 8, TB], BF16)
        aT = sb("aTC", [128, NF, TB], BF16)
        rstd = sb("rstdC", [128, TB], F32)
        sg = sb("sgC", [128, TB], F32)
        sqh = sb("sqhC", [128, 4, TB], BF16)
        banks = [st.enter_context(nc.psum_tensor(f"L{l}bkC{i}", [128, 512], F32)) for i in range(8)]
        bankB = [Buf() for _ in range(8)]
        wgB = [Buf() for _ in range(NF // 2)]
        wuB = [Buf() for _ in range(NF // 2)]
        wdB = [Buf() for _ in range(NF)]
        xB = [[Buf() for _ in range(8)] for _ in range(2)]
        hB, rstdB, sgB, sqhB = Buf(), Buf(), Buf(), Buf()
        aB = [Buf() for _ in range(NF)]

        for g2 in range(NF // 2):
            csl = slice(g2 * 256, (g2 + 1) * 256)
            R.dma("pool", f"wg{g2}", lambda e, csl=csl: e.dma_start(out=wg[:, :, csl], in_=wg_d[:, :, csl]),
                  writes=[wgB[g2]])
            R.dma("pool", f"wu{g2}", lambda e, csl=csl: e.dma_start(out=wu[:, :, csl], in_=wu_d[:, :, csl]),
                  writes=[wuB[g2]])
        for g2 in range(NF // 2):
            R.dma("pool", f"wd{g2}", lambda e, g2=g2: e.dma_start(out=wd[:, 2 * g2:2 * g2 + 2, :],
                                                                  in_=wd_d[:, 2 * g2:2 * g2 + 2, :]),
                  writes=[wdB[2 * g2], wdB[2 * g2 + 1]])

        xv = x_view(xsrc)
        xo = x_view(xdst)
        prm_g = pb + OFF_GF
        ones, onesB = G["ones"], G["onesB"]

        def load(tb):
            i = tb % 2
            R.dma("sp", f"xC{i}", lambda e: e.dma_start(out=xb[i][:, :, :], in_=xv[:, :, tb * TB:(tb + 1) * TB]),
                  writes=xB[i])

        def norm_steps(i):
            def sq(half):
                R.op("act", lambda e: e.activation(
                    out=sqh[:, :, :], in_=xb[i][:, 4 * half:4 * half + 4, :], func=AF.Square),
                    reads=xB[i][4 * half:4 * half + 4], writes=[sqhB])

            def mm(half):
                for k4 in range(4):
                    kc = 4 * half + k4
                    R.op("pe", lambda e, k4=k4, kc=kc: e.matmul(banks[7][:, :], ones[:, :], sqh[:, k4, :],
                                                                start=(kc == 0), stop=(kc == 7)),
                         reads=[sqhB, onesB], writes=[bankB[7]], signal=(k4 == 3))

            def lnexp():
                R.op("act", lambda e: e.activation(out=rstd[:, :], in_=banks[7][:, :], func=AF.Ln,
                                                   bias=prm[:, OFF_EPS:OFF_EPS + 1], scale=1.0 / D),
                     reads=[bankB[7]], writes=[rstdB])
                R.op("act", lambda e: e.activation(out=rstd[:, :], in_=rstd[:, :], func=AF.Exp, scale=-0.5),
                     reads=[rstdB], writes=[rstdB])
            return [lambda: sq(0), lambda: (mm(0), sq(1)), lambda: (mm(1), lnexp())]

        def norm_rstd(i):
            for st_ in norm_steps(i):
                st_()

        def norm_apply(i, dst, dstB_of, gcol0):
            for kc in range(8):
                R.op("dve", lambda e, kc=kc: e.scalar_tensor_tensor(
                    out=dst[:, kc, :], in0=xb[i][:, kc, :], scalar=prm[:, gcol0 + kc:gcol0 + kc + 1],
                    in1=rstd[:, :], op0=ALU.mult, op1=ALU.mult),
                    reads=[xB[i][kc], rstdB], writes=[dstB_of(kc)])

        pending = []
        nxt = []

        def finish_steps(tb):
            i = tb % 2
            tsl = slice(tb * TB, (tb + 1) * TB)

            def store():
                R.dma("pool", f"xstC{i}", lambda e: e.dma_start(out=xo[:, :, tsl], in_=xb[i][:, :, :]),
                      reads=xB[i])
            if not final:
                return [store]
            ns = norm_steps(i)
            return [ns[0], ns[1], lambda: (ns[2](), norm_apply(i, xb[i], lambda kc: xB[i][kc], OFF_GFIN), store())]

        def block(tb):
            i = tb % 2
            sched = False
            for f in range(NF):
                if f >= 1:
                    if pending:
                        pending.pop(0)()
                    if not pending and not sched and tb + 1 < NTB:
                        load(tb + 1)
                        nxt.extend(norm_steps((tb + 1) % 2))
                        sched = True
                    elif nxt and f >= 13:
                        nxt.pop(0)()
                bg, bgB = banks[(2 * f) % 6], bankB[(2 * f) % 6]
                bu, buB = banks[(2 * f + 1) % 6], bankB[(2 * f + 1) % 6]
                for kc in range(8):
                    R.op("pe", lambda e, kc=kc, f=f, bg=bg: e.matmul(
                        bg[:, :], wg[:, kc, f * 128:(f + 1) * 128], hT[:, kc, :],
                        start=(kc == 0), stop=(kc == 7)), reads=[wgB[f // 2], hB], writes=[bgB], signal=(kc == 7))
                for kc in range(8):
                    R.op("pe", lambda e, kc=kc, f=f, bu=bu: e.matmul(
                        bu[:, :], wu[:, kc, f * 128:(f + 1) * 128], hT[:, kc, :],
                        start=(kc == 0), stop=(kc == 7)), reads=[wuB[f // 2], hB], writes=[buB], signal=(kc == 7))
                R.op("act", lambda e, f=f, bg=bg: e.activation(out=sg[:, :], in_=bg[:, :], func=AF.Silu),
                     reads=[bgB], writes=[sgB])
                R.op("dve", lambda e, f=f, bu=bu: e.tensor_tensor(
                    out=aT[:, f, :], in0=bu[:, :], in1=sg[:, :], op=ALU.mult),
                    reads=[buB, sgB], writes=[aB[f]])
            while nxt:
                nxt.pop(0)()
            if tb + 1 < NTB:
                norm_apply((tb + 1) % 2, hT, lambda kc: hB, prm_g)
            for oc in range(8):
                bk, bkB = banks[oc % 6], bankB[oc % 6]
                for f in range(NF):
                    R.op("pe", lambda e, f=f, oc=oc, bk=bk: e.matmul(
                        bk[:, :], wd[:, f, oc * 128:(oc + 1) * 128], aT[:, f, :],
                        start=(f == 0), stop=(f == NF - 1)), reads=[wdB[f], aB[f]], writes=[bkB],
                        signal=(f == NF - 1))
                R.op("dve", lambda e, oc=oc, bk=bk: e.tensor_tensor(
                    out=xb[i][:, oc, :], in0=bk[:, :], in1=xb[i][:, oc, :], op=ALU.add),
                    reads=[bkB, xB[i][oc]], writes=[xB[i][oc]])
            pending.extend(finish_steps(tb))

        load(0)
        norm_rstd(0)
        norm_apply(0, hT, lambda kc: hB, prm_g)
        for tb in range(NTB):
            block(tb)
        while pending:
            pending.pop(0)()
        R.barrier()
        R.emit()


def build_program(stop_after=None, debug=False):
    nc = bass.Bass("TRN2", target_bir_lowering=False)
    dk = "ExternalOutput" if debug else "Internal"
    G = {}
    xT = nc.dram_tensor("xT", [D, S], F32, kind="ExternalInput").ap()
    G["w_in"] = nc.dram_tensor("w_in", [DEPTH, D, INC], F32, kind="ExternalInput").ap()
    G["w_o"] = nc.dram_tensor("w_o", [DEPTH, D, D], F32, kind="ExternalInput").ap()
    G["w_gate"] = nc.dram_tensor("w_gate", [DEPTH, D, DFF], F32, kind="ExternalInput").ap()
    G["w_up"] = nc.dram_tensor("w_up", [DEPTH, D, DFF], F32, kind="ExternalInput").ap()
    G["w_down"] = nc.dram_tensor("w_down", [DEPTH, DFF, D], F32, kind="ExternalInput").ap()
    G["w_pool"] = nc.dram_tensor("w_pool", [DEPTH, 4, 64, 64], F32, kind="ExternalInput").ap()
    prm_d = nc.dram_tensor("prm", [128, NP], F32, kind="ExternalInput").ap()
    G["tbl"] = nc.dram_tensor("tbl", [128, 4, TW], F32, kind="ExternalInput").ap()
    G["ident"] = nc.dram_tensor("ident", [128, 128], F32, kind="ExternalInput").ap()
    yT = nc.dram_tensor("yT", [D, S], F32, kind="ExternalOutput").ap()
    xs = nc.dram_tensor("xs", [D, S], F32, kind=dk).ap()
    G["qT"] = nc.dram_tensor("qT", [4, 128, S], BF16, kind=dk).ap()
    G["kT"] = nc.dram_tensor("kT", [4, 128, S], BF16, kind=dk).ap()
    G["vS"] = nc.dram_tensor("vS", [128, 32, 516], BF16, kind=dk).ap()
    G["mixT"] = nc.dram_tensor("mixT", [D, S], BF16, kind=dk).ap()

    with ExitStack() as es:
        R = Rec(nc, es)
        prm = es.enter_context(nc.sbuf_tensor("prm_sb", [128, NP], F32))
        ones = es.enter_context(nc.sbuf_tensor("ones_sb", [128, 128], BF16))
        G["prm"], G["ones"], G["onesB"] = prm, ones, Buf()
        prmB = Buf()
        R.dma("sp", "prm", lambda e: e.dma_start(out=prm[:, :], in_=prm_d[:, :]), writes=[prmB])
        R.op("dve", lambda e: e.memset(ones[:, :], 1.0), writes=[G["onesB"]])
        R.barrier()
        done = False
        for l in range(DEPTH):
            xin = xT if l == 0 else xs
            phase_A(R, nc, G, l, xin)
            if stop_after == (l, "A"):
                done = True
                break
            phase_B(R, nc, G, l, xin, xs)
            if stop_after == (l, "B"):
                done = True
                break
            last = (l == DEPTH - 1)
            phase_C(R, nc, G, l, xs, yT if last else xs, last)
            if stop_after == (l, "C"):
                done = True
                break
    return nc


def _rel_bucket_np(d):
    n = np.maximum(d, 0)
    nf = np.maximum(n, 1).astype(np.float32)
    large = 16 + (np.log(nf / np.float32(16)) / np.float32(math.log(128 / 16)) * np.float32(16)).astype(np.int32)
    large = np.minimum(large, 31)
    return np.where(n < 16, n, large)


def host_prep(inputs):
    f32 = np.float32
    g = {k: np.asarray(v, dtype=f32) for k, v in inputs.items()}
    prm = np.zeros((128, NP), f32)
    for l in range(DEPTH):
        pb = l * PL
        prm[:, pb + OFF_GM:pb + OFF_GM + 8] = g["g_mix"][l].reshape(8, 128).T
        prm[:, pb + OFF_GF:pb + OFF_GF + 8] = g["g_ffn"][l].reshape(8, 128).T
        prm[:, pb + OFF_PS:pb + OFF_PS + 2] = g["pool_scale"][l].reshape(2, 128).T
        cw = g["conv_w"][l].reshape(3, 2, 128)
        prm[:, pb + OFF_CW:pb + OFF_CW + 6] = cw.transpose(2, 0, 1).reshape(128, 6)
        prm[:, pb + OFF_SG:pb + OFF_SG + 128] = np.broadcast_to(g["subln_g"][l][None, :], (128, 128))
        lo = pb + OFF_LAM
        prm[:, lo:lo + 64] = g["lambda_q1"][l][None, :]
        prm[:, lo + 64:lo + 128] = g["lambda_k1"][l][None, :]
        prm[:, lo + 128:lo + 192] = g["lambda_q2"][l][None, :]
        prm[:, lo + 192:lo + 256] = g["lambda_k2"][l][None, :]
    prm[:, OFF_GFIN:OFF_GFIN + 8] = g["g_final"].reshape(8, 128).T
    prm[:, OFF_CH:OFF_CH + 4] = g["rel_bias"][31][None, :]
    wins = [2.0, 4.0, 8.0, 16.0]
    t1 = np.arange(1, 17, dtype=f32)
    for c in range(2):
        for half in range(2):
            w = wins[c * 2 + half]
            prm[half * 64:(half + 1) * 64, OFF_ID0 + c * 16:OFF_ID0 + c * 16 + 16] = \
                (1.0 / np.minimum(t1, w)).astype(f32)[None, :]
    prm[:, OFF_EPS] = EPS
    prm[:, OFF_SEPS] = SUBLN_EPS
    kl = np.arange(128)[:, None]
    m = np.arange(TW)[None, :]
    d = m - kl
    bidx = _rel_bucket_np(d)
    tbl = np.empty((128, 4, TW), f32)
    for h in range(4):
        tbl[:, h, :] = np.where(d >= 0, g["rel_bias"][:, h][bidx], f32(MASKV))
    ident = np.eye(128, dtype=f32)
    common = {
        "w_in": g["w_in"], "w_o": g["w_o"], "w_gate": g["w_gate"], "w_up": g["w_up"],
        "w_down": g["w_down"], "w_pool": g["w_pool"], "prm": prm, "tbl": tbl, "ident": ident,
    }
    return g, common


_NC_CACHE = {}


def kernel(**inputs):
    g, common = host_prep(inputs)
    x = g["x"]
    B = x.shape[0]
    if "nc" not in _NC_CACHE:
        _NC_CACHE["nc"] = build_program()
    nc = _NC_CACHE["nc"]
    in_maps = []
    for b in range(B):
        m = dict(common)
        m["xT"] = np.ascontiguousarray(x[b].T)
        in_maps.append(m)
    res = run_bass_kernel_spmd(nc, in_maps, core_ids=list(range(B)))
    out = np.stack([np.ascontiguousarray(res.results[b]["yT"].T) for b in range(B)], axis=0)
    return out.astype(np.float32)
```

```python
import math
from contextlib import ExitStack

import numpy as np
import concourse.bass as bass
import concourse.mybir as mybir
from concourse.bass_utils import run_bass_kernel_spmd

F32 = mybir.dt.float32
BF16 = mybir.dt.bfloat16
AF = mybir.ActivationFunctionType
ALU = mybir.AluOpType

S = 4096
D = 1024
DEPTH = 2
NH = 4
DFF = 2816
NF = DFF // 128
TB = 512
NTB = S // TB
INC = 2560
EPS = 1e-6
SUBLN_EPS = 1e-5
MASKV = -30000.0
TW = 256
BCAST_TBL = True

PL = 8 + 8 + 2 + 6 + 128 + 256
OFF_GM, OFF_GF, OFF_PS, OFF_CW, OFF_SG, OFF_LAM = 0, 8, 16, 18, 24, 152
OFF_GFIN = DEPTH * PL
OFF_CH = OFF_GFIN + 8
OFF_ID0 = OFF_CH + 4
OFF_EPS = OFF_ID0 + 32
OFF_SEPS = OFF_EPS + 1
NP = OFF_SEPS + 1


class Buf:
    __slots__ = ("w", "r")

    def __init__(self):
        self.w = None
        self.r = []


class Rec:
    ENG = ("pe", "act", "dve", "pool", "sp")
    CE = ("pe", "act", "dve", "pool")

    def __init__(self, nc, es):
        self.nc = nc
        self.es = es
        self.sems = {}
        self.count = {}
        self.streams = {e: [] for e in self.ENG}
        self.waited = {e: {} for e in self.ENG}
        for e in self.CE:
            self._sem(e)

    def _sem(self, key):
        if key not in self.sems:
            self.sems[key] = self.es.enter_context(self.nc.semaphore("s_" + key))
            self.count[key] = 0
        return self.sems[key]

    def _collect(self, eng, reads, writes, extra):
        need = {}

        def add(tok, kind):
            if tok is None:
                return
            sem, val = tok
            if sem == eng:
                if eng == "pe":
                    return
                if val > self.count[eng]:
                    return
            if need.get(sem, 0) < val:
                need[sem] = val

        for b in reads:
            add(b.w, "raw")
        for b in writes:
            add(b.w, "waw")
            for t in b.r:
                add(t, "war")
        for t in extra:
            add(t, "raw")
        out = []
        wd = self.waited[eng]
        for sem, val in need.items():
            if wd.get(sem, 0) < val:
                wd[sem] = val
                out.append((sem, val))
        return out

    def op(self, eng, fn, reads=(), writes=(), signal=True, extra=()):
        waits = self._collect(eng, reads, writes, extra)
        if signal:
            self.count[eng] += 1
            tok = (eng, self.count[eng])
            inc = (eng, 1)
        else:
            tok = (eng, self.count[eng] + 1)
            inc = None
        for b in reads:
            b.r.append(tok)
        for b in writes:
            b.w = tok
            b.r = []
        self.streams[eng].append((waits, fn, inc))
        return tok

    def dma(self, q, semkey, fn, reads=(), writes=(), extra=()):
        self._sem(semkey)
        waits = self._collect(q, reads, writes, extra)
        self.count[semkey] += 16
        tok = (semkey, self.count[semkey])
        for b in reads:
            b.r.append(tok)
        for b in writes:
            b.w = tok
            b.r = []
        self.streams[q].append((waits, fn, (semkey, 16)))
        return tok

    def barrier(self):
        toks = [(k, v) for k, v in self.count.items() if v > 0]
        for eng in self.ENG:
            waits = []
            wd = self.waited[eng]
            for sem, val in toks:
                if sem == eng:
                    continue
                if wd.get(sem, 0) < val:
                    wd[sem] = val
                    waits.append((sem, val))
            if waits:
                self.streams[eng].append((waits, None, None))

    def emit(self):
        nc = self.nc
        with nc.Block() as block:
            def mk(name):
                def run(e):
                    for waits, fn, inc in self.streams[name]:
                        for sem, val in waits:
                            e.wait_ge(self.sems[sem], val)
                        if fn is not None:
                            ins = fn(e)
                            if inc is not None:
                                ins.then_inc(self.sems[inc[0]], inc[1])
                return run
            block.tensor(mk("pe"))
            block.scalar(mk("act"))
            block.vector(mk("dve"))
            block.gpsimd(mk("pool"))
            block.sync(mk("sp"))
        for k in self.streams:
            self.streams[k] = []


def emit_norm_sq(R, xt, xB, sq, sqB):
    R.op("act", lambda e: e.activation(out=sq[:, :, :], in_=xt[:, :, :], func=AF.Square),
         reads=[xB], writes=sqB)


def emit_norm(R, G, xt, xB, sq, sqB, ssbank, ssB, lnv, lnvB, rstd, rstdB, out, outB, gcol0):
    prm, ones, onesB = G["prm"], G["ones"], G["onesB"]
    for kc in range(8):
        R.op("pe", lambda e, kc=kc: e.matmul(ssbank[:, :], ones[:, :], sq[:, kc, :],
                                             start=(kc == 0), stop=(kc == 7)),
             reads=sqB + [onesB], writes=[ssB], signal=(kc == 7))
    R.op("act", lambda e: e.activation(out=lnv[:, :], in_=ssbank[:, :], func=AF.Ln,
                                       bias=prm[:, OFF_EPS:OFF_EPS + 1], scale=1.0 / D),
         reads=[ssB], writes=[lnvB])
    R.op("act", lambda e: e.activation(out=rstd[:, :], in_=lnv[:, :], func=AF.Exp, scale=-0.5),
         reads=[lnvB], writes=[rstdB])
    for kc in range(8):
        R.op("dve", lambda e, kc=kc: e.scalar_tensor_tensor(
            out=out[:, kc, :], in0=xt[:, kc, :], scalar=prm[:, gcol0 + kc:gcol0 + kc + 1],
            in1=rstd[:, :], op0=ALU.mult, op1=ALU.mult),
            reads=[xB, rstdB], writes=[outB])


def x_view(ap2d):
    return ap2d.rearrange("(c p) n -> p c n", p=128)


def phase_A(R, nc, G, l, xsrc):
    prm = G["prm"]
    pb = l * PL
    win_d = G["w_in"][l].rearrange("(kc p) n -> p kc n", p=128)
    with ExitStack() as st:
        def sb(name, shape, dt):
            return st.enter_context(nc.sbuf_tensor(f"{name}_L{l}", shape, dt))

        win = sb("winA", [128, 8, INC], BF16)
        xb = [sb(f"xbA{i}", [128, 8, TB], F32) for i in range(2)]
        sq = sb("sqA", [128, 8, TB], BF16)
        hT = [sb(f"hTA{i}", [128, 8, TB], BF16) for i in range(2)]
        lnv = sb("lnvA", [128, TB], F32)
        rstd = sb("rstdA", [128, TB], F32)
        qst = sb("qstA", [128, 4, TB], BF16)
        kst = sb("kstA", [128, 4, TB], BF16)
        vst = sb("vstA", [128, 4, 516], BF16)
        mixst = sb("mixstA", [128, 4, TB], BF16)
        Pb = sb("PbA", [128, 2, 528], F32)
        Ub = sb("UbA", [128, 2, 528], F32)
        gcS = sb("gcSA", [128, 2, TB], F32)
        s2 = sb("s2A", [128, 2, 528], F32)
        s4 = sb("s4A", [128, 2, 528], F32)
        s8 = sb("s8A", [128, 528], F32)
        s16 = sb("s16A", [128, 528], F32)
        tmpf = sb("tmpfA", [128, 16], F32)
        pooled = sb("pooledA", [128, 2, TB], BF16)
        yv = sb("yvA", [128, 2, TB], F32)
        wblk = sb("wblkA", [128, 2, 128], BF16)
        banks = [st.enter_context(nc.psum_tensor(f"L{l}bkA{i}", [128, 512], F32)) for i in range(8)]
        bankB = [Buf() for _ in range(8)]

        winB = [Buf() for _ in range(5)]
        xB = [Buf(), Buf()]
        sqB = [Buf()]
        hB = [Buf(), Buf()]
        lnvB, rstdB, qstB, kstB, vstB, mixB = Buf(), Buf(), Buf(), Buf(), Buf(), Buf()
        PbB, UbB, gcB, s2B, s4B, s8B, s16B, tmpB, poolB, yvB, wblkB = (Buf() for _ in range(11))

        for cg in (0, 1, 3, 4, 2):
            R.dma("pool", f"win{cg}", lambda e, cg=cg: e.dma_start(
                out=win[:, :, cg * 512:(cg + 1) * 512], in_=win_d[:, :, cg * 512:(cg + 1) * 512]),
                writes=[winB[cg]])
        R.op("dve", lambda e: e.memset(wblk[:, :, :], 0.0), writes=[wblkB])
        for g in range(4):
            r0 = (g % 2) * 64
            R.dma("pool", "wblk", lambda e, g=g, r0=r0: e.dma_start(
                out=wblk[r0:r0 + 64, g // 2, r0:r0 + 64], in_=G["w_pool"][l, g, :, :]),
                writes=[wblkB])
        R.op("dve", lambda e: e.memset(
            vst[:, :, :].rearrange("p t (h d) -> p t h d", h=4)[:, :, :, 128:129], 1.0), writes=[vstB])
        R.op("dve", lambda e: e.memset(Pb[:, :, 0:16], 0.0), writes=[PbB])
        R.op("dve", lambda e: e.memset(Ub[:, :, 0:16], 0.0), writes=[UbB])

        xv = x_view(xsrc)

        def load(tb):
            i = tb % 2
            R.dma("sp", f"xA{i}", lambda e: e.dma_start(out=xb[i][:, :, :], in_=xv[:, :, tb * TB:(tb + 1) * TB]),
                  writes=[xB[i]])

        def norm(tb):
            i = tb % 2
            emit_norm(R, G, xb[i], xB[i], sq, sqB, banks[7], bankB[7], lnv, lnvB, rstd, rstdB,
                      hT[i], hB[i], pb + OFF_GM)

        rr = [0]

        def nb():
            i = rr[0] % 5
            rr[0] += 1
            return banks[i], bankB[i]

        def mm_chain(bank, bB, h, hb, col, tsub=None):
            for kc in range(8):
                if tsub is None:
                    fn = lambda e, kc=kc: e.matmul(bank[:, :], win[:, kc, col:col + 128], h[:, kc, :],
                                                   start=(kc == 0), stop=(kc == 7))
                else:
                    fn = lambda e, kc=kc: e.matmul(bank[:, :], h[:, kc, tsub * 128:(tsub + 1) * 128],
                                                   win[:, kc, 1024:1536], start=(kc == 0), stop=(kc == 7))
                R.op("pe", fn, reads=[winB[2 if tsub is not None else col // 512], hb], writes=[bB],
                     signal=(kc == 7))

        def norm_sq(tb):
            emit_norm_sq(R, xb[tb % 2], xB[tb % 2], sq, sqB)

        deferred = []

        load(0)
        load(1)
        norm_sq(0)
        norm(0)
        for tb in range(NTB):
            i = tb % 2
            h, hb = hT[i], hB[i]
            tsl = slice(tb * TB, (tb + 1) * TB)
            if tb + 1 < NTB:
                norm_sq(tb + 1)
            for hh in range(4):
                bank, bB = nb()
                mm_chain(bank, bB, h, hb, hh * 128)
                R.op("act", lambda e, bank=bank, hh=hh: e.mul(out=qst[:, hh, :], in_=bank[:, :], mul=0.125),
                     reads=[bB], writes=[qstB])
            while deferred:
                deferred.pop(0)()
            for hh in range(4):
                bank, bB = nb()
                mm_chain(bank, bB, h, hb, 512 + hh * 128)
                R.op("act", lambda e, bank=bank, hh=hh: e.copy(out=kst[:, hh, :], in_=bank[:, :]),
                     reads=[bB], writes=[kstB])
            R.dma("pool", "qstore", lambda e, tsl=tsl: e.dma_start(
                out=G["qT"].rearrange("h p n -> p h n")[:, :, tsl], in_=qst[:, :, :]), reads=[qstB])
            R.dma("pool", "kstore", lambda e, tsl=tsl: e.dma_start(
                out=G["kT"].rearrange("h p n -> p h n")[:, :, tsl], in_=kst[:, :, :]), reads=[kstB])
            if tb + 1 < NTB:
                norm(tb + 1)
            for c in range(2):
                bank, bB = nb()
                mm_chain(bank, bB, h, hb, 1536 + c * 128)
                R.op("act", lambda e, bank=bank, c=c: e.copy(out=Pb[:, c, 16:528], in_=bank[:, :]),
                     reads=[bB], writes=[PbB])
            for c in range(2):
                bank, bB = nb()
                mm_chain(bank, bB, h, hb, 2048 + c * 128)
                R.op("act", lambda e, bank=bank, c=c: e.copy(out=gcS[:, c, :], in_=bank[:, :]),
                     reads=[bB], writes=[gcB])
            for c in range(2):
                bank, bB = nb()
                mm_chain(bank, bB, h, hb, 2304 + c * 128)
                R.op("dve", lambda e, bank=bank, c=c: e.tensor_tensor(
                    out=Ub[:, c, 16:528], in0=bank[:, :], in1=gcS[:, c, :], op=ALU.mult),
                    reads=[bB, gcB], writes=[UbB])
            gbb = []
            for c in range(2):
                bank, bB = banks[5 + c], bankB[5 + c]
                mm_chain(bank, bB, h, hb, 1792 + c * 128)
                gbb.append((bank, bB))
            for ts in range(4):
                bank, bB = nb()
                mm_chain(bank, bB, h, hb, 0, tsub=ts)
                R.op("act", lambda e, bank=bank, ts=ts: e.copy(
                    out=vst[:, ts, :].rearrange("p (h d) -> p h d", h=4)[:, :, 0:128],
                    in_=bank[:, :].rearrange("p (h d) -> p h d", h=4)),
                    reads=[bB], writes=[vstB])
            R.dma("pool", "vstore", lambda e, tb=tb: e.dma_start(
                out=G["vS"][:, tb * 4:(tb + 1) * 4, :], in_=vst[:, :, :]), reads=[vstB])
            for c in range(2):
                cw = lambda k, c=c: prm[:, pb + OFF_CW + k * 2 + c:pb + OFF_CW + k * 2 + c + 1]
                R.op("dve", lambda e, c=c, cw=cw: e.tensor_scalar(
                    out=yv[:, c, :], in0=Ub[:, c, 14:526], scalar1=cw(0), scalar2=None, op0=ALU.mult),
                    reads=[UbB], writes=[yvB])
                R.op("dve", lambda e, c=c, cw=cw: e.scalar_tensor_tensor(
                    out=yv[:, c, :], in0=Ub[:, c, 15:527], scalar=cw(1), in1=yv[:, c, :],
                    op0=ALU.mult, op1=ALU.add), reads=[UbB, yvB], writes=[yvB])
                R.op("dve", lambda e, c=c, cw=cw: e.scalar_tensor_tensor(
                    out=yv[:, c, :], in0=Ub[:, c, 16:528], scalar=cw(2), in1=yv[:, c, :],
                    op0=ALU.mult, op1=ALU.add), reads=[UbB, yvB], writes=[yvB])
                bank, bB = gbb[c]
                R.op("dve", lambda e, c=c, bank=bank: e.tensor_tensor(
                    out=mixst[:, 2 + c, :], in0=bank[:, :], in1=yv[:, c, :], op=ALU.mult),
                    reads=[bB, yvB], writes=[mixB])
            R.op("dve", lambda e: e.tensor_copy(out=Ub[:, :, 14:16], in_=Ub[:, :, 526:528]),
                 reads=[UbB], writes=[UbB])
            R.op("dve", lambda e: e.tensor_tensor(out=s2[:, :, 1:528], in0=Pb[:, :, 1:528], in1=Pb[:, :, 0:527],
                                                  op=ALU.add), reads=[PbB], writes=[s2B])
            R.op("dve", lambda e: e.tensor_tensor(out=s4[:, :, 3:528], in0=s2[:, :, 3:528], in1=s2[:, :, 1:526],
                                                  op=ALU.add), reads=[s2B], writes=[s4B])
            R.op("dve", lambda e: e.tensor_tensor(out=s8[:, 7:528], in0=s4[:, 1, 7:528], in1=s4[:, 1, 3:524],
                                                  op=ALU.add), reads=[s4B], writes=[s8B])
            R.op("dve", lambda e: e.tensor_tensor(out=s16[64:128, 15:528], in0=s8[64:128, 15:528],
                                                  in1=s8[64:128, 7:520], op=ALU.add), reads=[s8B], writes=[s16B])
            grp = [(s2, lambda r: s2[r, 0, 16:528], 0, 0, 2.0, s2B),
                   (s4, lambda r: s4[r, 0, 16:528], 64, 0, 4.0, s4B),
                   (s8, lambda r: s8[r, 16:528], 0, 1, 8.0, s8B),
                   (s16, lambda r: s16[r, 16:528], 64, 1, 16.0, s16B)]
            for (_, src, r0, c, w, sB) in grp:
                rs = slice(r0, r0 + 64)
                R.op("dve", lambda e, src=src, rs=rs, c=c, w=w: e.scalar_tensor_tensor(
                    out=pooled[rs, c, :], in0=src(rs), scalar=1.0 / w, in1=Pb[rs, c, 16:528],
                    op0=ALU.mult, op1=ALU.subtract), reads=[sB, PbB], writes=[poolB])
            if tb == 0:
                for (_, src, r0, c, w, sB) in grp:
                    rs = slice(r0, r0 + 64)
                    R.op("dve", lambda e, src=src, rs=rs, c=c: e.tensor_tensor(
                        out=tmpf[rs, :], in0=src(rs)[:, 0:16],
                        in1=prm[rs, OFF_ID0 + c * 16:OFF_ID0 + c * 16 + 16], op=ALU.mult),
                        reads=[sB], writes=[tmpB])
                    R.op("dve", lambda e, rs=rs, c=c: e.tensor_tensor(
                        out=pooled[rs, c, 0:16], in0=tmpf[rs, :], in1=Pb[rs, c, 16:32], op=ALU.subtract),
                        reads=[tmpB, PbB], writes=[poolB])
            R.op("dve", lambda e: e.tensor_copy(out=Pb[:, :, 0:16], in_=Pb[:, :, 512:528]),
                 reads=[PbB], writes=[PbB])
            def pool_tail(tsl=tsl):
                for c in range(2):
                    bank, bB = nb()
                    R.op("pe", lambda e, bank=bank, c=c: e.matmul(bank[:, :], wblk[:, c, :], pooled[:, c, :],
                                                                  start=True, stop=True),
                         reads=[wblkB, poolB], writes=[bB])
                    R.op("dve", lambda e, bank=bank, c=c: e.tensor_scalar(
                        out=mixst[:, c, :], in0=bank[:, :], scalar1=prm[:, pb + OFF_PS + c:pb + OFF_PS + c + 1],
                        scalar2=None, op0=ALU.mult), reads=[bB], writes=[mixB])
                R.dma("pool", "mixstore", lambda e: e.dma_start(
                    out=x_view(G["mixT"])[:, 4:8, tsl], in_=mixst[:, :, :]), reads=[mixB])
            deferred.append(pool_tail)
            if tb + 2 < NTB:
                load(tb + 2)
        while deferred:
            deferred.pop(0)()
        R.barrier()
        R.emit()


def phase_B(R, nc, G, l, xsrc, xdst):
    prm = G["prm"]
    pb = l * PL
    lam_init = 0.8 - 0.6 * math.exp(-0.3 * l)
    wo_d = G["w_o"][l].rearrange("(kc p) n -> p kc n", p=128)
    with ExitStack() as st:
        def sb(name, shape, dt):
            return st.enter_context(nc.sbuf_tensor(f"{name}_L{l}", shape, dt))

        KT = sb("KTB", [128, 4, S], BF16)
        V = sb("VB", [128, 32, 516], BF16)
        wo = sb("woB", [128, 8, D], BF16)
        tbl = sb("tblB", [128, 4, TW], F32)
        tblh = sb("tblhB", [128, 4, TW], BF16)
        tbll = sb("tbllB", [128, 4, TW], BF16)
        tblhB, tbllB = Buf(), Buf()
        Qb = [sb(f"QbB{i}", [128, 4, TB], BF16) for i in range(2)]
        PT = [sb(f"PTB{i}", [128, 2, TB], BF16) for i in range(3)]
        mixb = [sb(f"mixbB{i}", [128, 8, TB], BF16) for i in range(2)]
        xb = [sb(f"xbB{i}", [128, 8, TB], F32) for i in range(2)]
        accS = sb("accSB", [128, 8, 129], F32)
        rl = sb("rlB", [128, 2, 4], F32)
        r2n = sb("r2nB", [128, 4], F32)
        tt = sb("ttB", [128, 128], F32)
        attf = sb("attfB", [128, 4, 128], F32)
        junk = sb("junkB", [128, 128], F32)
        ss = sb("ssB", [128, 4], F32)
        vv = sb("vvB", [128, 4], F32)
        nhalf = sb("nhalfB", [128, 4], F32)
        rstd = sb("rstdB", [128, 4], F32)
        attn = sb("attnB", [128, 4, 128], BF16)
        ident = sb("identB", [128, 128], BF16)
        Gp = sb("GpB", [128, 128], F32)
        lamt = sb("lamtB", [128, 8], F32)
        lprod = sb("lprodB", [128, 64], F32)
        Sp = [st.enter_context(nc.psum_tensor(f"L{l}SpB{i}", [128, 2, 512], F32)) for i in range(2)]
        accb = [st.enter_context(nc.psum_tensor(f"L{l}accB{i}", [128, 512], F32)) for i in range(3)]
        trb = st.enter_context(nc.psum_tensor(f"L{l}trB", [128, 512], BF16))

        SB_ = [Buf(), Buf()]
        WOB = [[Buf(), Buf()], [Buf(), Buf()]]
        PTB = [Buf(), Buf(), Buf()]
        accB = [Buf() for _ in range(8)]
        trB = Buf()
        KTB = [Buf() for _ in range(NTB)]
        VBf = [Buf() for _ in range(NTB)]
        QB = [Buf(), Buf()]
        mixatt = [[Buf() for _ in range(4)] for _ in range(2)]
        mixpc = [Buf(), Buf()]
        xB = [[Buf() for _ in range(8)] for _ in range(2)]
        woB, tblB, identB, GpB, lamB, lprodB, nhB = (Buf() for _ in range(7))
        accSB, rlB, r2nB, ttB, attfB, junkB, ssB_, vvB, rstdB, attnB = (Buf() for _ in range(10))

        def acc(idx):
            return accb[idx // 3][:, (idx % 3) * 129:(idx % 3 + 1) * 129]

        R.dma("sp", "tblB", lambda e: e.dma_start(out=tbl[:, :, :], in_=G["tbl"][:, :, :]), writes=[tblB])
        for hh in range(NH):
            R.op("dve", lambda e, hh=hh: e.tensor_scalar(
                out=tbl[:, hh, :], in0=tbl[:, hh, :], scalar1=prm[:, OFF_CH + hh:OFF_CH + hh + 1],
                scalar2=None, op0=ALU.subtract), reads=[tblB], writes=[tblB])
        R.op("dve", lambda e: e.tensor_copy(out=tblh[:, :, :], in_=tbl[:, :, :]), reads=[tblB], writes=[tblhB])
        R.op("dve", lambda e: e.tensor_tensor(out=tbl[:, :, :], in0=tbl[:, :, :], in1=tblh[:, :, :],
                                              op=ALU.subtract), reads=[tblB, tblhB], writes=[tblB])
        R.op("dve", lambda e: e.tensor_copy(out=tbll[:, :, :], in_=tbl[:, :, :]), reads=[tblB], writes=[tbllB])
        R.op("dve", lambda e: e.memset(nhalf[:, :], -0.5), writes=[nhB])
        kT_v = G["kT"].rearrange("h p n -> p h n")

        def kv_load(tb):
            R.dma("sp", f"kld{tb}", lambda e: e.dma_start(
                out=KT[:, :, tb * TB:(tb + 1) * TB], in_=kT_v[:, :, tb * TB:(tb + 1) * TB]), writes=[KTB[tb]])
            R.dma("sp", f"vld{tb}", lambda e: e.dma_start(
                out=V[:, tb * 4:(tb + 1) * 4, :], in_=G["vS"][:, tb * 4:(tb + 1) * 4, :]), writes=[VBf[tb]])
        R.dma("pool", "identB", lambda e: e.dma_start(out=ident[:, :], in_=G["ident"][:, :]), writes=[identB])
        lo = pb + OFF_LAM
        for j in range(2):
            R.op("dve", lambda e, j=j: e.tensor_tensor(
                out=lprod[:, :], in0=prm[:, lo + j * 128:lo + j * 128 + 64],
                in1=prm[:, lo + j * 128 + 64:lo + j * 128 + 128], op=ALU.mult), writes=[lprodB])
            R.op("dve", lambda e, j=j: e.reduce_sum(out=lamt[:, j:j + 1], in_=lprod[:, :],
                                                   axis=mybir.AxisListType.X),
                 reads=[lprodB], writes=[lamB])
        R.op("act", lambda e: e.activation(out=lamt[:, 2:4], in_=lamt[:, 0:2], func=AF.Exp),
             reads=[lamB], writes=[lamB])
        R.op("dve", lambda e: e.tensor_tensor(out=lamt[:, 4:5], in0=lamt[:, 2:3], in1=lamt[:, 3:4],
                                              op=ALU.subtract), reads=[lamB], writes=[lamB])
        R.op("dve", lambda e: e.tensor_scalar(out=lamt[:, 5:6], in0=lamt[:, 4:5], scalar1=lam_init,
                                              scalar2=-1.0, op0=ALU.add, op1=ALU.mult),
             reads=[lamB], writes=[lamB])
        R.op("dve", lambda e: e.tensor_scalar(out=Gp[:, :], in0=prm[:, pb + OFF_SG:pb + OFF_SG + 128],
                                              scalar1=1.0 - lam_init, scalar2=None, op0=ALU.mult),
             writes=[GpB])
        nlam = lamt[:, 5:6]

        xv = x_view(xsrc)
        xo = x_view(xdst)
        qT_v = G["qT"].rearrange("h p n -> p h n")
        mix_v = x_view(G["mixT"])

        def loads(qb):
            i = qb % 2
            tsl = slice(qb * TB, (qb + 1) * TB)
            R.dma("sp", f"qld{i}", lambda e: e.dma_start(out=Qb[i][:, :, :], in_=qT_v[:, :, tsl]), writes=[QB[i]])
            R.dma("sp", f"mld{i}", lambda e: e.dma_start(out=mixb[i][:, 4:8, :], in_=mix_v[:, 4:8, tsl]),
                  writes=[mixpc[i]])
            R.dma("sp", f"xld{i}", lambda e: e.dma_start(out=xb[i][:, :, :], in_=xv[:, :, tsl]), writes=xB[i])

        pending = []

        def flush_pending():
            while pending:
                pending.pop(0)()

        def attn_head(qb, h):
            i = qb % 2
            nk = 4 * (qb + 1)

            def qk(kt):
                j = kt - 4 * qb
                qlo = max(0, 128 * j)
                p = kt % 2
                near = (j >= -1)
                for m in range(2):
                    R.op("pe", lambda e, m=m, p=p, kt=kt, qlo=qlo, near=near: e.matmul(
                        Sp[p][:, m, qlo:512], KT[m * 64:(m + 1) * 64, h, kt * 128:(kt + 1) * 128],
                        Qb[i][m * 64:(m + 1) * 64, h, qlo:512], start=True, stop=(not near)),
                        reads=[KTB[kt // 4], QB[i]], writes=[SB_[p], WOB[p][0], WOB[p][1]],
                        signal=(m == 1 and not near))
                if near:
                    mlo = 128 if j < 0 else 0
                    mhi = min(256, mlo + 512 - qlo)
                    c0, c1 = mlo + 128 * j, mhi + 128 * j
                    for m in range(2):
                        for part, tb_ in enumerate((tblh, tbll)):
                            R.op("pe", lambda e, m=m, p=p, c0=c0, c1=c1, mlo=mlo, mhi=mhi, tb_=tb_, part=part: e.matmul(
                                Sp[p][:, m, c0:c1], ident[:, :], tb_[:, h, mlo:mhi], start=False,
                                stop=(part == 1)),
                                reads=[identB, tblhB, tbllB], writes=[SB_[p]], signal=(m == 1 and part == 1))

            def soft(kt):
                j = kt - 4 * qb
                qlo = max(0, 128 * j)
                p = kt % 2
                p3 = kt % 3
                R.op("act", lambda e, p=p, p3=p3, qlo=qlo: e.activation(
                    out=PT[p3][:, :, qlo:512], in_=Sp[p][:, :, qlo:512], func=AF.Exp),
                    reads=[SB_[p]], writes=[PTB[p3]])

            def pv(kt):
                j = kt - 4 * qb
                p = kt % 3
                s0 = max(j, 0)
                seen = set()
                for m in range(2):
                    for sub in range(s0, 4):
                        idx = sub * 2 + m
                        last = (kt == 4 * qb + sub)
                        bnk = idx // 3
                        st_ = (kt == 0 and bnk not in seen)
                        seen.add(bnk)
                        R.op("pe", lambda e, m=m, p=p, sub=sub, kt=kt, last=last, st_=st_, idx=idx: e.matmul(
                            acc(idx), PT[p][:, m, sub * 128:(sub + 1) * 128], V[:, kt, h * 129:(h + 1) * 129],
                            start=st_, stop=last, skip_group_check=True),
                            reads=[PTB[p], VBf[kt // 4]], writes=[accB[idx]],
                            signal=(last or (m == 1 and sub == 3)))

            defer_at = min(nk - 1, 7)
            qk(0)
            qk(1)
            for kt in range(nk):
                soft(kt)
                if kt + 2 < nk:
                    qk(kt + 2)
                pv(kt)
                if kt == defer_at:
                    flush_pending()
            for bnk in range(3):
                n = 3 if bnk < 2 else 2
                R.op("dve", lambda e, bnk=bnk, n=n: e.tensor_copy(
                    out=accS[:, 3 * bnk:3 * bnk + n, :],
                    in_=accb[bnk][:, 0:n * 129].rearrange("p (a c) -> p a c", c=129)),
                    reads=accB[3 * bnk:3 * bnk + n], writes=[accSB])
            accS4 = accS[:, :, :].rearrange("p (s m) c -> p s m c", m=2)
            for m in range(2):
                R.op("dve", lambda e, m=m: e.reciprocal(out=rl[:, m, :], in_=accS4[:, :, m, 128]),
                     reads=[accSB], writes=[rlB])
            R.op("dve", lambda e: e.tensor_scalar(out=r2n[:, :], in0=rl[:, 1, :], scalar1=nlam,
                                                  scalar2=None, op0=ALU.mult),
                 reads=[rlB, lamB], writes=[r2nB])
            for sub in range(4):
                R.op("dve", lambda e, sub=sub: e.tensor_scalar(
                    out=tt[:, :], in0=accS[:, 2 * sub, 0:128], scalar1=rl[:, 0, sub:sub + 1], scalar2=None,
                    op0=ALU.mult), reads=[accSB, rlB], writes=[ttB])
                R.op("dve", lambda e, sub=sub: e.scalar_tensor_tensor(
                    out=attf[:, sub, :], in0=accS[:, 2 * sub + 1, 0:128], scalar=r2n[:, sub:sub + 1],
                    in1=tt[:, :], op0=ALU.mult, op1=ALU.add),
                    reads=[accSB, r2nB, ttB], writes=[attfB])
                R.op("dve", lambda e, sub=sub: e.scalar_tensor_tensor(
                    out=junk[:, :], in0=attf[:, sub, :], scalar=1.0, in1=attf[:, sub, :],
                    op0=ALU.mult, op1=ALU.mult, accum_out=ss[:, sub:sub + 1]),
                    reads=[attfB], writes=[junkB, ssB_])
            R.op("dve", lambda e: e.tensor_scalar(out=vv[:, :], in0=ss[:, :], scalar1=1.0 / 128,
                                                  scalar2=SUBLN_EPS, op0=ALU.mult, op1=ALU.add),
                 reads=[ssB_], writes=[vvB])
            R.op("pool", lambda e: e.tensor_tensor(out=rstd[:, :], in0=vv[:, :], in1=nhalf[:, :], op=ALU.pow),
                 reads=[vvB, nhB], writes=[rstdB])
            for sub in range(4):
                R.op("dve", lambda e, sub=sub: e.scalar_tensor_tensor(
                    out=attn[:, sub, :], in0=attf[:, sub, :], scalar=rstd[:, sub:sub + 1], in1=Gp[:, :],
                    op0=ALU.mult, op1=ALU.mult), reads=[attfB, rstdB, GpB], writes=[attnB])

            def stage2():
                for sub in range(4):
                    R.op("pe", lambda e, sub=sub: e.transpose(trb[:, sub * 128:(sub + 1) * 128], attn[:, sub, :],
                                                              ident[:, :]),
                         reads=[attnB, identB], writes=[trB], signal=(sub == 3))
                R.op("dve", lambda e: e.tensor_copy(out=mixb[i][:, h, :], in_=trb[:, :]),
                     reads=[trB], writes=[mixatt[i][h]])
            pending.append(stage2)

        def wo_block(qb):
            i = qb % 2
            tsl = slice(qb * TB, (qb + 1) * TB)
            dst = {}
            for oc in range(4):
                p, m = (oc // 2) % 2, oc % 2
                dst[oc] = (Sp[p][:, m, :], [WOB[p][m], SB_[p]], [WOB[p][m]])
            for oc in range(4, 7):
                bnk = oc - 4
                bl = accB[3 * bnk:3 * bnk + 3]
                dst[oc] = (accb[bnk][:, :], bl, bl)

            def mm(oc, kc, start, stop, signal):
                ap, wl, _ = dst[oc]
                rd = [woB, mixatt[i][kc]] if kc < 4 else [woB, mixpc[i]]
                R.op("pe", lambda e: e.matmul(ap, wo[:, kc, oc * 128:(oc + 1) * 128], mixb[i][:, kc, :],
                                              start=start, stop=stop, skip_group_check=True),
                     reads=rd, writes=wl, signal=signal)

            def add(oc):
                ap, _, rl_ = dst[oc]
                R.op("dve", lambda e: e.tensor_tensor(out=xb[i][:, oc, :], in0=ap, in1=xb[i][:, oc, :],
                                                      op=ALU.add),
                     reads=rl_ + [xB[i][oc]], writes=[xB[i][oc]])

            for oc in range(7):
                for kc in (4, 5, 6, 7, 0, 1, 2):
                    mm(oc, kc, kc == 4, False, False)
            flush_pending()
            for oc in range(7):
                mm(oc, 3, False, True, True)
                add(oc)
            dst[7] = dst[0]
            for kc in (4, 5, 6, 7, 0, 1, 2, 3):
                mm(7, kc, kc == 4, kc == 3, kc == 3)
            add(7)
            R.dma("pool", f"xst{i}", lambda e: e.dma_start(out=xo[:, :, tsl], in_=xb[i][:, :, :]),
                  reads=xB[i])

        loads(0)
        kv_load(0)
        R.dma("pool", "woB", lambda e: e.dma_start(out=wo[:, :, :], in_=wo_d[:, :, :]), writes=[woB],
              extra=[VBf[0].w, KTB[0].w, QB[0].w])
        for qb in range(NTB):
            if qb + 1 < NTB:
                loads(qb + 1)
                kv_load(qb + 1)
            for h in range(NH):
                attn_head(qb, h)
            wo_block(qb)
        R.barrier()
        R.emit()


def phase_C(R, nc, G, l, xsrc, xdst, final):
    prm = G["prm"]
    pb = l * PL
    wg_d = G["w_gate"][l].rearrange("(kc p) n -> p kc n", p=128)
    wu_d = G["w_up"][l].rearrange("(kc p) n -> p kc n", p=128)
    wd_d = G["w_down"][l].rearrange("(f p) n -> p f n", p=128)
    with ExitStack() as st:
        def sb(name, shape, dt):
            return st.enter_context(nc.sbuf_tensor(f"{name}_L{l}", shape, dt))

        wg = sb("wgC", [128, 8, DFF], BF16)
        wu = sb("wuC", [128, 8, DFF], BF16)
        wd = sb("wdC", [128, NF, D], BF16)
        xb = [sb(f"xbC{i}", [128, 8, TB], F32) for i in range(2)]
        hT = sb("hTC", [128, 8, TB], BF16)
        aT = sb("aTC", [128, NF, TB], BF16)
        rstd = sb("rstdC", [128, TB], F32)
        sg = sb("sgC", [128, TB], F32)
        sqh = sb("sqhC", [128, 4, TB], BF16)
        banks = [st.enter_context(nc.psum_tensor(f"L{l}bkC{i}", [128, 512], F32)) for i in range(8)]
        bankB = [Buf() for _ in range(8)]
        wgB = [Buf() for _ in range(NF // 2)]
        wuB = [Buf() for _ in range(NF // 2)]
        wdB = [Buf() for _ in range(NF)]
        xB = [[Buf() for _ in range(8)] for _ in range(2)]
        hB, rstdB, sgB, sqhB = Buf(), Buf(), Buf(), Buf()
        aB = [Buf() for _ in range(NF)]

        for g2 in range(NF // 2):
            csl = slice(g2 * 256, (g2 + 1) * 256)
            R.dma("pool", f"wg{g2}", lambda e, csl=csl: e.dma_start(out=wg[:, :, csl], in_=wg_d[:, :, csl]),
                  writes=[wgB[g2]])
            R.dma("pool", f"wu{g2}", lambda e, csl=csl: e.dma_start(out=wu[:, :, csl], in_=wu_d[:, :, csl]),
                  writes=[wuB[g2]])
        for g2 in range(NF // 2):
            R.dma("pool", f"wd{g2}", lambda e, g2=g2: e.dma_start(out=wd[:, 2 * g2:2 * g2 + 2, :],
                                                                  in_=wd_d[:, 2 * g2:2 * g2 + 2, :]),
                  writes=[wdB[2 * g2], wdB[2 * g2 + 1]])

        xv = x_view(xsrc)
        xo = x_view(xdst)
        prm_g = pb + OFF_GF
        ones, onesB = G["ones"], G["onesB"]

        def load(tb):
            i = tb % 2
            R.dma("sp", f"xC{i}", lambda e: e.dma_start(out=xb[i][:, :, :], in_=xv[:, :, tb * TB:(tb + 1) * TB]),
                  writes=xB[i])

        def norm_steps(i):
            def sq(half):
                R.op("act", lambda e: e.activation(
                    out=sqh[:, :, :], in_=xb[i][:, 4 * half:4 * half + 4, :], func=AF.Square),
                    reads=xB[i][4 * half:4 * half + 4], writes=[sqhB])

            def mm(half):
                for k4 in range(4):
                    kc = 4 * half + k4
                    R.op("pe", lambda e, k4=k4, kc=kc: e.matmul(banks[7][:, :], ones[:, :], sqh[:, k4, :],
                                                                start=(kc == 0), stop=(kc == 7)),
                         reads=[sqhB, onesB], writes=[bankB[7]], signal=(k4 == 3))

            def lnexp():
                R.op("act", lambda e: e.activation(out=rstd[:, :], in_=banks[7][:, :], func=AF.Ln,
                                                   bias=prm[:, OFF_EPS:OFF_EPS + 1], scale=1.0 / D),
                     reads=[bankB[7]], writes=[rstdB])
                R.op("act", lambda e: e.activation(out=rstd[:, :], in_=rstd[:, :], func=AF.Exp, scale=-0.5),
                     reads=[rstdB], writes=[rstdB])
            return [lambda: sq(0), lambda: (mm(0), sq(1)), lambda: (mm(1), lnexp())]

        def norm_rstd(i):
            for st_ in norm_steps(i):
                st_()

        def norm_apply(i, dst, dstB_of, gcol0):
            for kc in range(8):
                R.op("dve", lambda e, kc=kc: e.scalar_tensor_tensor(
                    out=dst[:, kc, :], in0=xb[i][:, kc, :], scalar=prm[:, gcol0 + kc:gcol0 + kc + 1],
                    in1=rstd[:, :], op0=ALU.mult, op1=ALU.mult),
                    reads=[xB[i][kc], rstdB], writes=[dstB_of(kc)])

        pending = []
        nxt = []

        def finish_steps(tb):
            i = tb % 2
            tsl = slice(tb * TB, (tb + 1) * TB)

            def store():
                R.dma("pool", f"xstC{i}", lambda e: e.dma_start(out=xo[:, :, tsl], in_=xb[i][:, :, :]),
                      reads=xB[i])
            if not final:
                return [store]
            ns = norm_steps(i)
            return [ns[0], ns[1], lambda: (ns[2](), norm_apply(i, xb[i], lambda kc: xB[i][kc], OFF_GFIN), store())]

        def block(tb):
            i = tb % 2
            sched = False
            for f in range(NF):
                if f >= 1:
                    if pending:
                        pending.pop(0)()
                    if not pending and not sched and tb + 1 < NTB:
                        load(tb + 1)
                        nxt.extend(norm_steps((tb + 1) % 2))
                        sched = True
                    elif nxt and f >= 13:
                        nxt.pop(0)()
                bg, bgB = banks[(2 * f) % 6], bankB[(2 * f) % 6]
                bu, buB = banks[(2 * f + 1) % 6], bankB[(2 * f + 1) % 6]
                for kc in range(8):
                    R.op("pe", lambda e, kc=kc, f=f, bg=bg: e.matmul(
                        bg[:, :], wg[:, kc, f * 128:(f + 1) * 128], hT[:, kc, :],
                        start=(kc == 0), stop=(kc == 7)), reads=[wgB[f // 2], hB], writes=[bgB], signal=(kc == 7))
                for kc in range(8):
                    R.op("pe", lambda e, kc=kc, f=f, bu=bu: e.matmul(
                        bu[:, :], wu[:, kc, f * 128:(f + 1) * 128], hT[:, kc, :],
                        start=(kc == 0), stop=(kc == 7)), reads=[wuB[f // 2], hB], writes=[buB], signal=(kc == 7))
                R.op("act", lambda e, f=f, bg=bg: e.activation(out=sg[:, :], in_=bg[:, :], func=AF.Silu),
                     reads=[bgB], writes=[sgB])
                R.op("dve", lambda e, f=f, bu=bu: e.tensor_tensor(
                    out=aT[:, f, :], in0=bu[:, :], in1=sg[:, :], op=ALU.mult),
                    reads=[buB, sgB], writes=[aB[f]])
            while nxt:
                nxt.pop(0)()
            if tb + 1 < NTB:
                norm_apply((tb + 1) % 2, hT, lambda kc: hB, prm_g)
            for oc in range(8):
                bk, bkB = banks[oc % 6], bankB[oc % 6]
                for f in range(NF):
                    R.op("pe", lambda e, f=f, oc=oc, bk=bk: e.matmul(
                        bk[:, :], wd[:, f, oc * 128:(oc + 1) * 128], aT[:, f, :],
                        start=(f == 0), stop=(f == NF - 1)), reads=[wdB[f], aB[f]], writes=[bkB],
                        signal=(f == NF - 1))
                R.op("dve", lambda e, oc=oc, bk=bk: e.tensor_tensor(
                    out=xb[i][:, oc, :], in0=bk[:, :], in1=xb[i][:, oc, :], op=ALU.add),
                    reads=[bkB, xB[i][oc]], writes=[xB[i][oc]])
            pending.extend(finish_steps(tb))

        load(0)
        norm_rstd(0)
        norm_apply(0, hT, lambda kc: hB, prm_g)
        for tb in range(NTB):
            block(tb)
        while pending:
            pending.pop(0)()
        R.barrier()
        R.emit()


def build_program(stop_after=None, debug=False):
    nc = bass.Bass("TRN2", target_bir_lowering=False)
    dk = "ExternalOutput" if debug else "Internal"
    G = {}
    xT = nc.dram_tensor("xT", [D, S], F32, kind="ExternalInput").ap()
    G["w_in"] = nc.dram_tensor("w_in", [DEPTH, D, INC], F32, kind="ExternalInput").ap()
    G["w_o"] = nc.dram_tensor("w_o", [DEPTH, D, D], F32, kind="ExternalInput").ap()
    G["w_gate"] = nc.dram_tensor("w_gate", [DEPTH, D, DFF], F32, kind="ExternalInput").ap()
    G["w_up"] = nc.dram_tensor("w_up", [DEPTH, D, DFF], F32, kind="ExternalInput").ap()
    G["w_down"] = nc.dram_tensor("w_down", [DEPTH, DFF, D], F32, kind="ExternalInput").ap()
    G["w_pool"] = nc.dram_tensor("w_pool", [DEPTH, 4, 64, 64], F32, kind="ExternalInput").ap()
    prm_d = nc.dram_tensor("prm", [128, NP], F32, kind="ExternalInput").ap()
    G["tbl"] = nc.dram_tensor("tbl", [128, 4, TW], F32, kind="ExternalInput").ap()
    G["ident"] = nc.dram_tensor("ident", [128, 128], F32, kind="ExternalInput").ap()
    yT = nc.dram_tensor("yT", [D, S], F32, kind="ExternalOutput").ap()
    xs = nc.dram_tensor("xs", [D, S], F32, kind=dk).ap()
    G["qT"] = nc.dram_tensor("qT", [4, 128, S], BF16, kind=dk).ap()
    G["kT"] = nc.dram_tensor("kT", [4, 128, S], BF16, kind=dk).ap()
    G["vS"] = nc.dram_tensor("vS", [128, 32, 516], BF16, kind=dk).ap()
    G["mixT"] = nc.dram_tensor("mixT", [D, S], BF16, kind=dk).ap()

    with ExitStack() as es:
        R = Rec(nc, es)
        prm = es.enter_context(nc.sbuf_tensor("prm_sb", [128, NP], F32))
        ones = es.enter_context(nc.sbuf_tensor("ones_sb", [128, 128], BF16))
        G["prm"], G["ones"], G["onesB"] = prm, ones, Buf()
        prmB = Buf()
        R.dma("sp", "prm", lambda e: e.dma_start(out=prm[:, :], in_=prm_d[:, :]), writes=[prmB])
        R.op("dve", lambda e: e.memset(ones[:, :], 1.0), writes=[G["onesB"]])
        R.barrier()
        done = False
        for l in range(DEPTH):
            xin = xT if l == 0 else xs
            phase_A(R, nc, G, l, xin)
            if stop_after == (l, "A"):
                done = True
                break
            phase_B(R, nc, G, l, xin, xs)
            if stop_after == (l, "B"):
                done = True
                break
            last = (l == DEPTH - 1)
            phase_C(R, nc, G, l, xs, yT if last else xs, last)
            if stop_after == (l, "C"):
                done = True
                break
    return nc


def _rel_bucket_np(d):
    n = np.maximum(d, 0)
    nf = np.maximum(n, 1).astype(np.float32)
    large = 16 + (np.log(nf / np.float32(16)) / np.float32(math.log(128 / 16)) * np.float32(16)).astype(np.int32)
    large = np.minimum(large, 31)
    return np.where(n < 16, n, large)


def host_prep(inputs):
    f32 = np.float32
    g = {k: np.asarray(v, dtype=f32) for k, v in inputs.items()}
    prm = np.zeros((128, NP), f32)
    for l in range(DEPTH):
        pb = l * PL
        prm[:, pb + OFF_GM:pb + OFF_GM + 8] = g["g_mix"][l].reshape(8, 128).T
        prm[:, pb + OFF_GF:pb + OFF_GF + 8] = g["g_ffn"][l].reshape(8, 128).T
        prm[:, pb + OFF_PS:pb + OFF_PS + 2] = g["pool_scale"][l].reshape(2, 128).T
        cw = g["conv_w"][l].reshape(3, 2, 128)
        prm[:, pb + OFF_CW:pb + OFF_CW + 6] = cw.transpose(2, 0, 1).reshape(128, 6)
        prm[:, pb + OFF_SG:pb + OFF_SG + 128] = np.broadcast_to(g["subln_g"][l][None, :], (128, 128))
        lo = pb + OFF_LAM
        prm[:, lo:lo + 64] = g["lambda_q1"][l][None, :]
        prm[:, lo + 64:lo + 128] = g["lambda_k1"][l][None, :]
        prm[:, lo + 128:lo + 192] = g["lambda_q2"][l][None, :]
        prm[:, lo + 192:lo + 256] = g["lambda_k2"][l][None, :]
    prm[:, OFF_GFIN:OFF_GFIN + 8] = g["g_final"].reshape(8, 128).T
    prm[:, OFF_CH:OFF_CH + 4] = g["rel_bias"][31][None, :]
    wins = [2.0, 4.0, 8.0, 16.0]
    t1 = np.arange(1, 17, dtype=f32)
    for c in range(2):
        for half in range(2):
            w = wins[c * 2 + half]
            prm[half * 64:(half + 1) * 64, OFF_ID0 + c * 16:OFF_ID0 + c * 16 + 16] = \
                (1.0 / np.minimum(t1, w)).astype(f32)[None, :]
    prm[:, OFF_EPS] = EPS
    prm[:, OFF_SEPS] = SUBLN_EPS
    kl = np.arange(128)[:, None]
    m = np.arange(TW)[None, :]
    d = m - kl
    bidx = _rel_bucket_np(d)
    tbl = np.empty((128, 4, TW), f32)
    for h in range(4):
        tbl[:, h, :] = np.where(d >= 0, g["rel_bias"][:, h][bidx], f32(MASKV))
    ident = np.eye(128, dtype=f32)
    common = {
        "w_in": g["w_in"], "w_o": g["w_o"], "w_gate": g["w_gate"], "w_up": g["w_up"],
        "w_down": g["w_down"], "w_pool": g["w_pool"], "prm": prm, "tbl": tbl, "ident": ident,
    }
    return g, common


_NC_CACHE = {}


def kernel(**inputs):
    g, common = host_prep(inputs)
    x = g["x"]
    B = x.shape[0]
    if "nc" not in _NC_CACHE:
        _NC_CACHE["nc"] = build_program()
    nc = _NC_CACHE["nc"]
    in_maps = []
    for b in range(B):
        m = dict(common)
        m["xT"] = np.ascontiguousarray(x[b].T)
        in_maps.append(m)
    res = run_bass_kernel_spmd(nc, in_maps, core_ids=list(range(B)))
    out = np.stack([np.ascontiguousarray(res.results[b]["yT"].T) for b in range(B)], axis=0)
    return out.astype(np.float32)
```
